# Optimizing a Trainium2 kernel written in Bass

```python
import math
import jax, jax.numpy as jnp
from jax import lax
import numpy as np

D_MODEL = 1024
BATCH = 2
SEQ = 8192
DEPTH = 4
DEC_BATCH = 128
DEC_SEQ = 4
PAST_LEN = 8192
PAGE_SIZE = 128

N_MIXERS = 2
N_HEADS = 16
N_KV_HEADS = 4
HEAD_DIM = D_MODEL // N_HEADS
GROUP = N_HEADS // N_KV_HEADS
QKV_DIM = (N_HEADS + 2 * N_KV_HEADS) * HEAD_DIM
WINDOW = 128
CONV_WIDTH = 3
CONV_DIM = D_MODEL
N_MEM = 256
MEM_HEADS = 4
MEM_HEAD_DIM = D_MODEL // MEM_HEADS
PEER_HEADS = 8
N_KEYS = 128
N_EXPERTS = N_KEYS * N_KEYS
PEER_TOPK = 16
PEER_QDIM = 256
PEER_HALF = PEER_QDIM // 2
PEER_BLOCK = 128
N_ATTN_LAYERS = (DEPTH + 1) // 2
N_CONV_LAYERS = DEPTH // 2
DN_ALPHA = (2.0 * DEPTH) ** 0.25
DN_BETA = (8.0 * DEPTH) ** -0.25
LN_EPS = 1e-5
NEG = -1e30

kernel_name = "hybrid_swa_sink_shortconv_peer_decoder_step"


def layer_norm(x, g, b):
    xf = x.astype(jnp.float32)
    mu = jnp.mean(xf, -1, keepdims=True)
    var = jnp.mean(jnp.square(xf - mu), -1, keepdims=True)
    return ((xf - mu) * lax.rsqrt(var + LN_EPS) * g.astype(jnp.float32) + b.astype(jnp.float32)).astype(x.dtype)


def deepnorm(x, fx, g, b):
    return layer_norm(DN_ALPHA * x + fx, g, b)


def split_qkv(x, w_qkv):
    B, L, _ = x.shape
    qkv = x @ w_qkv
    nq = N_HEADS * HEAD_DIM
    nk = N_KV_HEADS * HEAD_DIM
    q = qkv[..., :nq].reshape(B, L, N_KV_HEADS, GROUP, HEAD_DIM)
    k = qkv[..., nq:nq + nk].reshape(B, L, N_KV_HEADS, HEAD_DIM)
    v = qkv[..., nq + nk:].reshape(B, L, N_KV_HEADS, HEAD_DIM)
    return q, k, v


def sink_attention(q, k, v, mask, sinks):
    s = jnp.einsum("...qkgd,...skd->...kgqs", q, k).astype(jnp.float32) * (HEAD_DIM ** -0.5)
    s = jnp.where(mask, s, NEG)
    sk = sinks.astype(jnp.float32).reshape(N_KV_HEADS, GROUP)[:, :, None, None]
    m = jnp.maximum(jnp.max(s, -1, keepdims=True), sk)
    p = jnp.exp(s - m)
    p = (p / (jnp.sum(p, -1, keepdims=True) + jnp.exp(sk - m))).astype(v.dtype)
    return jnp.einsum("...kgqs,...skd->...qkgd", p, v)


def window_attn_prompt(x, w_qkv, sinks, w_o):
    B, L, _ = x.shape
    nb = L // WINDOW
    q, k, v = split_qkv(x, w_qkv)
    qb = q.reshape(B, nb, WINDOW, N_KV_HEADS, GROUP, HEAD_DIM)
    kb = k.reshape(B, nb, WINDOW, N_KV_HEADS, HEAD_DIM)
    vb = v.reshape(B, nb, WINDOW, N_KV_HEADS, HEAD_DIM)
    pad = ((0, 0), (1, 0), (0, 0), (0, 0), (0, 0))
    kw = jnp.concatenate([jnp.pad(kb, pad)[:, :-1], kb], axis=2)
    vw = jnp.concatenate([jnp.pad(vb, pad)[:, :-1], vb], axis=2)
    blk = jnp.arange(nb)[:, None] * WINDOW
    qpos = blk + jnp.arange(WINDOW)[None]
    kpos = blk - WINDOW + jnp.arange(2 * WINDOW)[None]
    d = qpos[:, :, None] - kpos[:, None, :]
    mask = (d >= 0) & (d <= WINDOW) & (kpos[:, None, :] >= 0)
    o = sink_attention(qb, kw, vw, mask[:, None, None], sinks)
    y = o.reshape(B, L, N_HEADS * HEAD_DIM) @ w_o
    wc = min(WINDOW, L)
    return y, k[:, L - wc:], v[:, L - wc:]


def window_attn_sample(x, ck, cv, w_qkv, sinks, w_o):
    B, L, _ = x.shape
    P = ck.shape[1]
    q, k, v = split_qkv(x, w_qkv)
    kk = jnp.concatenate([ck, k.astype(ck.dtype)], axis=1)
    vv = jnp.concatenate([cv, v.astype(cv.dtype)], axis=1)
    qpos = PAST_LEN + jnp.arange(L)
    kpos = PAST_LEN - P + jnp.arange(P + L)
    d = qpos[:, None] - kpos[None, :]
    mask = (d >= 0) & (d <= WINDOW)
    o = sink_attention(q, kk, vv, mask, sinks)
    y = o.reshape(B, L, N_HEADS * HEAD_DIM) @ w_o
    return y, kk[:, L:], vv[:, L:]


def short_conv(x, state, w_in, w_conv, w_out):
    L = x.shape[1]
    gb, gc, h = jnp.split(x @ w_in, 3, axis=-1)
    u = gc * h
    up = jnp.concatenate([state.astype(u.dtype), u], axis=1)
    z = w_conv[0] * up[:, 0:L]
    for j in range(1, CONV_WIDTH):
        z = z + w_conv[j] * up[:, j:j + L]
    return (gb * z) @ w_out, up[:, L:]


def mem_kv(mem, w_kv):
    B, M, _ = mem.shape
    k, v = jnp.split(mem @ w_kv, 2, axis=-1)
    return k.reshape(B, M, MEM_HEADS, MEM_HEAD_DIM), v.reshape(B, M, MEM_HEADS, MEM_HEAD_DIM)


def mem_attn(x, mk, mv, w_q, w_o):
    B, L, _ = x.shape
    q = (x @ w_q).reshape(B, L, MEM_HEADS, MEM_HEAD_DIM)
    s = jnp.einsum("blhd,bmhd->bhlm", q, mk.astype(q.dtype)).astype(jnp.float32) * (MEM_HEAD_DIM ** -0.5)
    p = jax.nn.softmax(s, axis=-1).astype(x.dtype)
    o = jnp.einsum("bhlm,bmhd->blhd", p, mv.astype(x.dtype))
    return o.reshape(B, L, D_MODEL) @ w_o


def peer_block(xb, w_q, sub_keys, u_tab, v_tab):
    T = xb.shape[0]
    q = (xb @ w_q).reshape(T, PEER_HEADS, 2, PEER_HALF)
    s = jnp.einsum("thcd,cnd->thcn", q, sub_keys).astype(jnp.float32)
    sv, si = lax.top_k(s, PEER_TOPK)
    comb = (sv[:, :, 0, :, None] + sv[:, :, 1, None, :]).reshape(T, PEER_HEADS, PEER_TOPK * PEER_TOPK)
    cs, ci = lax.top_k(comb, PEER_TOPK)
    i1 = jnp.take_along_axis(si[:, :, 0], ci // PEER_TOPK, axis=-1)
    i2 = jnp.take_along_axis(si[:, :, 1], ci % PEER_TOPK, axis=-1)
    eid = i1 * N_KEYS + i2
    g = jax.nn.softmax(cs, axis=-1)
    u = u_tab[eid]
    v = v_tab[eid]
    a = jax.nn.gelu(jnp.einsum("td,thkd->thk", xb, u).astype(jnp.float32), approximate=False)
    return jnp.einsum("thk,thkd->td", (g * a).astype(v.dtype), v)


def peer(x, w_q, sub_keys, u_tab, v_tab):
    B, L, D = x.shape
    T = B * L
    nblk = -(-T // PEER_BLOCK)
    xf = jnp.pad(x.reshape(T, D), ((0, nblk * PEER_BLOCK - T), (0, 0)))
    yb = lax.map(lambda xb: peer_block(xb, w_q, sub_keys, u_tab, v_tab), xf.reshape(nblk, PEER_BLOCK, D))
    return yb.reshape(nblk * PEER_BLOCK, D)[:T].reshape(B, L, D)


def setup_inputs(seed: int = 0) -> dict:
    key = jax.random.key(seed)
    ks = jax.random.split(key, 24)

    def nrm(k, shape, scale):
        return jax.random.normal(k, shape, jnp.float32) * scale

    wc = min(WINDOW, PAST_LEN)
    dm = D_MODEL ** -0.5
    return {
        "x_prompt": nrm(ks[0], (BATCH, SEQ, D_MODEL), 1.0),
        "x_sample": nrm(ks[1], (DEC_BATCH, DEC_SEQ, D_MODEL), 1.0),
        "cache_win_k": nrm(ks[2], (N_ATTN_LAYERS, DEC_BATCH, wc, N_KV_HEADS, HEAD_DIM), 1.0),
        "cache_win_v": nrm(ks[3], (N_ATTN_LAYERS, DEC_BATCH, wc, N_KV_HEADS, HEAD_DIM), 1.0),
        "state_conv": nrm(ks[4], (N_CONV_LAYERS, DEC_BATCH, CONV_WIDTH - 1, CONV_DIM), 1.0),
        "cache_mem_k": nrm(ks[5], (DEPTH, DEC_BATCH, N_MEM, MEM_HEADS, MEM_HEAD_DIM), 1.0),
        "cache_mem_v": nrm(ks[6], (DEPTH, DEC_BATCH, N_MEM, MEM_HEADS, MEM_HEAD_DIM), 1.0),
        "mem_prompt": nrm(ks[7], (BATCH, N_MEM, D_MODEL), 1.0),
        "attn_w_qkv": nrm(ks[8], (N_ATTN_LAYERS, D_MODEL, QKV_DIM), dm),
        "attn_sinks": nrm(ks[9], (N_ATTN_LAYERS, N_HEADS), 0.5),
        "attn_w_o": nrm(ks[10], (N_ATTN_LAYERS, N_HEADS * HEAD_DIM, D_MODEL), (N_HEADS * HEAD_DIM) ** -0.5 * DN_BETA),
        "conv_w_in": nrm(ks[11], (N_CONV_LAYERS, D_MODEL, 3 * CONV_DIM), dm),
        "conv_w": nrm(ks[12], (N_CONV_LAYERS, CONV_WIDTH, CONV_DIM), CONV_WIDTH ** -0.5),
        "conv_w_out": nrm(ks[13], (N_CONV_LAYERS, CONV_DIM, D_MODEL), CONV_DIM ** -0.5 * DN_BETA),
        "mem_w_q": nrm(ks[14], (DEPTH, D_MODEL, D_MODEL), dm),
        "mem_w_kv": nrm(ks[15], (DEPTH, D_MODEL, 2 * D_MODEL), dm),
        "mem_w_o": nrm(ks[16], (DEPTH, D_MODEL, D_MODEL), dm * DN_BETA),
        "peer_w_q": nrm(ks[17], (DEPTH, D_MODEL, PEER_HEADS * PEER_QDIM), dm),
        "peer_sub_keys": nrm(ks[18], (DEPTH, 2, N_KEYS, PEER_HALF), PEER_HALF ** -0.5),
        "peer_u": nrm(ks[19], (DEPTH, N_EXPERTS, D_MODEL), dm),
        "peer_v": nrm(ks[20], (DEPTH, N_EXPERTS, D_MODEL), DN_BETA * PEER_HEADS ** -0.5),
        "ln_g": 1.0 + nrm(ks[21], (DEPTH, 3, D_MODEL), 0.02),
        "ln_b": nrm(ks[22], (DEPTH, 3, D_MODEL), 0.02),
    }


def reference(x_prompt, x_sample, cache_win_k, cache_win_v, state_conv, cache_mem_k, cache_mem_v, mem_prompt,
              attn_w_qkv, attn_sinks, attn_w_o, conv_w_in, conv_w, conv_w_out, mem_w_q, mem_w_kv, mem_w_o,
              peer_w_q, peer_sub_keys, peer_u, peer_v, ln_g, ln_b):
    xp, xs = x_prompt, x_sample
    wk_p, wv_p, cv_p, mk_p_all, mv_p_all = [], [], [], [], []
    wk_s, wv_s, cv_s = [], [], []
    for i in range(DEPTH):
        j = i // N_MIXERS
        if i % N_MIXERS == 0:
            fp, kp, vp = window_attn_prompt(xp, attn_w_qkv[j], attn_sinks[j], attn_w_o[j])
            fs, ks_, vs_ = window_attn_sample(xs, cache_win_k[j], cache_win_v[j], attn_w_qkv[j], attn_sinks[j], attn_w_o[j])
            wk_p.append(kp); wv_p.append(vp); wk_s.append(ks_); wv_s.append(vs_)
        else:
            zero_state = jnp.zeros((xp.shape[0], CONV_WIDTH - 1, CONV_DIM), xp.dtype)
            fp, sp = short_conv(xp, zero_state, conv_w_in[j], conv_w[j], conv_w_out[j])
            fs, ss = short_conv(xs, state_conv[j], conv_w_in[j], conv_w[j], conv_w_out[j])
            cv_p.append(sp); cv_s.append(ss)
        xp = deepnorm(xp, fp, ln_g[i, 0], ln_b[i, 0])
        xs = deepnorm(xs, fs, ln_g[i, 0], ln_b[i, 0])
        mk, mv = mem_kv(mem_prompt, mem_w_kv[i])
        mk_p_all.append(mk); mv_p_all.append(mv)
        xp = deepnorm(xp, mem_attn(xp, mk, mv, mem_w_q[i], mem_w_o[i]), ln_g[i, 1], ln_b[i, 1])
        xs = deepnorm(xs, mem_attn(xs, cache_mem_k[i], cache_mem_v[i], mem_w_q[i], mem_w_o[i]), ln_g[i, 1], ln_b[i, 1])
        xp = deepnorm(xp, peer(xp, peer_w_q[i], peer_sub_keys[i], peer_u[i], peer_v[i]), ln_g[i, 2], ln_b[i, 2])
        xs = deepnorm(xs, peer(xs, peer_w_q[i], peer_sub_keys[i], peer_u[i], peer_v[i]), ln_g[i, 2], ln_b[i, 2])
    return (xp, xs,
            jnp.stack(wk_p), jnp.stack(wv_p), jnp.stack(cv_p), jnp.stack(mk_p_all), jnp.stack(mv_p_all),
            jnp.stack(wk_s), jnp.stack(wv_s), jnp.stack(cv_s))
```

```python
from contextlib import ExitStack
import numpy as np
import concourse.bass as bass
import concourse.mybir as mybir
from concourse.alu_op_type import AluOpType as ALU
from concourse.bass_utils import run_bass_kernel_spmd

F32 = mybir.dt.float32
BF16 = mybir.dt.bfloat16
I32 = mybir.dt.int32
U32 = mybir.dt.uint32
AF = mybir.ActivationFunctionType
AX = mybir.AxisListType

SAME_ENGINE_SYNC = True


class Buf:
    def __init__(self, prog, name, t, space):
        self.prog = prog
        self.name = name
        self.t = t
        self.space = space
        self.sem = None
        self.dma_cnt = 0
        self.last_w = None
        self.readers = {}

    def __getitem__(self, key):
        return self.t[key]


class Op:
    __slots__ = ("eng", "fn", "deps", "signal", "sig_val", "is_dma", "sem_buf", "dma_val", "idx")

    def __init__(self, eng, fn, is_dma=False):
        self.eng = eng
        self.fn = fn
        self.deps = []
        self.signal = False
        self.sig_val = 0
        self.is_dma = is_dma
        self.sem_buf = None
        self.dma_val = 0


class Prog:
    ENGS = ("pe", "act", "dve", "pool", "sp")

    def __init__(self, nc, stack):
        self.nc = nc
        self.stack = stack
        self.ops = {e: [] for e in self.ENGS}
        self.bufs = []
        self.nops = 0
        self.out_bufs = []
        import os
        self.limit = int(os.environ.get("KCUT", "1000000000"))

    def sbuf(self, name, shape, dtype):
        t = self.stack.enter_context(self.nc.sbuf_tensor(name, list(shape), dtype))
        b = Buf(self, name, t, "sb")
        self.bufs.append(b)
        return b

    def psum(self, name, shape, dtype):
        t = self.stack.enter_context(self.nc.psum_tensor(name, list(shape), dtype))
        b = Buf(self, name, t, "ps")
        self.bufs.append(b)
        return b

    def dram(self, name, shape, dtype, kind):
        t = self.nc.dram_tensor(name, list(shape), dtype, kind=kind).ap()
        b = Buf(self, name, t, "dr")
        self.bufs.append(b)
        if kind == "ExternalOutput":
            self.out_bufs.append(b)
        return b

    def alias(self, name, t, space="sb"):
        b = Buf(self, name, t, space)
        self.bufs.append(b)
        return b

    def _track(self, op, reads, writes):
        deps = op.deps
        ps_reads = [b for b in reads if b.space == "ps"]
        if ps_reads:
            reads = [b for b in reads if b.space != "ps"]
            writes = list(writes) + [b for b in ps_reads if b not in writes]
        for b in reads:
            if b.last_w is not None:
                deps.append(b.last_w)
        for b in writes:
            if b.last_w is not None:
                deps.append(b.last_w)
            for r in b.readers.values():
                deps.append(r)
        for b in reads:
            key = ("dma", id(op)) if op.is_dma else op.eng
            b.readers[key] = op
        for b in writes:
            b.last_w = op
            b.readers = {}

    def op(self, eng, fn, reads=(), writes=()):
        if self.nops >= self.limit:
            return None
        o = Op(eng, fn)
        self._track(o, reads, writes)
        self.ops[eng].append(o)
        self.nops += 1
        return o

    def dma(self, eng, fn, reads=(), writes=(), sem_buf=None):
        if self.nops >= self.limit:
            return None
        o = Op(eng, fn, is_dma=True)
        if sem_buf is None:
            sem_buf = writes[0]
        o.sem_buf = sem_buf
        sem_buf.dma_cnt += 1
        o.dma_val = 16 * sem_buf.dma_cnt
        self._track(o, reads, writes)
        self.ops[eng].append(o)
        self.nops += 1
        return o

    def emit(self):
        nc = self.nc
        stack = self.stack
        for e in self.ENGS:
            for o in self.ops[e]:
                for d in o.deps:
                    if d.is_dma:
                        continue
                    if d.eng == "pe" and o.eng == "pe" and not o.is_dma:
                        continue
                    if (not SAME_ENGINE_SYNC) and d.eng == o.eng and not o.is_dma:
                        continue
                    d.signal = True
        esem = {}
        for e in self.ENGS:
            if e == "sp":
                continue
            esem[e] = stack.enter_context(nc.semaphore("s_" + e))
            c = 0
            for o in self.ops[e]:
                if o.signal and not o.is_dma:
                    c += 1
                    o.sig_val = c
        for b in self.bufs:
            if b.dma_cnt > 0:
                b.sem = stack.enter_context(nc.semaphore("d_" + b.name))
        self.max_sig = {e: max([o.sig_val for o in self.ops[e]] + [0]) for e in self.ENGS}
        block = stack.enter_context(nc.Block())
        engobj = {"pe": block.tensor, "act": block.scalar, "dve": block.vector, "pool": block.gpsimd, "sp": block.sync}
        prog = self

        def make(e):
            def body(eng):
                waited = {}
                for o in prog.ops[e]:
                    for d in o.deps:
                        if d.is_dma:
                            sem, val = d.sem_buf.sem, d.dma_val
                        else:
                            if d.eng == "pe" and e == "pe" and not o.is_dma:
                                continue
                            if (not SAME_ENGINE_SYNC) and d.eng == e and not o.is_dma:
                                continue
                            sem, val = esem[d.eng], d.sig_val
                        k = id(sem)
                        if waited.get(k, 0) >= val:
                            continue
                        waited[k] = val
                        eng.wait_ge(sem, val)
                    ins = o.fn(eng)
                    if o.is_dma:
                        ins.then_inc(o.sem_buf.sem, 16)
                    elif o.signal:
                        ins.then_inc(esem[e], 1)
                if e == "sp":
                    for b in prog.out_bufs:
                        if b.dma_cnt > 0:
                            eng.wait_ge(b.sem, 16 * b.dma_cnt)
            return body

        for e in self.ENGS:
            engobj[e](make(e))


ALPHA = 8.0 ** 0.25
LN_EPS = 1e-5
NEG = -1e30


class Cfg:
    def __init__(s, D=4, B=2, SEQ=8192, DB=128, n_cores=8):
        s.D, s.B, s.SEQ, s.DB, s.n = D, B, SEQ, DB, n_cores
        s.cps = n_cores // B
        s.NOWN = SEQ // 128 // s.cps
        s.NH = 3
        s.NPB = s.NOWN + s.NH
        s.NSB = DB // n_cores
        s.TS = s.NSB * 4
        s.NA = (D + 1) // 2
        s.NC = D // 2


def build(cfg):
    nc = bass.Bass("TRN2", target_bir_lowering=False)
    stack = ExitStack()
    P = Prog(nc, stack)
    D, NPB, NH, NSB, TS, NA, NC = cfg.D, cfg.NPB, cfg.NH, cfg.NSB, cfg.TS, cfg.NA, cfg.NC
    NOWN = cfg.NOWN
    SU = NPB

    def din(name, shape, dt=F32):
        return P.dram(name, shape, dt, "ExternalInput")

    def dout(name, shape, dt=F32):
        return P.dram(name, shape, dt, "ExternalOutput")

    xp_d = din("xp", [NPB * 128, 1024])
    xs_d = din("xs", [TS, 1024])
    maskA_d = din("maskA", [128, 256])
    maskF_d = din("maskF", [128, 256])
    cflag_d = din("cflag", [128, 1])
    wqkv_d = din("wqkv", [NA, 1024, 2048])
    wo_d = din("wo", [NA, 1024, 1024])
    sinks_d = din("sinks", [1, NA * 16])
    cwkT_d = din("cwkT", [NA, 128, NSB, 4, 128])
    cwk_d = din("cwk", [NA, NSB, 128, 256])
    cwv_d = din("cwv", [NA, NSB, 128, 256])
    win_d = din("win", [max(NC, 1), 1024, 3072])
    cw_d = din("cw", [128, max(NC, 1) * 8 * 3])
    wout_d = din("wout", [max(NC, 1), 1024, 1024])
    stT_d = din("stT", [max(NC, 1), 128, 8 * NSB * 2])
    mwq_d = din("mwq", [D, 1024, 1024])
    mwkv_d = din("mwkv", [D, 1024, 2048])
    mwo_d = din("mwo", [D, 1024, 1024])
    memT_d = din("memT", [128, 8 * 256])
    cmkT_d = din("cmkT", [D, NSB, 128, 8 * 256])
    cmv_d = din("cmv", [D, NSB, 256, 1024])
    pwq_d = din("pwq", [D, 1024, 2048])
    skT_d = din("skT", [D, 2, 128, 128])
    pu_d = din("pu", [D * 16384, 1024])
    pv_d = din("pv", [D * 16384, 1024])
    lng_d = din("lng", [D, 3, 1024])
    lnb_d = din("lnb", [D, 3, 1024])

    yp_d = dout("yp", [NOWN * 128, 1024])
    ys_d = dout("ys", [TS, 1024])
    wkp_d = dout("wkp", [NA, 128, 256])
    wvp_d = dout("wvp", [NA, 128, 256])
    cvp_d = dout("cvp", [max(NC, 1), 2, 1024])
    mkp_d = dout("mkp", [D, 256, 1024])
    mvp_d = dout("mvp", [D, 256, 1024])
    wks_d = dout("wks", [NA, NSB, 128, 256])
    wvs_d = dout("wvs", [NA, NSB, 128, 256])
    cvs_d = dout("cvs", [max(NC, 1), NSB * 2, 1024])

    X = [P.sbuf("X%d" % u, [128, 1024], F32) for u in range(NPB + 1)]
    identf = P.sbuf("identf", [128, 128], F32)
    identb = P.sbuf("identb", [128, 128], BF16)
    maskA = P.sbuf("maskA_s", [128, 256], F32)
    maskF = P.sbuf("maskF_s", [128, 256], F32)
    cflag = P.sbuf("cflag_s", [128, 1], F32)
    sinkb = P.sbuf("sinkb", [128, NA * 16], F32)
    CW = P.sbuf("CW", [128, max(NC, 1) * 8 * 3], F32)
    iota16 = P.sbuf("iota16", [128, 16], F32)
    GB = P.sbuf("GB", [128, 1024], F32)
    BB = P.sbuf("BB", [128, 1024], F32)
    XB = P.sbuf("XB", [128, 1024], BF16)
    OB = P.sbuf("OB", [128, 1024], BF16)
    XT = P.sbuf("XT", [128, 8, 128], BF16)
    OT = P.sbuf("OT", [128, 8, 128], BF16)
    Y = P.sbuf("Y", [128, 1024], F32)
    ST = P.sbuf("ST", [128, 12], F32)
    MV2 = P.sbuf("MV2", [128, 2], F32)
    SD = P.sbuf("SD", [128, 1], F32)
    RSTD = P.sbuf("RSTD", [128, 1], F32)
    NMR = P.sbuf("NMR", [128, 1], F32)
    DUM = P.sbuf("DUM", [128, 4], F32)
    tmpi = P.sbuf("tmpi", [128, 128], I32)
    tmpf = P.sbuf("tmpf", [128, 128], F32)
    tmpr = P.sbuf("tmpr", [128, 1], F32)

    ARENA_BYTES = 100 * 1024
    ARENA = P.sbuf("ARENA", [128, ARENA_BYTES // 4], F32)
    arena_views = {F32: ARENA[:], BF16: ARENA[:].bitcast(BF16), I32: ARENA[:].bitcast(I32), U32: ARENA[:].bitcast(U32)}
    esz = {F32: 4, BF16: 2, I32: 4, U32: 4}
    arena_cache = {}
    arena_off = [0]

    class AV:
        def __init__(self, buf, ap):
            self.buf = buf
            self.ap = ap

        def __getitem__(self, key):
            return self.ap[key]

    def aalloc(phase, name, shape, dt):
        n = int(np.prod(shape[1:]))
        nb = (n * esz[dt] + 63) // 64 * 64
        off = arena_off[0]
        arena_off[0] += nb
        assert arena_off[0] <= ARENA_BYTES, (phase, name, arena_off[0])
        key = (phase, name)
        if key in arena_cache:
            assert arena_cache[key][1] == off
            return arena_cache[key][0]
        e0 = off // esz[dt]
        ap = arena_views[dt][:, e0:e0 + n]
        if len(shape) == 3:
            ap = ap.rearrange("p (a b) -> p a b", a=shape[1])
        elif len(shape) == 4:
            ap = ap.rearrange("p (a b c) -> p a b c", a=shape[1], b=shape[2])
        buf = P.alias(phase + "_" + name, ap, "sb")
        av = AV(buf, ap)
        arena_cache[key] = (av, off)
        return av

    def barrier():
        bufs = [v[0].buf for v in arena_cache.values()]
        P.op("dve", lambda e: e.memset(DUM[:, 0:1], 0.0), reads=[], writes=bufs + [DUM])

    Fp = P.psum("Fp", [128, 1024], F32)
    Sp = P.psum("Sp", [128, 2048], F32)
    Ap = P.psum("Ap", [128, 512], F32)
    Bp = P.psum("Bp", [128, 512], F32)
    Sb = [P.alias("Sb%d" % i, Sp.t[:, i * 512:(i + 1) * 512], "ps") for i in range(4)]
    A_bf = Ap[:].bitcast(BF16).rearrange("p (k t) -> p k t", k=8)
    pj_banks = [(Bp, Bp.t), (Sb[3], Sp.t[:, 1536:2048])]
    pj_i = [0]

    def next_pj():
        b = pj_banks[pj_i[0] % 2]
        pj_i[0] += 1
        return b

    def mm(out, lhsT, rhs, start, stop, reads, writes):
        P.op("pe", lambda e: e.matmul(out, lhsT=lhsT, rhs=rhs, start=start, stop=stop), reads=reads, writes=writes)

    def tr(out, in_, ident, reads, writes):
        P.op("pe", lambda e: e.transpose(out=out, in_=in_, identity=ident), reads=reads, writes=writes)

    def dve(f, reads, writes):
        P.op("dve", f, reads=reads, writes=writes)

    def act(f, reads, writes):
        P.op("act", f, reads=reads, writes=writes)

    def pool(f, reads, writes):
        P.op("pool", f, reads=reads, writes=writes)

    def ld(out, in_, reads, writes, eng="sp", **kw):
        P.dma(eng, lambda e: e.dma_start(out=out, in_=in_, **kw), reads=reads, writes=writes)

    def st(out, in_, reads, dbuf, eng="sp", **kw):
        P.dma(eng, lambda e: e.dma_start(out=out, in_=in_, **kw), reads=reads, writes=[dbuf], sem_buf=dbuf)


    def TT(out, in0, in1, op, reads, writes, eng="dve"):
        P.op(eng, lambda e: e.tensor_tensor(out=out, in0=in0, in1=in1, op=op), reads=reads, writes=writes)

    def TSC(out, in0, s1, s2, op0, op1, reads, writes, eng="dve"):
        if op1 is None:
            P.op(eng, lambda e: e.tensor_scalar(out=out, in0=in0, scalar1=s1, scalar2=None, op0=op0), reads=reads, writes=writes)
        else:
            P.op(eng, lambda e: e.tensor_scalar(out=out, in0=in0, scalar1=s1, scalar2=s2, op0=op0, op1=op1), reads=reads, writes=writes)

    def STT(out, in0, scalar, in1, op0, op1, reads, writes):
        P.op("dve", lambda e: e.scalar_tensor_tensor(out=out, in0=in0, scalar=scalar, in1=in1, op0=op0, op1=op1),
             reads=reads, writes=writes)

    def TCOPY(out, in_, reads, writes, eng="dve"):
        P.op(eng, lambda e: e.tensor_copy(out=out, in_=in_), reads=reads, writes=writes)

    def TRED(out, in_, op, reads, writes):
        P.op("dve", lambda e: e.tensor_reduce(out=out, in_=in_, axis=AX.X, op=op), reads=reads, writes=writes)

    def ACTF(out, in_, func, reads, writes):
        P.op("act", lambda e: e.activation(out=out, in_=in_, func=func), reads=reads, writes=writes)

    def ACOPY(out, in_, reads, writes):
        P.op("act", lambda e: e.copy(out=out, in_=in_), reads=reads, writes=writes)

    def AMUL(out, in_, mul, reads, writes):
        P.op("act", lambda e: e.mul(out=out, in_=in_, mul=mul), reads=reads, writes=writes)

    def RECIP(out, in_, reads, writes):
        P.op("dve", lambda e: e.reciprocal(out=out, in_=in_), reads=reads, writes=writes)

    def MEMSET(out, val, writes, eng="pool"):
        P.op(eng, lambda e: e.memset(out, val), reads=[], writes=writes)

    def MAX8(out, in_, reads, writes):
        P.op("dve", lambda e: e.max(out=out, in_=in_), reads=reads, writes=writes)

    def MAXIDX(out, in_max, in_values, reads, writes):
        P.op("dve", lambda e: e.max_index(out=out, in_max=in_max, in_values=in_values), reads=reads, writes=writes)

    def MATCHREP(out, in_to_replace, in_values, reads, writes):
        P.op("dve", lambda e: e.match_replace(out=out, in_to_replace=in_to_replace, in_values=in_values, imm_value=NEG),
             reads=reads, writes=writes)

    def TSS(out, in_, scalar, op, reads, writes):
        P.op("dve", lambda e: e.tensor_single_scalar(out=out, in_=in_, scalar=scalar, op=op), reads=reads, writes=writes)

    def GATHER(slot_ap, table_ap, idx_ap, reads, writes):
        P.dma("pool", lambda e: e.indirect_dma_start(out=slot_ap, out_offset=None, in_=table_ap,
                                                     in_offset=bass.IndirectOffsetOnAxis(ap=idx_ap, axis=0)),
              reads=reads, writes=writes)

    def TTR(out, in0, in1, accum_out, reads, writes):
        P.op("dve", lambda e: e.scalar_tensor_tensor(out=out, in0=in0, scalar=1.0, in1=in1, op0=ALU.mult, op1=ALU.mult,
                                                     accum_out=accum_out), reads=reads, writes=writes)

    def load_w(dst_av, nk, ncols, src_ap, src_buf):
        for k in range(nk):
            ld(dst_av[:, k, 0:ncols], src_ap[k * 128:(k + 1) * 128, :], [src_buf], [dst_av.buf], eng="pool")

    pool(lambda e: e.iota(tmpi[:], pattern=[[1, 128]], base=0, channel_multiplier=0), [], [tmpi])
    dve(lambda e: e.tensor_copy(out=tmpf[:], in_=tmpi[:]), [tmpi], [tmpf])
    dve(lambda e: e.tensor_copy(out=iota16[:], in_=tmpi[:, 0:16]), [tmpi], [iota16])
    pool(lambda e: e.iota(tmpi[:, 0:1], pattern=[[1, 1]], base=0, channel_multiplier=1), [tmpf, iota16], [tmpi])
    dve(lambda e: e.tensor_copy(out=tmpr[:], in_=tmpi[:, 0:1]), [tmpi], [tmpr])
    dve(lambda e: e.tensor_scalar(out=identf[:], in0=tmpf[:], scalar1=tmpr[:, 0:1], scalar2=None, op0=ALU.is_equal),
        [tmpf, tmpr], [identf])
    dve(lambda e: e.tensor_copy(out=identb[:], in_=identf[:]), [identf], [identb])
    ld(maskA[:], maskA_d[:], [maskA_d], [maskA])
    ld(maskF[:], maskF_d[:], [maskF_d], [maskF])
    ld(cflag[:], cflag_d[:], [cflag_d], [cflag])
    ld(sinkb[:], sinks_d[0:1, :].partition_broadcast(128), [sinks_d], [sinkb])
    ld(CW[:], cw_d[:], [cw_d], [CW])
    for u in range(NPB):
        ld(X[u][:], xp_d[u * 128:(u + 1) * 128, :], [xp_d], [X[u]])
    ld(X[SU][0:TS, :], xs_d[:], [xs_d], [X[SU]])

    units = [(u, 128) for u in range(NPB)] + [(SU, TS)]
    import os
    units = units[:int(os.environ.get('KUNITS', '999'))]

    def transpose8(src, dst, T):
        for k in range(8):
            tr(A_bf[:, k, 0:T], src[0:T, k * 128:(k + 1) * 128], identb[0:T, 0:T], [src, identb], [Ap])
        TCOPY(dst[:, :, 0:T], A_bf[:, :, 0:T], [Ap], [dst])

    def make_xT(u, T):
        ACOPY(XB[0:T, :], X[u][0:T, :], [X[u]], [XB])
        transpose8(XB, XT, T)

    def proj_fm(W, col0, nch, T, dst, scale, src=XT):
        for c0 in range(0, nch, 4):
            n = min(4, nch - c0)
            pb, pt = next_pj()
            pv = pt.rearrange("p (c t) -> p c t", c=4)
            for c in range(n):
                for k in range(8):
                    mm(pv[:, c, 0:T], W[:, k, col0 + (c0 + c) * 128: col0 + (c0 + c + 1) * 128], src[:, k, 0:T],
                       k == 0, k == 7, [W.buf, src], [pb])
            AMUL(dst[:, c0:c0 + n, 0:T], pv[:, 0:n, 0:T], scale, [pb], [dst.buf])

    def proj_F(W, T, srcT, srcbuf):
        for n in range(2):
            for k in range(8):
                mm(Fp[0:T, n * 512:(n + 1) * 512], srcT[:, k, 0:T], W[:, k, n * 512:(n + 1) * 512], k == 0, k == 7,
                   [srcbuf, W.buf], [Fp])

    def load_ln(i, j):
        ld(GB[:], lng_d[i, j:j + 1, :].partition_broadcast(128), [lng_d], [GB])
        ld(BB[:], lnb_d[i, j:j + 1, :].partition_broadcast(128), [lnb_d], [BB])

    def resid_ln(u, T, f_ap, f_bufs):
        x = X[u]
        STT(Y[0:T, :], x[0:T, :], ALPHA, f_ap, ALU.mult, ALU.add, [x] + f_bufs, [Y])
        P.op("dve", lambda e: e.bn_stats(out=ST[0:T, 0:6], in_=Y[0:T, 0:512]), reads=[Y], writes=[ST])
        P.op("dve", lambda e: e.bn_stats(out=ST[0:T, 6:12], in_=Y[0:T, 512:1024]), reads=[Y], writes=[ST])
        P.op("dve", lambda e: e.bn_aggr(out=MV2[0:T, :], in_=ST[0:T, :]), reads=[ST], writes=[MV2])
        TSC(SD[0:T, :], MV2[0:T, 1:2], LN_EPS, None, ALU.add, None, [MV2], [SD])
        P.op("act", lambda e: e.sqrt(out=SD[0:T, :], in_=SD[0:T, :]), reads=[SD], writes=[SD])
        RECIP(RSTD[0:T, :], SD[0:T, :], [SD], [RSTD])
        STT(NMR[0:T, :], MV2[0:T, 0:1], -1.0, RSTD[0:T, :], ALU.mult, ALU.mult, [MV2, RSTD], [NMR])
        TSC(x[0:T, :], Y[0:T, :], RSTD[0:T, 0:1], NMR[0:T, 0:1], ALU.mult, ALU.add, [Y, RSTD, NMR], [x])
        TT(x[0:T, :], x[0:T, :], GB[0:T, :], ALU.mult, [x, GB], [x], eng="pool")
        TT(x[0:T, :], x[0:T, :], BB[0:T, :], ALU.add, [x, BB], [x], eng="pool")

    def state_out(uc_out_ap, cols_ap, cols_bufs, n, UC, USB, dst_ap, dst_buf):
        TCOPY(uc_out_ap, cols_ap, cols_bufs, [UC.buf], eng="pool")
        for k in range(8):
            tr(Fp[0:n, k * 128:(k + 1) * 128], UC[:, k, 0:n], identf[:, :], [UC.buf, identf], [Fp])
        ACOPY(USB[0:n, :], Fp[0:n, :], [Fp], [USB.buf])
        st(dst_ap, USB[0:n, :], [USB.buf], dst_buf)

    def attn_group(T, j, kvh, QT, qc0, ktp, ktp_buf, kto, kto_buf, vp, vp_buf, vo, vo_buf, mask, SS, PBt, PTS, sm,
                   Odst, Odst_buf):
        NK = 256
        S3 = Sp.t[0:T, 0:1024].rearrange("p (g s) -> p g s", g=4)
        for g in range(4):
            h = kvh * 4 + g
            ch, hf = h // 2, h % 2
            s = 2 * (g % 2) + g // 2
            ps = slice(hf * 64, hf * 64 + 64)
            sb = Sb[s // 2]
            mm(S3[:, s, 0:128], QT[ps, ch, qc0:qc0 + T], ktp(ps), True, True, [QT.buf, ktp_buf], [sb])
            mm(S3[:, s, 128:NK], QT[ps, ch, qc0:qc0 + T], kto(ps), True, True, [QT.buf, kto_buf], [sb])
        mx, den, es = sm[0:T, 0:4], sm[0:T, 4:8], sm[0:T, 8:12]
        mx3 = mx.rearrange("p (a b) -> p a b", a=2)
        es3 = es.rearrange("p (a b) -> p a b", a=2)
        b0 = j * 16 + kvh * 4
        sk3 = sinkb[0:T, b0:b0 + 4].rearrange("p (hi lo) -> p lo hi", hi=2)
        TT(SS[0:T, :, 0:NK], S3[:, :, 0:NK], mask[0:T, 0:NK].unsqueeze(1).to_broadcast([T, 4, NK]), ALU.add,
           [Sb[0], Sb[1], mask], [SS.buf])
        TRED(mx, SS[0:T, :, 0:NK], ALU.max, [SS.buf], [sm.buf])
        TT(mx3, mx3, sk3, ALU.max, [sm.buf, sinkb], [sm.buf])
        TT(SS[0:T, :, 0:NK], SS[0:T, :, 0:NK], mx.unsqueeze(2).to_broadcast([T, 4, NK]), ALU.subtract,
           [SS.buf, sm.buf], [SS.buf])
        ACTF(PBt[0:T, :, 0:NK], SS[0:T, :, 0:NK], AF.Exp, [SS.buf], [PBt.buf])
        TRED(den, PBt[0:T, :, 0:NK], ALU.add, [PBt.buf], [sm.buf])
        TT(es3, sk3, mx3, ALU.subtract, [sm.buf, sinkb], [sm.buf])
        ACTF(es, es, AF.Exp, [sm.buf], [sm.buf])
        TT(den, den, es, ALU.add, [sm.buf], [sm.buf])
        RECIP(den, den, [sm.buf], [sm.buf])
        for s in range(4):
            tr(A_bf[:, s * 2, 0:T], PBt[0:T, s, 0:128], identb[0:T, 0:T], [PBt.buf, identb], [Ap])
            tr(A_bf[:, s * 2 + 1, 0:T], PBt[0:T, s, 128:NK], identb[0:T, 0:T], [PBt.buf, identb], [Ap])
        ACOPY(PTS[:, :, 0:T], A_bf[:, :, 0:T], [Ap], [PTS.buf])
        O3 = Sp.t[0:T, 1024:1280].rearrange("p (g d) -> p g d", g=4)
        for s in range(4):
            mm(O3[:, s, :], PTS[:, s * 2, 0:T], vp, True, False, [PTS.buf, vp_buf], [Sb[2]])
            mm(O3[:, s, :], PTS[:, s * 2 + 1, 0:T], vo, False, True, [PTS.buf, vo_buf], [Sb[2]])
        TT(Odst.rearrange("p (a b) d -> p a b d", a=2), O3.rearrange("p (a b) d -> p b a d", a=2),
           den.rearrange("p (a b) -> p b a", a=2).unsqueeze(3).to_broadcast([T, 2, 2, 64]), ALU.mult,
           [Sb[2], sm.buf], [Odst_buf])

    def attn_phase(i):
        j = i // 2
        ph = "attn"
        arena_off[0] = 0
        W = aalloc(ph, "W", [128, 8, 2048], BF16)
        WO = aalloc(ph, "WO", [128, 8, 1024], BF16)
        QT = aalloc(ph, "QT", [128, 8, 128], BF16)
        KT = [aalloc(ph, "KT%d" % t, [128, 4, 128], BF16) for t in range(2)]
        VB = [aalloc(ph, "VB%d" % t, [128, 256], BF16) for t in range(2)]
        KVTOK = aalloc(ph, "KVTOK", [128, 512], F32)
        SS = aalloc(ph, "SS", [128, 4, 256], F32)
        PBt = aalloc(ph, "PB", [128, 4, 256], BF16)
        PTS = aalloc(ph, "PTS", [128, 8, 128], BF16)
        sm = aalloc(ph, "sm", [128, 16], F32)
        KTC = [aalloc(ph, "KTC%d" % t, [128, 4, 128], BF16) for t in range(2)]
        VC = [aalloc(ph, "VC%d" % t, [128, 256], BF16) for t in range(2)]
        VOWN = aalloc(ph, "VOWN", [128, 256], BF16)
        KTO = aalloc(ph, "KTO", [128, 4, 128], BF16)
        OB4 = [aalloc(ph, "OB4%d" % t, [128, 1024], BF16) for t in range(2)]
        barrier()
        load_w(W, 8, 2048, wqkv_d[j], wqkv_d)
        load_w(WO, 8, 1024, wo_d[j], wo_d)
        load_ln(i, 0)
        MEMSET(KT[1][:], 0.0, [KT[1].buf])
        MEMSET(VB[1][:], 0.0, [VB[1].buf])
        for (u, T) in units:
            samp = (u == SU)
            cur = u % 2
            prv = 1 - cur
            make_xT(u, T)
            proj_fm(W, 0, 8, T, QT, 0.125)
            proj_fm(W, 1024, 4, T, KT[cur], 1.0)
            pb, pt = next_pj()
            for k in range(8):
                mm(pt[0:T, 0:512], XT[:, k, 0:T], W[:, k, 1536:2048], k == 0, k == 7, [XT, W.buf], [pb])
            ACOPY(KVTOK[0:T, :], pt[0:T, 0:512], [pb], [KVTOK.buf])
            if not samp:
                TCOPY(VB[cur][0:T, :], KVTOK[0:T, 256:512], [KVTOK.buf], [VB[cur].buf])
                mask = maskF if u == NH else maskA
                for kvh in range(4):
                    attn_group(T, j, kvh, QT, 0,
                               lambda ps, kvh=kvh, prv=prv: KT[prv][ps, kvh, :], KT[prv].buf,
                               lambda ps, kvh=kvh, cur=cur: KT[cur][ps, kvh, :], KT[cur].buf,
                               VB[prv][:, kvh * 64:(kvh + 1) * 64], VB[prv].buf,
                               VB[cur][:, kvh * 64:(kvh + 1) * 64], VB[cur].buf,
                               mask, SS, PBt, PTS, sm,
                               OB[0:T, kvh * 256:(kvh + 1) * 256].rearrange("p (g d) -> p g d", g=4), OB)
                if u == NPB - 1:
                    st(wkp_d[j], KVTOK[:, 0:256], [KVTOK.buf], wkp_d)
                    st(wvp_d[j], KVTOK[:, 256:512], [KVTOK.buf], wvp_d)
            else:
                st(wks_d[j][:, 0:124, :], cwk_d[j][:, 4:128, :], [cwk_d], wks_d)
                st(wvs_d[j][:, 0:124, :], cwv_d[j][:, 4:128, :], [cwv_d], wvs_d)
                for b in range(NSB):
                    st(wks_d[j, b, 124:128, :], KVTOK[b * 4:(b + 1) * 4, 0:256], [KVTOK.buf], wks_d)
                    st(wvs_d[j, b, 124:128, :], KVTOK[b * 4:(b + 1) * 4, 256:512], [KVTOK.buf], wvs_d)
                MEMSET(KTO[:], 0.0, [KTO.buf])
                MEMSET(VOWN[:], 0.0, [VOWN.buf])
                for b in range(NSB):
                    t2 = b % 2
                    ld(KTC[t2][:], cwkT_d[j, :, b], [cwkT_d], [KTC[t2].buf], eng="pool")
                    ld(VC[t2][:], cwv_d[j, b], [cwv_d], [VC[t2].buf], eng="pool")
                    TCOPY(KTO[:, :, 0:4], KT[cur][:, :, b * 4:(b + 1) * 4], [KT[cur].buf], [KTO.buf], eng="pool")
                    pb, pt = next_pj()
                    for k in range(8):
                        mm(pt[0:4, 0:256], XT[:, k, b * 4:(b + 1) * 4], W[:, k, 1792:2048], k == 0, k == 7,
                           [XT, W.buf], [pb])
                    ACOPY(VOWN[0:4, :], pt[0:4, 0:256], [pb], [VOWN.buf])
                    for kvh in range(4):
                        attn_group(4, j, kvh, QT, b * 4,
                                   lambda ps, kvh=kvh, t2=t2: KTC[t2][ps, kvh, :], KTC[t2].buf,
                                   lambda ps, kvh=kvh: KTO[ps, kvh, :], KTO.buf,
                                   VC[t2][:, kvh * 64:(kvh + 1) * 64], VC[t2].buf,
                                   VOWN[:, kvh * 64:(kvh + 1) * 64], VOWN.buf,
                                   maskA, SS, PBt, PTS, sm,
                                   OB4[t2][0:4, kvh * 256:(kvh + 1) * 256].rearrange("p (g d) -> p g d", g=4),
                                   OB4[t2].buf)
                    ld(OB[b * 4:(b + 1) * 4, :], OB4[t2][0:4, :], [OB4[t2].buf], [OB])
            transpose8(OB, OT, T)
            proj_F(WO, T, OT, OT)
            resid_ln(u, T, Fp[0:T, :], [Fp])

    def conv_phase(i):
        j = i // 2
        ph = "conv"
        arena_off[0] = 0
        W = aalloc(ph, "W", [128, 8, 3072], BF16)
        WO = aalloc(ph, "WO", [128, 8, 1024], BF16)
        UP = aalloc(ph, "UP", [128, 8, 1, 130], F32)
        UPS = aalloc(ph, "UPS", [128, 8, NSB, 6], F32)
        STT_ = aalloc(ph, "STT", [128, 8, NSB, 2], F32)
        HS = aalloc(ph, "HS", [128, 128], F32)
        Z = aalloc(ph, "Z", [128, 128], F32)
        GZ = aalloc(ph, "GZ", [128, 8, 128], BF16)
        UC = aalloc(ph, "UC", [128, 8, 32], F32)
        USB = aalloc(ph, "USB", [128, 1024], F32)
        barrier()
        load_w(W, 8, 3072, win_d[j], win_d)
        load_w(WO, 8, 1024, wout_d[j], wout_d)
        load_ln(i, 0)
        MEMSET(UP[:, :, :, 0:2], 0.0, [UP.buf])
        ld(STT_[:].rearrange("p a b c -> p (a b c)"), stT_d[j], [stT_d], [STT_.buf])
        TCOPY(UPS[:, :, :, 0:2], STT_[:], [STT_.buf], [UPS.buf], eng="pool")
        for (u, T) in units:
            samp = (u == SU)
            up = UPS if samp else UP
            nb, L = (NSB, 4) if samp else (1, 128)
            if u == NH:
                TSC(UP[:, :, :, 0:2], UP[:, :, :, 0:2], cflag[:, 0:1], None, ALU.mult, None, [UP.buf, cflag], [UP.buf])
            make_xT(u, T)

            def v3(ap, nb=nb):
                return ap.rearrange("p (b l) -> p b l", b=nb)
            for c in range(8):
                pb, pt = next_pj()
                pv = pt.rearrange("p (c t) -> p c t", c=4)
                for part in range(3):
                    for k in range(8):
                        mm(pv[:, part, 0:T], W[:, k, part * 1024 + c * 128: part * 1024 + (c + 1) * 128], XT[:, k, 0:T],
                           k == 0, k == 7, [W.buf, XT], [pb])
                ACOPY(HS[:, 0:T], pv[:, 2, 0:T], [pb], [HS.buf])
                TT(up[:, c, :, 2:2 + L], v3(pv[:, 1, 0:T]), v3(HS[:, 0:T]), ALU.mult, [pb, HS.buf], [up.buf])
                o = (j * 8 + c) * 3
                TSC(v3(Z[:, 0:T]), up[:, c, :, 0:L], CW[:, o:o + 1], None, ALU.mult, None, [up.buf, CW], [Z.buf])
                STT(v3(Z[:, 0:T]), up[:, c, :, 1:1 + L], CW[:, o + 1:o + 2], v3(Z[:, 0:T]), ALU.mult, ALU.add,
                    [up.buf, CW, Z.buf], [Z.buf])
                STT(v3(Z[:, 0:T]), up[:, c, :, 2:2 + L], CW[:, o + 2:o + 3], v3(Z[:, 0:T]), ALU.mult, ALU.add,
                    [up.buf, CW, Z.buf], [Z.buf])
                TT(GZ[:, c, 0:T], pv[:, 0, 0:T], Z[:, 0:T], ALU.mult, [pb, Z.buf], [GZ.buf])
            proj_F(WO, T, GZ, GZ.buf)
            resid_ln(u, T, Fp[0:T, :], [Fp])
            if samp:
                state_out(UC[:, :, 0:NSB * 2].rearrange("p a (b r) -> p a b r", r=2), UPS[:, :, :, 4:6], [UPS.buf],
                          NSB * 2, UC, USB, cvs_d[j], cvs_d)
            else:
                if u == NPB - 1:
                    state_out(UC[:, :, 0:2], UP[:, :, 0, 128:130], [UP.buf], 2, UC, USB, cvp_d[j], cvp_d)
                TCOPY(UP[:, :, :, 0:2], UP[:, :, :, 128:130], [UP.buf], [UP.buf], eng="pool")

    def mem_unit(T, QT, qc0, MKt, MVt, SS, PBt, PTS, sm, Odst, Odst_buf):
        S3 = Sp.t[0:T, 0:1024].rearrange("p (g s) -> p g s", g=4)
        for h in range(4):
            sb = Sb[h // 2]
            mm(S3[:, h, :], QT[:, 2 * h, qc0:qc0 + T], MKt[:, 2 * h, :], True, False, [QT.buf, MKt.buf], [sb])
            mm(S3[:, h, :], QT[:, 2 * h + 1, qc0:qc0 + T], MKt[:, 2 * h + 1, :], False, True, [QT.buf, MKt.buf], [sb])
        mx, den = sm[0:T, 0:4], sm[0:T, 4:8]
        TRED(mx, S3, ALU.max, [Sb[0], Sb[1]], [sm.buf])
        TT(SS[0:T, :, :], S3, mx.unsqueeze(2).to_broadcast([T, 4, 256]), ALU.subtract, [Sb[0], Sb[1], sm.buf], [SS.buf])
        ACTF(PBt[0:T, :, :], SS[0:T, :, :], AF.Exp, [SS.buf], [PBt.buf])
        TRED(den, PBt[0:T, :, :], ALU.add, [PBt.buf], [sm.buf])
        RECIP(den, den, [sm.buf], [sm.buf])
        for h in range(4):
            for mc in range(2):
                tr(A_bf[:, h * 2 + mc, 0:T], PBt[0:T, h, mc * 128:(mc + 1) * 128], identb[0:T, 0:T], [PBt.buf, identb], [Ap])
        ACOPY(PTS[:, :, 0:T], A_bf[:, :, 0:T], [Ap], [PTS.buf])
        for h in range(4):
            for mc in range(2):
                mm(Fp[0:T, h * 256:(h + 1) * 256], PTS[:, h * 2 + mc, 0:T], MVt[:, mc, h * 256:(h + 1) * 256],
                   mc == 0, mc == 1, [PTS.buf, MVt.buf], [Fp])
        TT(Odst, Fp[0:T, :].rearrange("p (h d) -> p h d", h=4), den.unsqueeze(2).to_broadcast([T, 4, 256]), ALU.mult,
           [Fp, sm.buf], [Odst_buf])

    def mem_phase(i):
        ph = "mem"
        arena_off[0] = 0
        MK = aalloc(ph, "MK", [128, 8, 256], BF16)
        MVt = aalloc(ph, "MV", [128, 2, 1024], BF16)
        mark = arena_off[0]
        WKV = aalloc(ph, "WKV", [128, 8, 2048], BF16)
        MEMT = aalloc(ph, "MEMT", [128, 8, 256], BF16)
        Y2 = aalloc(ph, "Y2", [128, 1024], F32)
        barrier()
        load_w(WKV, 8, 2048, mwkv_d[i], mwkv_d)
        ld(MEMT[:].rearrange("p a b -> p (a b)"), memT_d[:], [memT_d], [MEMT.buf], eng="pool")
        for mc in range(2):
            for kv in range(2):
                for n in range(2):
                    for k in range(8):
                        mm(Fp[:, n * 512:(n + 1) * 512], MEMT[:, k, mc * 128:(mc + 1) * 128],
                           WKV[:, k, kv * 1024 + n * 512: kv * 1024 + (n + 1) * 512], k == 0, k == 7,
                           [MEMT.buf, WKV.buf], [Fp])
                if kv == 0:
                    ACOPY(Y[:, :], Fp[:, :], [Fp], [Y])
                    st(mkp_d[i, mc * 128:(mc + 1) * 128, :], Y[:, :], [Y], mkp_d)
                else:
                    ACOPY(Y2[:, :], Fp[:, :], [Fp], [Y2.buf])
                    st(mvp_d[i, mc * 128:(mc + 1) * 128, :], Y2[:, :], [Y2.buf], mvp_d)
                    TCOPY(MVt[:, mc, :], Fp[:, :], [Fp], [MVt.buf])
        for c0 in range(0, 8, 2):
            pb, pt = next_pj()
            pv = pt.rearrange("p (c t) -> p c t", c=2)
            for c in range(2):
                for k in range(8):
                    mm(pv[:, c, :], WKV[:, k, (c0 + c) * 128:(c0 + c + 1) * 128], MEMT[:, k, :], k == 0, k == 7,
                       [WKV.buf, MEMT.buf], [pb])
            ACOPY(MK[:, c0:c0 + 2, :], pv[:, :, :], [pb], [MK.buf])
        arena_off[0] = mark
        WQ = aalloc(ph, "WQ", [128, 8, 1024], BF16)
        WO = aalloc(ph, "WO", [128, 8, 1024], BF16)
        QT = aalloc(ph, "QT", [128, 8, 128], BF16)
        SS = aalloc(ph, "SS", [128, 4, 256], F32)
        PBt = aalloc(ph, "PB", [128, 4, 256], BF16)
        PTS = aalloc(ph, "PTS", [128, 8, 128], BF16)
        sm = aalloc(ph, "sm", [128, 16], F32)
        MKC = [aalloc(ph, "MKC%d" % t, [128, 8, 256], BF16) for t in range(2)]
        MVC = [aalloc(ph, "MVC%d" % t, [128, 2, 1024], BF16) for t in range(2)]
        OB4 = [aalloc(ph, "OB4%d" % t, [128, 1024], BF16) for t in range(2)]
        barrier()
        load_w(WQ, 8, 1024, mwq_d[i], mwq_d)
        load_w(WO, 8, 1024, mwo_d[i], mwo_d)
        load_ln(i, 1)
        for (u, T) in units:
            samp = (u == SU)
            make_xT(u, T)
            proj_fm(WQ, 0, 8, T, QT, 1.0 / 16.0)
            if not samp:
                mem_unit(T, QT, 0, MK, MVt, SS, PBt, PTS, sm, OB[0:T, :].rearrange("p (h d) -> p h d", h=4), OB)
            else:
                for b in range(NSB):
                    t2 = b % 2
                    ld(MKC[t2][:].rearrange("p a b -> p (a b)"), cmkT_d[i, b], [cmkT_d], [MKC[t2].buf], eng="pool")
                    ld(MVC[t2][:], cmv_d[i, b].rearrange("(mc p) c -> p mc c", p=128), [cmv_d], [MVC[t2].buf], eng="pool")
                    mem_unit(4, QT, b * 4, MKC[t2], MVC[t2], SS, PBt, PTS, sm,
                             OB4[t2][0:4, :].rearrange("p (h d) -> p h d", h=4), OB4[t2].buf)
                    ld(OB[b * 4:(b + 1) * 4, :], OB4[t2][0:4, :], [OB4[t2].buf], [OB])
            transpose8(OB, OT, T)
            proj_F(WO, T, OT, OT)
            resid_ln(u, T, Fp[0:T, :], [Fp])

    def peer_phase(i, last):
        ph = "peer"
        arena_off[0] = 0
        W = aalloc(ph, "W", [128, 8, 2048], BF16)
        SK = aalloc(ph, "SK", [128, 2, 128], BF16)
        QT = aalloc(ph, "QT", [128, 16, 128], BF16)
        SV = aalloc(ph, "SV", [128, 16, 16], F32)
        SI = aalloc(ph, "SI", [128, 16, 16], U32)
        SIF = aalloc(ph, "SIF", [128, 16, 16], F32)
        SW = aalloc(ph, "SW", [128, 256], F32)
        COMB = aalloc(ph, "COMB", [128, 8, 256], F32)
        EQ = aalloc(ph, "EQ", [128, 8, 256], F32)
        CS = aalloc(ph, "CS", [128, 8, 16], F32)
        CI = aalloc(ph, "CI", [128, 8, 16], U32)
        CA = aalloc(ph, "CA", [128, 8, 16], U32)
        CAF = aalloc(ph, "CAF", [128, 8, 16], F32)
        GE = aalloc(ph, "GE", [128, 8, 16], F32)
        I1 = aalloc(ph, "I1", [128, 8, 16], F32)
        I2 = aalloc(ph, "I2", [128, 8, 16], F32)
        EIDI = aalloc(ph, "EIDI", [128, 128], I32)
        AA = aalloc(ph, "AA", [128, 128], F32)
        WW = aalloc(ph, "WW", [128, 128], F32)
        sm = aalloc(ph, "sm", [128, 16], F32)
        ACC = aalloc(ph, "ACC", [128, 1024], F32)
        JUNK = aalloc(ph, "JUNK", [128, 1024], BF16)
        NG = 8
        RING = [aalloc(ph, "RING%d" % t, [128, 1024], F32) for t in range(NG)]
        barrier()
        load_w(W, 8, 2048, pwq_d[i], pwq_d)
        for c in range(2):
            ld(SK[:, c, :], skT_d[i, c], [skT_d], [SK.buf], eng="pool")
        load_ln(i, 2)
        MEMSET(EIDI[:], 0, [EIDI.buf])
        gi = [0]
        for (u, T) in units:
            x = X[u]
            make_xT(u, T)
            proj_fm(W, 0, 16, T, QT, 1.0)
            S3 = Sp.t[0:T, :].rearrange("p (c n) -> p c n", c=16)
            for c in range(16):
                mm(S3[:, c, :], QT[:, c, 0:T], SK[:, c % 2, :], True, True, [QT.buf, SK.buf], [Sb[c // 4]])
            for c in range(16):
                sb = Sb[c // 4]
                src = S3[:, c, :]
                MAX8(SV[0:T, c, 0:8], src, [sb], [SV.buf])
                MAXIDX(SI[0:T, c, 0:8], SV[0:T, c, 0:8], src, [sb, SV.buf], [SI.buf])
                MATCHREP(SW[0:T, 0:128], SV[0:T, c, 0:8], src, [sb, SV.buf], [SW.buf])
                MAX8(SV[0:T, c, 8:16], SW[0:T, 0:128], [SW.buf], [SV.buf])
                MAXIDX(SI[0:T, c, 8:16], SV[0:T, c, 8:16], SW[0:T, 0:128], [SW.buf, SV.buf], [SI.buf])
            SV4 = SV[0:T].rearrange("p (h c) k -> p h c k", c=2)
            SIF4 = SIF[0:T].rearrange("p (h c) k -> p h c k", c=2)
            C4 = COMB[0:T].rearrange("p h (a b) -> p h a b", a=16)
            E4 = EQ[0:T].rearrange("p h (a b) -> p h a b", a=16)
            TCOPY(SIF[0:T], SI[0:T], [SI.buf], [SIF.buf])
            TT(C4, SV4[:, :, 0, :].unsqueeze(3).to_broadcast([T, 8, 16, 16]),
               SV4[:, :, 1, :].unsqueeze(2).to_broadcast([T, 8, 16, 16]), ALU.add, [SV.buf], [COMB.buf])
            for h in range(8):
                src = COMB[0:T, h, :]
                MAX8(CS[0:T, h, 0:8], src, [COMB.buf], [CS.buf])
                MAXIDX(CI[0:T, h, 0:8], CS[0:T, h, 0:8], src, [COMB.buf, CS.buf], [CI.buf])
                MATCHREP(SW[0:T, :], CS[0:T, h, 0:8], src, [COMB.buf, CS.buf], [SW.buf])
                MAX8(CS[0:T, h, 8:16], SW[0:T, :], [SW.buf], [CS.buf])
                MAXIDX(CI[0:T, h, 8:16], CS[0:T, h, 8:16], SW[0:T, :], [SW.buf, CS.buf], [CI.buf])
            TT(GE[0:T], CS[0:T], CS[0:T, :, 0:1].to_broadcast([T, 8, 16]), ALU.subtract, [CS.buf], [GE.buf])
            ACTF(GE[0:T], GE[0:T], AF.Exp, [GE.buf], [GE.buf])
            TRED(sm[0:T, 0:8], GE[0:T], ALU.add, [GE.buf], [sm.buf])
            RECIP(sm[0:T, 0:8], sm[0:T, 0:8], [sm.buf], [sm.buf])
            TT(GE[0:T], GE[0:T], sm[0:T, 0:8].unsqueeze(2).to_broadcast([T, 8, 16]), ALU.mult, [GE.buf, sm.buf], [GE.buf])
            for which, Idst in ((0, I1), (1, I2)):
                if which == 0:
                    TSS(CA[0:T], CI[0:T], 4, ALU.logical_shift_right, [CI.buf], [CA.buf])
                else:
                    TSS(CA[0:T], CI[0:T], 15, ALU.bitwise_and, [CI.buf], [CA.buf])
                TCOPY(CAF[0:T], CA[0:T], [CA.buf], [CAF.buf])
                TT(E4, iota16[0:T, :].unsqueeze(1).unsqueeze(1).to_broadcast([T, 8, 16, 16]),
                   CAF[0:T].unsqueeze(3).to_broadcast([T, 8, 16, 16]), ALU.is_equal, [iota16, CAF.buf], [EQ.buf])
                TT(E4, E4, SIF4[:, :, which, :].unsqueeze(2).to_broadcast([T, 8, 16, 16]), ALU.mult,
                   [EQ.buf, SIF.buf], [EQ.buf])
                TRED(Idst[0:T], E4, ALU.add, [EQ.buf], [Idst.buf])
            STT(I1[0:T], I1[0:T], 128.0, I2[0:T], ALU.mult, ALU.add, [I1.buf, I2.buf], [I1.buf])
            TSC(I1[0:T], I1[0:T], float(i * 16384), None, ALU.add, None, [I1.buf], [I1.buf])
            TCOPY(EIDI[0:T, :], I1[0:T].rearrange("p h k -> p (h k)"), [I1.buf], [EIDI.buf])
            for jj in range(128):
                slot = RING[gi[0] % NG]
                gi[0] += 1
                GATHER(slot[:, :], pu_d[:, :], EIDI[:, jj:jj + 1], [EIDI.buf, pu_d], [slot.buf])
                TTR(JUNK[0:T, :], x[0:T, :], slot[0:T, :], AA[0:T, jj:jj + 1], [x, slot.buf], [JUNK.buf, AA.buf])
            ACTF(WW[0:T, :], AA[0:T, :], AF.Gelu, [AA.buf], [WW.buf])
            TT(WW[0:T, :], WW[0:T, :], GE[0:T].rearrange("p h k -> p (h k)"), ALU.mult, [WW.buf, GE.buf], [WW.buf])
            for jj in range(128):
                slot = RING[gi[0] % NG]
                gi[0] += 1
                GATHER(slot[:, :], pv_d[:, :], EIDI[:, jj:jj + 1], [EIDI.buf, pv_d], [slot.buf])
                if jj == 0:
                    TSC(ACC[0:T, :], slot[0:T, :], WW[0:T, 0:1], None, ALU.mult, None, [slot.buf, WW.buf], [ACC.buf])
                else:
                    STT(ACC[0:T, :], slot[0:T, :], WW[0:T, jj:jj + 1], ACC[0:T, :], ALU.mult, ALU.add,
                        [slot.buf, WW.buf, ACC.buf], [ACC.buf])
            resid_ln(u, T, ACC[0:T, :], [ACC.buf])
            if last:
                if u == SU:
                    st(ys_d[:, :], x[0:TS, :], [x], ys_d)
                elif u >= NH:
                    st(yp_d[(u - NH) * 128:(u - NH + 1) * 128, :], x[:, :], [x], yp_d)

    import os
    PH = os.environ.get("KPHASES", "acmp")
    for i in range(D):
        if i % 2 == 0:
            if "a" in PH:
                attn_phase(i)
        else:
            if "c" in PH:
                conv_phase(i)
        if "m" in PH:
            mem_phase(i)
        if "p" in PH:
            peer_phase(i, i == D - 1)
    P.emit()
    return nc, stack, P


def prep_inputs(cfg, inp):
    D, B, SEQ, DB, n = cfg.D, cfg.B, cfg.SEQ, cfg.DB, cfg.n
    NA, NC, NSB, NH, NOWN, cps = cfg.NA, cfg.NC, cfg.NSB, cfg.NH, cfg.NOWN, cfg.cps
    f = lambda a: np.ascontiguousarray(np.asarray(a, dtype=np.float32))
    g = {k: np.asarray(v) for k, v in inp.items()}
    wqkv_src = g["attn_w_qkv"]
    q = wqkv_src[:, :, 0:1024]
    kk = wqkv_src[:, :, 1024:1280]
    vv = wqkv_src[:, :, 1280:1536]
    kdup = np.concatenate([np.concatenate([kk[:, :, h * 64:(h + 1) * 64]] * 2, axis=2) for h in range(4)], axis=2)
    wqkv = f(np.concatenate([q, kdup, kk, vv], axis=2))
    ii = np.arange(128)
    maskA = np.full((128, 256), NEG, np.float32)
    maskA[:, 0:128][ii[None, :] >= ii[:, None]] = 0.0
    maskA[:, 128:256][ii[None, :] <= ii[:, None]] = 0.0
    maskN = maskA.copy()
    maskN[:, 0:128] = NEG
    ncv = max(NC, 1)
    if NC > 0:
        cw = g["conv_w"].reshape(NC, 3, 8, 128)
        cw = f(np.transpose(cw, (3, 0, 2, 1)).reshape(128, NC * 8 * 3))
        win = f(g["conv_w_in"])
        wout = f(g["conv_w_out"])
    else:
        cw = np.zeros((128, 24), np.float32)
        win = np.zeros((1, 1024, 3072), np.float32)
        wout = np.zeros((1, 1024, 1024), np.float32)
    skT = f(np.transpose(g["peer_sub_keys"], (0, 1, 3, 2)))
    shared = {
        "maskA": maskA,
        "wqkv": wqkv, "wo": f(g["attn_w_o"]), "sinks": f(g["attn_sinks"]).reshape(1, NA * 16),
        "win": win, "cw": cw, "wout": wout,
        "mwq": f(g["mem_w_q"]), "mwkv": f(g["mem_w_kv"]), "mwo": f(g["mem_w_o"]),
        "pwq": f(g["peer_w_q"]), "skT": skT,
        "pu": f(g["peer_u"]).reshape(D * 16384, 1024), "pv": f(g["peer_v"]).reshape(D * 16384, 1024),
        "lng": f(g["ln_g"]), "lnb": f(g["ln_b"]),
    }
    maps = []
    for c in range(n):
        b, qd = c // cps, c % cps
        t0 = qd * NOWN * 128
        xp = np.zeros(((NOWN + NH) * 128, 1024), np.float32)
        lo = t0 - NH * 128
        if lo >= 0:
            xp[:] = g["x_prompt"][b, lo:t0 + NOWN * 128]
        else:
            xp[NH * 128:] = g["x_prompt"][b, t0:t0 + NOWN * 128]
        first = (qd == 0)
        sb = slice(c * NSB, (c + 1) * NSB)
        cwk = g["cache_win_k"][:, sb].reshape(NA, NSB, 128, 256)
        cwv = g["cache_win_v"][:, sb].reshape(NA, NSB, 128, 256)
        kT = np.transpose(g["cache_win_k"][:, sb], (0, 4, 1, 3, 2))
        cwkT = np.concatenate([kT, kT], axis=1)
        if NC > 0:
            stt = g["state_conv"][:, sb].reshape(NC, NSB, 2, 8, 128)
            stT = np.transpose(stt, (0, 4, 3, 1, 2)).reshape(NC, 128, 8 * NSB * 2)
        else:
            stT = np.zeros((1, 128, 8 * NSB * 2), np.float32)
        memT = np.transpose(g["mem_prompt"][b].reshape(256, 8, 128), (2, 1, 0)).reshape(128, 8 * 256)
        cmk = g["cache_mem_k"][:, sb].reshape(D, NSB, 256, 4, 2, 128)
        cmkT = np.transpose(cmk, (0, 1, 5, 3, 4, 2)).reshape(D, NSB, 128, 8 * 256)
        cmv = g["cache_mem_v"][:, sb].reshape(D, NSB, 256, 1024)
        m = dict(shared)
        m.update({
            "xp": f(xp), "xs": f(g["x_sample"][sb].reshape(NSB * 4, 1024)),
            "maskF": maskN if first else maskA,
            "cflag": np.full((128, 1), 0.0 if first else 1.0, np.float32),
            "cwkT": f(cwkT), "cwk": f(cwk), "cwv": f(cwv), "stT": f(stT),
            "memT": f(memT), "cmkT": f(cmkT), "cmv": f(cmv),
        })
        maps.append(m)
    return maps


def assemble(cfg, res):
    D, B, SEQ, DB, n = cfg.D, cfg.B, cfg.SEQ, cfg.DB, cfg.n
    NA, NC, NSB, NOWN, cps = cfg.NA, cfg.NC, cfg.NSB, cfg.NOWN, cfg.cps
    y_p = np.zeros((B, SEQ, 1024), np.float32)
    y_s = np.zeros((DB, 4, 1024), np.float32)
    wkp = np.zeros((NA, B, 128, 4, 64), np.float32)
    wvp = np.zeros((NA, B, 128, 4, 64), np.float32)
    cvp = np.zeros((NC, B, 2, 1024), np.float32)
    mkp = np.zeros((D, B, 256, 4, 256), np.float32)
    mvp = np.zeros((D, B, 256, 4, 256), np.float32)
    wks = np.zeros((NA, DB, 128, 4, 64), np.float32)
    wvs = np.zeros((NA, DB, 128, 4, 64), np.float32)
    cvs = np.zeros((NC, DB, 2, 1024), np.float32)
    for c in range(n):
        r = res[c]
        b, qd = c // cps, c % cps
        t0 = qd * NOWN * 128
        y_p[b, t0:t0 + NOWN * 128] = r["yp"]
        sb = slice(c * NSB, (c + 1) * NSB)
        y_s[sb] = r["ys"].reshape(NSB, 4, 1024)
        wks[:, sb] = r["wks"].reshape(NA, NSB, 128, 4, 64)
        wvs[:, sb] = r["wvs"].reshape(NA, NSB, 128, 4, 64)
        if NC > 0:
            cvs[:, sb] = r["cvs"].reshape(NC, NSB, 2, 1024)
        if qd == cps - 1:
            wkp[:, b] = r["wkp"].reshape(NA, 128, 4, 64)
            wvp[:, b] = r["wvp"].reshape(NA, 128, 4, 64)
            if NC > 0:
                cvp[:, b] = r["cvp"][:NC]
        if qd == 0:
            mkp[:, b] = r["mkp"].reshape(D, 256, 4, 256)
            mvp[:, b] = r["mvp"].reshape(D, 256, 4, 256)
    return (y_p, y_s, wkp, wvp, cvp, mkp, mvp, wks, wvs, cvs)


def run_cfg(cfg, inputs, trace=False):
    nc, stack, P = build(cfg)
    maps = prep_inputs(cfg, inputs)
    res = run_bass_kernel_spmd(nc, maps, core_ids=list(range(cfg.n)))
    return assemble(cfg, res.results)


def kernel(**inputs):
    cfg = Cfg()
    return run_cfg(cfg, inputs)
```

```python
from contextlib import ExitStack
import numpy as np
import concourse.bass as bass
import concourse.mybir as mybir
from concourse.alu_op_type import AluOpType as ALU
from concourse.bass_utils import run_bass_kernel_spmd

F32 = mybir.dt.float32
BF16 = mybir.dt.bfloat16
I32 = mybir.dt.int32
U32 = mybir.dt.uint32
AF = mybir.ActivationFunctionType
AX = mybir.AxisListType

SAME_ENGINE_SYNC = True


class Buf:
    def __init__(self, prog, name, t, space):
        self.prog = prog
        self.name = name
        self.t = t
        self.space = space
        self.sem = None
        self.dma_cnt = 0
        self.last_w = None
        self.readers = {}

    def __getitem__(self, key):
        return self.t[key]


class Op:
    __slots__ = ("eng", "fn", "deps", "signal", "sig_val", "is_dma", "sem_buf", "dma_val", "idx")

    def __init__(self, eng, fn, is_dma=False):
        self.eng = eng
        self.fn = fn
        self.deps = []
        self.signal = False
        self.sig_val = 0
        self.is_dma = is_dma
        self.sem_buf = None
        self.dma_val = 0


class Prog:
    ENGS = ("pe", "act", "dve", "pool", "sp")

    def __init__(self, nc, stack):
        self.nc = nc
        self.stack = stack
        self.ops = {e: [] for e in self.ENGS}
        self.bufs = []
        self.nops = 0
        self.out_bufs = []
        import os
        self.limit = int(os.environ.get("KCUT", "1000000000"))

    def sbuf(self, name, shape, dtype):
        t = self.stack.enter_context(self.nc.sbuf_tensor(name, list(shape), dtype))
        b = Buf(self, name, t, "sb")
        self.bufs.append(b)
        return b

    def psum(self, name, shape, dtype):
        t = self.stack.enter_context(self.nc.psum_tensor(name, list(shape), dtype))
        b = Buf(self, name, t, "ps")
        self.bufs.append(b)
        return b

    def dram(self, name, shape, dtype, kind):
        t = self.nc.dram_tensor(name, list(shape), dtype, kind=kind).ap()
        b = Buf(self, name, t, "dr")
        self.bufs.append(b)
        if kind == "ExternalOutput":
            self.out_bufs.append(b)
        return b

    def alias(self, name, t, space="sb"):
        b = Buf(self, name, t, space)
        self.bufs.append(b)
        return b

    def _track(self, op, reads, writes):
        deps = op.deps
        ps_reads = [b for b in reads if b.space == "ps"]
        if ps_reads:
            reads = [b for b in reads if b.space != "ps"]
            writes = list(writes) + [b for b in ps_reads if b not in writes]
        for b in reads:
            if b.last_w is not None:
                deps.append(b.last_w)
        for b in writes:
            if b.last_w is not None:
                deps.append(b.last_w)
            for r in b.readers.values():
                deps.append(r)
        for b in reads:
            key = ("dma", id(op)) if op.is_dma else op.eng
            b.readers[key] = op
        for b in writes:
            b.last_w = op
            b.readers = {}

    def op(self, eng, fn, reads=(), writes=()):
        if self.nops >= self.limit:
            return None
        o = Op(eng, fn)
        self._track(o, reads, writes)
        self.ops[eng].append(o)
        self.nops += 1
        return o

    def dma(self, eng, fn, reads=(), writes=(), sem_buf=None):
        if self.nops >= self.limit:
            return None
        o = Op(eng, fn, is_dma=True)
        if sem_buf is None:
            sem_buf = writes[0]
        o.sem_buf = sem_buf
        sem_buf.dma_cnt += 1
        o.dma_val = 16 * sem_buf.dma_cnt
        self._track(o, reads, writes)
        self.ops[eng].append(o)
        self.nops += 1
        return o

    def emit(self):
        nc = self.nc
        stack = self.stack
        for e in self.ENGS:
            for o in self.ops[e]:
                for d in o.deps:
                    if d.is_dma:
                        continue
                    if d.eng == "pe" and o.eng == "pe" and not o.is_dma:
                        continue
                    if (not SAME_ENGINE_SYNC) and d.eng == o.eng and not o.is_dma:
                        continue
                    d.signal = True
        esem = {}
        for e in self.ENGS:
            if e == "sp":
                continue
            esem[e] = stack.enter_context(nc.semaphore("s_" + e))
            c = 0
            for o in self.ops[e]:
                if o.signal and not o.is_dma:
                    c += 1
                    o.sig_val = c
        for b in self.bufs:
            if b.dma_cnt > 0:
                b.sem = stack.enter_context(nc.semaphore("d_" + b.name))
        self.max_sig = {e: max([o.sig_val for o in self.ops[e]] + [0]) for e in self.ENGS}
        block = stack.enter_context(nc.Block())
        engobj = {"pe": block.tensor, "act": block.scalar, "dve": block.vector, "pool": block.gpsimd, "sp": block.sync}
        prog = self

        def make(e):
            def body(eng):
                waited = {}
                for o in prog.ops[e]:
                    for d in o.deps:
                        if d.is_dma:
                            sem, val = d.sem_buf.sem, d.dma_val
                        else:
                            if d.eng == "pe" and e == "pe" and not o.is_dma:
                                continue
                            if (not SAME_ENGINE_SYNC) and d.eng == e and not o.is_dma:
                                continue
                            sem, val = esem[d.eng], d.sig_val
                        k = id(sem)
                        if waited.get(k, 0) >= val:
                            continue
                        waited[k] = val
                        eng.wait_ge(sem, val)
                    ins = o.fn(eng)
                    if o.is_dma:
                        ins.then_inc(o.sem_buf.sem, 16)
                    elif o.signal:
                        ins.then_inc(esem[e], 1)
                if e == "sp":
                    for b in prog.out_bufs:
                        if b.dma_cnt > 0:
                            eng.wait_ge(b.sem, 16 * b.dma_cnt)
            return body

        for e in self.ENGS:
            engobj[e](make(e))


ALPHA = 8.0 ** 0.25
LN_EPS = 1e-5
NEG = -1e30


class Cfg:
    def __init__(s, D=4, B=2, SEQ=8192, DB=128, n_cores=8):
        s.D, s.B, s.SEQ, s.DB, s.n = D, B, SEQ, DB, n_cores
        s.cps = n_cores // B
        s.NOWN = SEQ // 128 // s.cps
        s.NH = 3
        s.NPB = s.NOWN + s.NH
        s.NSB = DB // n_cores
        s.TS = s.NSB * 4
        s.NA = (D + 1) // 2
        s.NC = D // 2


def build(cfg):
    nc = bass.Bass("TRN2", target_bir_lowering=False)
    stack = ExitStack()
    P = Prog(nc, stack)
    D, NPB, NH, NSB, TS, NA, NC = cfg.D, cfg.NPB, cfg.NH, cfg.NSB, cfg.TS, cfg.NA, cfg.NC
    NOWN = cfg.NOWN
    SU = NPB

    def din(name, shape, dt=F32):
        return P.dram(name, shape, dt, "ExternalInput")

    def dout(name, shape, dt=F32):
        return P.dram(name, shape, dt, "ExternalOutput")

    xp_d = din("xp", [NPB * 128, 1024])
    xs_d = din("xs", [TS, 1024])
    maskA_d = din("maskA", [128, 256])
    maskF_d = din("maskF", [128, 256])
    cflag_d = din("cflag", [128, 1])
    wqkv_d = din("wqkv", [NA, 1024, 2048])
    wo_d = din("wo", [NA, 1024, 1024])
    sinks_d = din("sinks", [1, NA * 16])
    cwkT_d = din("cwkT", [NA, 128, NSB, 4, 128])
    cwk_d = din("cwk", [NA, NSB, 128, 256])
    cwv_d = din("cwv", [NA, NSB, 128, 256])
    win_d = din("win", [max(NC, 1), 1024, 3072])
    cw_d = din("cw", [128, max(NC, 1) * 8 * 3])
    wout_d = din("wout", [max(NC, 1), 1024, 1024])
    stT_d = din("stT", [max(NC, 1), 128, 8 * NSB * 2])
    mwq_d = din("mwq", [D, 1024, 1024])
    mwkv_d = din("mwkv", [D, 1024, 2048])
    mwo_d = din("mwo", [D, 1024, 1024])
    memT_d = din("memT", [128, 8 * 256])
    cmkT_d = din("cmkT", [D, NSB, 128, 8 * 256])
    cmv_d = din("cmv", [D, NSB, 256, 1024])
    pwq_d = din("pwq", [D, 1024, 2048])
    skT_d = din("skT", [D, 2, 128, 128])
    pu_d = din("pu", [D * 16384, 1024])
    pv_d = din("pv", [D * 16384, 1024])
    lng_d = din("lng", [D, 3, 1024])
    lnb_d = din("lnb", [D, 3, 1024])

    yp_d = dout("yp", [NOWN * 128, 1024])
    ys_d = dout("ys", [TS, 1024])
    wkp_d = dout("wkp", [NA, 128, 256])
    wvp_d = dout("wvp", [NA, 128, 256])
    cvp_d = dout("cvp", [max(NC, 1), 2, 1024])
    mkp_d = dout("mkp", [D, 256, 1024])
    mvp_d = dout("mvp", [D, 256, 1024])
    wks_d = dout("wks", [NA, NSB, 128, 256])
    wvs_d = dout("wvs", [NA, NSB, 128, 256])
    cvs_d = dout("cvs", [max(NC, 1), NSB * 2, 1024])

    X = [P.sbuf("X%d" % u, [128, 1024], F32) for u in range(NPB + 1)]
    identf = P.sbuf("identf", [128, 128], F32)
    identb = P.sbuf("identb", [128, 128], BF16)
    maskA = P.sbuf("maskA_s", [128, 256], F32)
    maskF = P.sbuf("maskF_s", [128, 256], F32)
    cflag = P.sbuf("cflag_s", [128, 1], F32)
    sinkb = P.sbuf("sinkb", [128, NA * 16], F32)
    CW = P.sbuf("CW", [128, max(NC, 1) * 8 * 3], F32)
    iota16 = P.sbuf("iota16", [128, 16], F32)
    GB = P.sbuf("GB", [128, 1024], F32)
    BB = P.sbuf("BB", [128, 1024], F32)
    XB = P.sbuf("XB", [128, 1024], BF16)
    OB = P.sbuf("OB", [128, 1024], BF16)
    XT = P.sbuf("XT", [128, 8, 128], BF16)
    OT = P.sbuf("OT", [128, 8, 128], BF16)
    Y = P.sbuf("Y", [128, 1024], F32)
    ST = P.sbuf("ST", [128, 12], F32)
    MV2 = P.sbuf("MV2", [128, 2], F32)
    SD = P.sbuf("SD", [128, 1], F32)
    RSTD = P.sbuf("RSTD", [128, 1], F32)
    NMR = P.sbuf("NMR", [128, 1], F32)
    DUM = P.sbuf("DUM", [128, 4], F32)
    tmpi = P.sbuf("tmpi", [128, 128], I32)
    tmpf = P.sbuf("tmpf", [128, 128], F32)
    tmpr = P.sbuf("tmpr", [128, 1], F32)

    ARENA_BYTES = 100 * 1024
    ARENA = P.sbuf("ARENA", [128, ARENA_BYTES // 4], F32)
    arena_views = {F32: ARENA[:], BF16: ARENA[:].bitcast(BF16), I32: ARENA[:].bitcast(I32), U32: ARENA[:].bitcast(U32)}
    esz = {F32: 4, BF16: 2, I32: 4, U32: 4}
    arena_cache = {}
    arena_off = [0]

    class AV:
        def __init__(self, buf, ap):
            self.buf = buf
            self.ap = ap

        def __getitem__(self, key):
            return self.ap[key]

    def aalloc(phase, name, shape, dt):
        n = int(np.prod(shape[1:]))
        nb = (n * esz[dt] + 63) // 64 * 64
        off = arena_off[0]
        arena_off[0] += nb
        assert arena_off[0] <= ARENA_BYTES, (phase, name, arena_off[0])
        key = (phase, name)
        if key in arena_cache:
            assert arena_cache[key][1] == off
            return arena_cache[key][0]
        e0 = off // esz[dt]
        ap = arena_views[dt][:, e0:e0 + n]
        if len(shape) == 3:
            ap = ap.rearrange("p (a b) -> p a b", a=shape[1])
        elif len(shape) == 4:
            ap = ap.rearrange("p (a b c) -> p a b c", a=shape[1], b=shape[2])
        buf = P.alias(phase + "_" + name, ap, "sb")
        av = AV(buf, ap)
        arena_cache[key] = (av, off)
        return av

    def barrier():
        bufs = [v[0].buf for v in arena_cache.values()]
        P.op("dve", lambda e: e.memset(DUM[:, 0:1], 0.0), reads=[], writes=bufs + [DUM])

    Fp = P.psum("Fp", [128, 1024], F32)
    Sp = P.psum("Sp", [128, 2048], F32)
    Ap = P.psum("Ap", [128, 512], F32)
    Bp = P.psum("Bp", [128, 512], F32)
    Sb = [P.alias("Sb%d" % i, Sp.t[:, i * 512:(i + 1) * 512], "ps") for i in range(4)]
    A_bf = Ap[:].bitcast(BF16).rearrange("p (k t) -> p k t", k=8)
    pj_banks = [(Bp, Bp.t), (Sb[3], Sp.t[:, 1536:2048])]
    pj_i = [0]

    def next_pj():
        b = pj_banks[pj_i[0] % 2]
        pj_i[0] += 1
        return b

    def mm(out, lhsT, rhs, start, stop, reads, writes):
        P.op("pe", lambda e: e.matmul(out, lhsT=lhsT, rhs=rhs, start=start, stop=stop), reads=reads, writes=writes)

    def tr(out, in_, ident, reads, writes):
        P.op("pe", lambda e: e.transpose(out=out, in_=in_, identity=ident), reads=reads, writes=writes)

    def dve(f, reads, writes):
        P.op("dve", f, reads=reads, writes=writes)

    def act(f, reads, writes):
        P.op("act", f, reads=reads, writes=writes)

    def pool(f, reads, writes):
        P.op("pool", f, reads=reads, writes=writes)

    def ld(out, in_, reads, writes, eng="sp", **kw):
        P.dma(eng, lambda e: e.dma_start(out=out, in_=in_, **kw), reads=reads, writes=writes)

    def st(out, in_, reads, dbuf, eng="sp", **kw):
        P.dma(eng, lambda e: e.dma_start(out=out, in_=in_, **kw), reads=reads, writes=[dbuf], sem_buf=dbuf)


    def TT(out, in0, in1, op, reads, writes, eng="dve"):
        P.op(eng, lambda e: e.tensor_tensor(out=out, in0=in0, in1=in1, op=op), reads=reads, writes=writes)

    def TSC(out, in0, s1, s2, op0, op1, reads, writes, eng="dve"):
        if op1 is None:
            P.op(eng, lambda e: e.tensor_scalar(out=out, in0=in0, scalar1=s1, scalar2=None, op0=op0), reads=reads, writes=writes)
        else:
            P.op(eng, lambda e: e.tensor_scalar(out=out, in0=in0, scalar1=s1, scalar2=s2, op0=op0, op1=op1), reads=reads, writes=writes)

    def STT(out, in0, scalar, in1, op0, op1, reads, writes):
        P.op("dve", lambda e: e.scalar_tensor_tensor(out=out, in0=in0, scalar=scalar, in1=in1, op0=op0, op1=op1),
             reads=reads, writes=writes)

    def TCOPY(out, in_, reads, writes, eng="dve"):
        P.op(eng, lambda e: e.tensor_copy(out=out, in_=in_), reads=reads, writes=writes)

    def TRED(out, in_, op, reads, writes):
        P.op("dve", lambda e: e.tensor_reduce(out=out, in_=in_, axis=AX.X, op=op), reads=reads, writes=writes)

    def ACTF(out, in_, func, reads, writes):
        P.op("act", lambda e: e.activation(out=out, in_=in_, func=func), reads=reads, writes=writes)

    def ACOPY(out, in_, reads, writes):
        P.op("act", lambda e: e.copy(out=out, in_=in_), reads=reads, writes=writes)

    def AMUL(out, in_, mul, reads, writes):
        P.op("act", lambda e: e.mul(out=out, in_=in_, mul=mul), reads=reads, writes=writes)

    def RECIP(out, in_, reads, writes):
        P.op("dve", lambda e: e.reciprocal(out=out, in_=in_), reads=reads, writes=writes)

    def MEMSET(out, val, writes, eng="pool"):
        P.op(eng, lambda e: e.memset(out, val), reads=[], writes=writes)

    def MAX8(out, in_, reads, writes):
        P.op("dve", lambda e: e.max(out=out, in_=in_), reads=reads, writes=writes)

    def MAXIDX(out, in_max, in_values, reads, writes):
        P.op("dve", lambda e: e.max_index(out=out, in_max=in_max, in_values=in_values), reads=reads, writes=writes)

    def MATCHREP(out, in_to_replace, in_values, reads, writes):
        P.op("dve", lambda e: e.match_replace(out=out, in_to_replace=in_to_replace, in_values=in_values, imm_value=NEG),
             reads=reads, writes=writes)

    def TSS(out, in_, scalar, op, reads, writes):
        P.op("dve", lambda e: e.tensor_single_scalar(out=out, in_=in_, scalar=scalar, op=op), reads=reads, writes=writes)

    def GATHER(slot_ap, table_ap, idx_ap, reads, writes):
        P.dma("pool", lambda e: e.indirect_dma_start(out=slot_ap, out_offset=None, in_=table_ap,
                                                     in_offset=bass.IndirectOffsetOnAxis(ap=idx_ap, axis=0)),
              reads=reads, writes=writes)

    def TTR(out, in0, in1, accum_out, reads, writes):
        P.op("dve", lambda e: e.scalar_tensor_tensor(out=out, in0=in0, scalar=1.0, in1=in1, op0=ALU.mult, op1=ALU.mult,
                                                     accum_out=accum_out), reads=reads, writes=writes)

    def load_w(dst_av, nk, ncols, src_ap, src_buf):
        for k in range(nk):
            ld(dst_av[:, k, 0:ncols], src_ap[k * 128:(k + 1) * 128, :], [src_buf], [dst_av.buf], eng="pool")

    pool(lambda e: e.iota(tmpi[:], pattern=[[1, 128]], base=0, channel_multiplier=0), [], [tmpi])
    dve(lambda e: e.tensor_copy(out=tmpf[:], in_=tmpi[:]), [tmpi], [tmpf])
    dve(lambda e: e.tensor_copy(out=iota16[:], in_=tmpi[:, 0:16]), [tmpi], [iota16])
    pool(lambda e: e.iota(tmpi[:, 0:1], pattern=[[1, 1]], base=0, channel_multiplier=1), [tmpf, iota16], [tmpi])
    dve(lambda e: e.tensor_copy(out=tmpr[:], in_=tmpi[:, 0:1]), [tmpi], [tmpr])
    dve(lambda e: e.tensor_scalar(out=identf[:], in0=tmpf[:], scalar1=tmpr[:, 0:1], scalar2=None, op0=ALU.is_equal),
        [tmpf, tmpr], [identf])
    dve(lambda e: e.tensor_copy(out=identb[:], in_=identf[:]), [identf], [identb])
    ld(maskA[:], maskA_d[:], [maskA_d], [maskA])
    ld(maskF[:], maskF_d[:], [maskF_d], [maskF])
    ld(cflag[:], cflag_d[:], [cflag_d], [cflag])
    ld(sinkb[:], sinks_d[0:1, :].partition_broadcast(128), [sinks_d], [sinkb])
    ld(CW[:], cw_d[:], [cw_d], [CW])
    for u in range(NPB):
        ld(X[u][:], xp_d[u * 128:(u + 1) * 128, :], [xp_d], [X[u]])
    ld(X[SU][0:TS, :], xs_d[:], [xs_d], [X[SU]])

    units = [(u, 128) for u in range(NPB)] + [(SU, TS)]
    import os
    units = units[:int(os.environ.get('KUNITS', '999'))]

    def transpose8(src, dst, T):
        for k in range(8):
            tr(A_bf[:, k, 0:T], src[0:T, k * 128:(k + 1) * 128], identb[0:T, 0:T], [src, identb], [Ap])
        TCOPY(dst[:, :, 0:T], A_bf[:, :, 0:T], [Ap], [dst])

    def make_xT(u, T):
        ACOPY(XB[0:T, :], X[u][0:T, :], [X[u]], [XB])
        transpose8(XB, XT, T)

    def proj_fm(W, col0, nch, T, dst, scale, src=XT):
        for c0 in range(0, nch, 4):
            n = min(4, nch - c0)
            pb, pt = next_pj()
            pv = pt.rearrange("p (c t) -> p c t", c=4)
            for c in range(n):
                for k in range(8):
                    mm(pv[:, c, 0:T], W[:, k, col0 + (c0 + c) * 128: col0 + (c0 + c + 1) * 128], src[:, k, 0:T],
                       k == 0, k == 7, [W.buf, src], [pb])
            AMUL(dst[:, c0:c0 + n, 0:T], pv[:, 0:n, 0:T], scale, [pb], [dst.buf])

    def proj_F(W, T, srcT, srcbuf):
        for n in range(2):
            for k in range(8):
                mm(Fp[0:T, n * 512:(n + 1) * 512], srcT[:, k, 0:T], W[:, k, n * 512:(n + 1) * 512], k == 0, k == 7,
                   [srcbuf, W.buf], [Fp])

    def load_ln(i, j):
        ld(GB[:], lng_d[i, j:j + 1, :].partition_broadcast(128), [lng_d], [GB])
        ld(BB[:], lnb_d[i, j:j + 1, :].partition_broadcast(128), [lnb_d], [BB])

    def resid_ln(u, T, f_ap, f_bufs):
        x = X[u]
        STT(Y[0:T, :], x[0:T, :], ALPHA, f_ap, ALU.mult, ALU.add, [x] + f_bufs, [Y])
        P.op("dve", lambda e: e.bn_stats(out=ST[0:T, 0:6], in_=Y[0:T, 0:512]), reads=[Y], writes=[ST])
        P.op("dve", lambda e: e.bn_stats(out=ST[0:T, 6:12], in_=Y[0:T, 512:1024]), reads=[Y], writes=[ST])
        P.op("dve", lambda e: e.bn_aggr(out=MV2[0:T, :], in_=ST[0:T, :]), reads=[ST], writes=[MV2])
        TSC(SD[0:T, :], MV2[0:T, 1:2], LN_EPS, None, ALU.add, None, [MV2], [SD])
        P.op("act", lambda e: e.sqrt(out=SD[0:T, :], in_=SD[0:T, :]), reads=[SD], writes=[SD])
        RECIP(RSTD[0:T, :], SD[0:T, :], [SD], [RSTD])
        STT(NMR[0:T, :], MV2[0:T, 0:1], -1.0, RSTD[0:T, :], ALU.mult, ALU.mult, [MV2, RSTD], [NMR])
        TSC(x[0:T, :], Y[0:T, :], RSTD[0:T, 0:1], NMR[0:T, 0:1], ALU.mult, ALU.add, [Y, RSTD, NMR], [x])
        TT(x[0:T, :], x[0:T, :], GB[0:T, :], ALU.mult, [x, GB], [x], eng="pool")
        TT(x[0:T, :], x[0:T, :], BB[0:T, :], ALU.add, [x, BB], [x], eng="pool")

    def state_out(uc_out_ap, cols_ap, cols_bufs, n, UC, USB, dst_ap, dst_buf):
        TCOPY(uc_out_ap, cols_ap, cols_bufs, [UC.buf], eng="pool")
        for k in range(8):
            tr(Fp[0:n, k * 128:(k + 1) * 128], UC[:, k, 0:n], identf[:, :], [UC.buf, identf], [Fp])
        ACOPY(USB[0:n, :], Fp[0:n, :], [Fp], [USB.buf])
        st(dst_ap, USB[0:n, :], [USB.buf], dst_buf)

    def attn_group(T, j, kvh, QT, qc0, ktp, ktp_buf, kto, kto_buf, vp, vp_buf, vo, vo_buf, mask, SS, PBt, PTS, sm,
                   Odst, Odst_buf):
        NK = 256
        S3 = Sp.t[0:T, 0:1024].rearrange("p (g s) -> p g s", g=4)
        for g in range(4):
            h = kvh * 4 + g
            ch, hf = h // 2, h % 2
            s = 2 * (g % 2) + g // 2
            ps = slice(hf * 64, hf * 64 + 64)
            sb = Sb[s // 2]
            mm(S3[:, s, 0:128], QT[ps, ch, qc0:qc0 + T], ktp(ps), True, True, [QT.buf, ktp_buf], [sb])
            mm(S3[:, s, 128:NK], QT[ps, ch, qc0:qc0 + T], kto(ps), True, True, [QT.buf, kto_buf], [sb])
        mx, den, es = sm[0:T, 0:4], sm[0:T, 4:8], sm[0:T, 8:12]
        mx3 = mx.rearrange("p (a b) -> p a b", a=2)
        es3 = es.rearrange("p (a b) -> p a b", a=2)
        b0 = j * 16 + kvh * 4
        sk3 = sinkb[0:T, b0:b0 + 4].rearrange("p (hi lo) -> p lo hi", hi=2)
        TT(SS[0:T, :, 0:NK], S3[:, :, 0:NK], mask[0:T, 0:NK].unsqueeze(1).to_broadcast([T, 4, NK]), ALU.add,
           [Sb[0], Sb[1], mask], [SS.buf])
        TRED(mx, SS[0:T, :, 0:NK], ALU.max, [SS.buf], [sm.buf])
        TT(mx3, mx3, sk3, ALU.max, [sm.buf, sinkb], [sm.buf])
        TT(SS[0:T, :, 0:NK], SS[0:T, :, 0:NK], mx.unsqueeze(2).to_broadcast([T, 4, NK]), ALU.subtract,
           [SS.buf, sm.buf], [SS.buf])
        ACTF(PBt[0:T, :, 0:NK], SS[0:T, :, 0:NK], AF.Exp, [SS.buf], [PBt.buf])
        TRED(den, PBt[0:T, :, 0:NK], ALU.add, [PBt.buf], [sm.buf])
        TT(es3, sk3, mx3, ALU.subtract, [sm.buf, sinkb], [sm.buf])
        ACTF(es, es, AF.Exp, [sm.buf], [sm.buf])
        TT(den, den, es, ALU.add, [sm.buf], [sm.buf])
        RECIP(den, den, [sm.buf], [sm.buf])
        for s in range(4):
            tr(A_bf[:, s * 2, 0:T], PBt[0:T, s, 0:128], identb[0:T, 0:T], [PBt.buf, identb], [Ap])
            tr(A_bf[:, s * 2 + 1, 0:T], PBt[0:T, s, 128:NK], identb[0:T, 0:T], [PBt.buf, identb], [Ap])
        ACOPY(PTS[:, :, 0:T], A_bf[:, :, 0:T], [Ap], [PTS.buf])
        O3 = Sp.t[0:T, 1024:1280].rearrange("p (g d) -> p g d", g=4)
        for s in range(4):
            mm(O3[:, s, :], PTS[:, s * 2, 0:T], vp, True, False, [PTS.buf, vp_buf], [Sb[2]])
            mm(O3[:, s, :], PTS[:, s * 2 + 1, 0:T], vo, False, True, [PTS.buf, vo_buf], [Sb[2]])
        TT(Odst.rearrange("p (a b) d -> p a b d", a=2), O3.rearrange("p (a b) d -> p b a d", a=2),
           den.rearrange("p (a b) -> p b a", a=2).unsqueeze(3).to_broadcast([T, 2, 2, 64]), ALU.mult,
           [Sb[2], sm.buf], [Odst_buf])

    def attn_phase(i):
        j = i // 2
        ph = "attn"
        arena_off[0] = 0
        W = aalloc(ph, "W", [128, 8, 2048], BF16)
        WO = aalloc(ph, "WO", [128, 8, 1024], BF16)
        QT = aalloc(ph, "QT", [128, 8, 128], BF16)
        KT = [aalloc(ph, "KT%d" % t, [128, 4, 128], BF16) for t in range(2)]
        VB = [aalloc(ph, "VB%d" % t, [128, 256], BF16) for t in range(2)]
        KVTOK = aalloc(ph, "KVTOK", [128, 512], F32)
        SS = aalloc(ph, "SS", [128, 4, 256], F32)
        PBt = aalloc(ph, "PB", [128, 4, 256], BF16)
        PTS = aalloc(ph, "PTS", [128, 8, 128], BF16)
        sm = aalloc(ph, "sm", [128, 16], F32)
        KTC = [aalloc(ph, "KTC%d" % t, [128, 4, 128], BF16) for t in range(2)]
        VC = [aalloc(ph, "VC%d" % t, [128, 256], BF16) for t in range(2)]
        VOWN = aalloc(ph, "VOWN", [128, 256], BF16)
        KTO = aalloc(ph, "KTO", [128, 4, 128], BF16)
        OB4 = [aalloc(ph, "OB4%d" % t, [128, 1024], BF16) for t in range(2)]
        barrier()
        load_w(W, 8, 2048, wqkv_d[j], wqkv_d)
        load_w(WO, 8, 1024, wo_d[j], wo_d)
        load_ln(i, 0)
        MEMSET(KT[1][:], 0.0, [KT[1].buf])
        MEMSET(VB[1][:], 0.0, [VB[1].buf])
        for (u, T) in units:
            samp = (u == SU)
            cur = u % 2
            prv = 1 - cur
            make_xT(u, T)
            proj_fm(W, 0, 8, T, QT, 0.125)
            proj_fm(W, 1024, 4, T, KT[cur], 1.0)
            pb, pt = next_pj()
            for k in range(8):
                mm(pt[0:T, 0:512], XT[:, k, 0:T], W[:, k, 1536:2048], k == 0, k == 7, [XT, W.buf], [pb])
            ACOPY(KVTOK[0:T, :], pt[0:T, 0:512], [pb], [KVTOK.buf])
            if not samp:
                TCOPY(VB[cur][0:T, :], KVTOK[0:T, 256:512], [KVTOK.buf], [VB[cur].buf])
                mask = maskF if u == NH else maskA
                for kvh in range(4):
                    attn_group(T, j, kvh, QT, 0,
                               lambda ps, kvh=kvh, prv=prv: KT[prv][ps, kvh, :], KT[prv].buf,
                               lambda ps, kvh=kvh, cur=cur: KT[cur][ps, kvh, :], KT[cur].buf,
                               VB[prv][:, kvh * 64:(kvh + 1) * 64], VB[prv].buf,
                               VB[cur][:, kvh * 64:(kvh + 1) * 64], VB[cur].buf,
                               mask, SS, PBt, PTS, sm,
                               OB[0:T, kvh * 256:(kvh + 1) * 256].rearrange("p (g d) -> p g d", g=4), OB)
                if u == NPB - 1:
                    st(wkp_d[j], KVTOK[:, 0:256], [KVTOK.buf], wkp_d)
                    st(wvp_d[j], KVTOK[:, 256:512], [KVTOK.buf], wvp_d)
            else:
                st(wks_d[j][:, 0:124, :], cwk_d[j][:, 4:128, :], [cwk_d], wks_d)
                st(wvs_d[j][:, 0:124, :], cwv_d[j][:, 4:128, :], [cwv_d], wvs_d)
                for b in range(NSB):
                    st(wks_d[j, b, 124:128, :], KVTOK[b * 4:(b + 1) * 4, 0:256], [KVTOK.buf], wks_d)
                    st(wvs_d[j, b, 124:128, :], KVTOK[b * 4:(b + 1) * 4, 256:512], [KVTOK.buf], wvs_d)
                MEMSET(KTO[:], 0.0, [KTO.buf])
                MEMSET(VOWN[:], 0.0, [VOWN.buf])
                for b in range(NSB):
                    t2 = b % 2
                    ld(KTC[t2][:], cwkT_d[j, :, b], [cwkT_d], [KTC[t2].buf], eng="pool")
                    ld(VC[t2][:], cwv_d[j, b], [cwv_d], [VC[t2].buf], eng="pool")
                    TCOPY(KTO[:, :, 0:4], KT[cur][:, :, b * 4:(b + 1) * 4], [KT[cur].buf], [KTO.buf], eng="pool")
                    pb, pt = next_pj()
                    for k in range(8):
                        mm(pt[0:4, 0:256], XT[:, k, b * 4:(b + 1) * 4], W[:, k, 1792:2048], k == 0, k == 7,
                           [XT, W.buf], [pb])
                    ACOPY(VOWN[0:4, :], pt[0:4, 0:256], [pb], [VOWN.buf])
                    for kvh in range(4):
                        attn_group(4, j, kvh, QT, b * 4,
                                   lambda ps, kvh=kvh, t2=t2: KTC[t2][ps, kvh, :], KTC[t2].buf,
                                   lambda ps, kvh=kvh: KTO[ps, kvh, :], KTO.buf,
                                   VC[t2][:, kvh * 64:(kvh + 1) * 64], VC[t2].buf,
                                   VOWN[:, kvh * 64:(kvh + 1) * 64], VOWN.buf,
                                   maskA, SS, PBt, PTS, sm,
                                   OB4[t2][0:4, kvh * 256:(kvh + 1) * 256].rearrange("p (g d) -> p g d", g=4),
                                   OB4[t2].buf)
                    ld(OB[b * 4:(b + 1) * 4, :], OB4[t2][0:4, :], [OB4[t2].buf], [OB])
            transpose8(OB, OT, T)
            proj_F(WO, T, OT, OT)
            resid_ln(u, T, Fp[0:T, :], [Fp])

    def conv_phase(i):
        j = i // 2
        ph = "conv"
        arena_off[0] = 0
        W = aalloc(ph, "W", [128, 8, 3072], BF16)
        WO = aalloc(ph, "WO", [128, 8, 1024], BF16)
        UP = aalloc(ph, "UP", [128, 8, 1, 130], F32)
        UPS = aalloc(ph, "UPS", [128, 8, NSB, 6], F32)
        STT_ = aalloc(ph, "STT", [128, 8, NSB, 2], F32)
        HS = aalloc(ph, "HS", [128, 128], F32)
        Z = aalloc(ph, "Z", [128, 128], F32)
        GZ = aalloc(ph, "GZ", [128, 8, 128], BF16)
        UC = aalloc(ph, "UC", [128, 8, 32], F32)
        USB = aalloc(ph, "USB", [128, 1024], F32)
        barrier()
        load_w(W, 8, 3072, win_d[j], win_d)
        load_w(WO, 8, 1024, wout_d[j], wout_d)
        load_ln(i, 0)
        MEMSET(UP[:, :, :, 0:2], 0.0, [UP.buf])
        ld(STT_[:].rearrange("p a b c -> p (a b c)"), stT_d[j], [stT_d], [STT_.buf])
        TCOPY(UPS[:, :, :, 0:2], STT_[:], [STT_.buf], [UPS.buf], eng="pool")
        for (u, T) in units:
            samp = (u == SU)
            up = UPS if samp else UP
            nb, L = (NSB, 4) if samp else (1, 128)
            if u == NH:
                TSC(UP[:, :, :, 0:2], UP[:, :, :, 0:2], cflag[:, 0:1], None, ALU.mult, None, [UP.buf, cflag], [UP.buf])
            make_xT(u, T)

            def v3(ap, nb=nb):
                return ap.rearrange("p (b l) -> p b l", b=nb)
            for c in range(8):
                pb, pt = next_pj()
                pv = pt.rearrange("p (c t) -> p c t", c=4)
                for part in range(3):
                    for k in range(8):
                        mm(pv[:, part, 0:T], W[:, k, part * 1024 + c * 128: part * 1024 + (c + 1) * 128], XT[:, k, 0:T],
                           k == 0, k == 7, [W.buf, XT], [pb])
                ACOPY(HS[:, 0:T], pv[:, 2, 0:T], [pb], [HS.buf])
                TT(up[:, c, :, 2:2 + L], v3(pv[:, 1, 0:T]), v3(HS[:, 0:T]), ALU.mult, [pb, HS.buf], [up.buf])
                o = (j * 8 + c) * 3
                TSC(v3(Z[:, 0:T]), up[:, c, :, 0:L], CW[:, o:o + 1], None, ALU.mult, None, [up.buf, CW], [Z.buf])
                STT(v3(Z[:, 0:T]), up[:, c, :, 1:1 + L], CW[:, o + 1:o + 2], v3(Z[:, 0:T]), ALU.mult, ALU.add,
                    [up.buf, CW, Z.buf], [Z.buf])
                STT(v3(Z[:, 0:T]), up[:, c, :, 2:2 + L], CW[:, o + 2:o + 3], v3(Z[:, 0:T]), ALU.mult, ALU.add,
                    [up.buf, CW, Z.buf], [Z.buf])
                TT(GZ[:, c, 0:T], pv[:, 0, 0:T], Z[:, 0:T], ALU.mult, [pb, Z.buf], [GZ.buf])
            proj_F(WO, T, GZ, GZ.buf)
            resid_ln(u, T, Fp[0:T, :], [Fp])
            if samp:
                state_out(UC[:, :, 0:NSB * 2].rearrange("p a (b r) -> p a b r", r=2), UPS[:, :, :, 4:6], [UPS.buf],
                          NSB * 2, UC, USB, cvs_d[j], cvs_d)
            else:
                if u == NPB - 1:
                    state_out(UC[:, :, 0:2], UP[:, :, 0, 128:130], [UP.buf], 2, UC, USB, cvp_d[j], cvp_d)
                TCOPY(UP[:, :, :, 0:2], UP[:, :, :, 128:130], [UP.buf], [UP.buf], eng="pool")

    def mem_unit(T, QT, qc0, MKt, MVt, SS, PBt, PTS, sm, Odst, Odst_buf):
        S3 = Sp.t[0:T, 0:1024].rearrange("p (g s) -> p g s", g=4)
        for h in range(4):
            sb = Sb[h // 2]
            mm(S3[:, h, :], QT[:, 2 * h, qc0:qc0 + T], MKt[:, 2 * h, :], True, False, [QT.buf, MKt.buf], [sb])
            mm(S3[:, h, :], QT[:, 2 * h + 1, qc0:qc0 + T], MKt[:, 2 * h + 1, :], False, True, [QT.buf, MKt.buf], [sb])
        mx, den = sm[0:T, 0:4], sm[0:T, 4:8]
        TRED(mx, S3, ALU.max, [Sb[0], Sb[1]], [sm.buf])
        TT(SS[0:T, :, :], S3, mx.unsqueeze(2).to_broadcast([T, 4, 256]), ALU.subtract, [Sb[0], Sb[1], sm.buf], [SS.buf])
        ACTF(PBt[0:T, :, :], SS[0:T, :, :], AF.Exp, [SS.buf], [PBt.buf])
        TRED(den, PBt[0:T, :, :], ALU.add, [PBt.buf], [sm.buf])
        RECIP(den, den, [sm.buf], [sm.buf])
        for h in range(4):
            for mc in range(2):
                tr(A_bf[:, h * 2 + mc, 0:T], PBt[0:T, h, mc * 128:(mc + 1) * 128], identb[0:T, 0:T], [PBt.buf, identb], [Ap])
        ACOPY(PTS[:, :, 0:T], A_bf[:, :, 0:T], [Ap], [PTS.buf])
        for h in range(4):
            for mc in range(2):
                mm(Fp[0:T, h * 256:(h + 1) * 256], PTS[:, h * 2 + mc, 0:T], MVt[:, mc, h * 256:(h + 1) * 256],
                   mc == 0, mc == 1, [PTS.buf, MVt.buf], [Fp])
        TT(Odst, Fp[0:T, :].rearrange("p (h d) -> p h d", h=4), den.unsqueeze(2).to_broadcast([T, 4, 256]), ALU.mult,
           [Fp, sm.buf], [Odst_buf])

    def mem_phase(i):
        ph = "mem"
        arena_off[0] = 0
        MK = aalloc(ph, "MK", [128, 8, 256], BF16)
        MVt = aalloc(ph, "MV", [128, 2, 1024], BF16)
        mark = arena_off[0]
        WKV = aalloc(ph, "WKV", [128, 8, 2048], BF16)
        MEMT = aalloc(ph, "MEMT", [128, 8, 256], BF16)
        Y2 = aalloc(ph, "Y2", [128, 1024], F32)
        barrier()
        load_w(WKV, 8, 2048, mwkv_d[i], mwkv_d)
        ld(MEMT[:].rearrange("p a b -> p (a b)"), memT_d[:], [memT_d], [MEMT.buf], eng="pool")
        for mc in range(2):
            for kv in range(2):
                for n in range(2):
                    for k in range(8):
                        mm(Fp[:, n * 512:(n + 1) * 512], MEMT[:, k, mc * 128:(mc + 1) * 128],
                           WKV[:, k, kv * 1024 + n * 512: kv * 1024 + (n + 1) * 512], k == 0, k == 7,
                           [MEMT.buf, WKV.buf], [Fp])
                if kv == 0:
                    ACOPY(Y[:, :], Fp[:, :], [Fp], [Y])
                    st(mkp_d[i, mc * 128:(mc + 1) * 128, :], Y[:, :], [Y], mkp_d)
                else:
                    ACOPY(Y2[:, :], Fp[:, :], [Fp], [Y2.buf])
                    st(mvp_d[i, mc * 128:(mc + 1) * 128, :], Y2[:, :], [Y2.buf], mvp_d)
                    TCOPY(MVt[:, mc, :], Fp[:, :], [Fp], [MVt.buf])
        for c0 in range(0, 8, 2):
            pb, pt = next_pj()
            pv = pt.rearrange("p (c t) -> p c t", c=2)
            for c in range(2):
                for k in range(8):
                    mm(pv[:, c, :], WKV[:, k, (c0 + c) * 128:(c0 + c + 1) * 128], MEMT[:, k, :], k == 0, k == 7,
                       [WKV.buf, MEMT.buf], [pb])
            ACOPY(MK[:, c0:c0 + 2, :], pv[:, :, :], [pb], [MK.buf])
        arena_off[0] = mark
        WQ = aalloc(ph, "WQ", [128, 8, 1024], BF16)
        WO = aalloc(ph, "WO", [128, 8, 1024], BF16)
        QT = aalloc(ph, "QT", [128, 8, 128], BF16)
        SS = aalloc(ph, "SS", [128, 4, 256], F32)
        PBt = aalloc(ph, "PB", [128, 4, 256], BF16)
        PTS = aalloc(ph, "PTS", [128, 8, 128], BF16)
        sm = aalloc(ph, "sm", [128, 16], F32)
        MKC = [aalloc(ph, "MKC%d" % t, [128, 8, 256], BF16) for t in range(2)]
        MVC = [aalloc(ph, "MVC%d" % t, [128, 2, 1024], BF16) for t in range(2)]
        OB4 = [aalloc(ph, "OB4%d" % t, [128, 1024], BF16) for t in range(2)]
        barrier()
        load_w(WQ, 8, 1024, mwq_d[i], mwq_d)
        load_w(WO, 8, 1024, mwo_d[i], mwo_d)
        load_ln(i, 1)
        for (u, T) in units:
            samp = (u == SU)
            make_xT(u, T)
            proj_fm(WQ, 0, 8, T, QT, 1.0 / 16.0)
            if not samp:
                mem_unit(T, QT, 0, MK, MVt, SS, PBt, PTS, sm, OB[0:T, :].rearrange("p (h d) -> p h d", h=4), OB)
            else:
                for b in range(NSB):
                    t2 = b % 2
                    ld(MKC[t2][:].rearrange("p a b -> p (a b)"), cmkT_d[i, b], [cmkT_d], [MKC[t2].buf], eng="pool")
                    ld(MVC[t2][:], cmv_d[i, b].rearrange("(mc p) c -> p mc c", p=128), [cmv_d], [MVC[t2].buf], eng="pool")
                    mem_unit(4, QT, b * 4, MKC[t2], MVC[t2], SS, PBt, PTS, sm,
                             OB4[t2][0:4, :].rearrange("p (h d) -> p h d", h=4), OB4[t2].buf)
                    ld(OB[b * 4:(b + 1) * 4, :], OB4[t2][0:4, :], [OB4[t2].buf], [OB])
            transpose8(OB, OT, T)
            proj_F(WO, T, OT, OT)
            resid_ln(u, T, Fp[0:T, :], [Fp])

    def peer_phase(i, last):
        ph = "peer"
        arena_off[0] = 0
        W = aalloc(ph, "W", [128, 8, 2048], BF16)
        SK = aalloc(ph, "SK", [128, 2, 128], BF16)
        QT = aalloc(ph, "QT", [128, 16, 128], BF16)
        SV = aalloc(ph, "SV", [128, 16, 16], F32)
        SI = aalloc(ph, "SI", [128, 16, 16], U32)
        SIF = aalloc(ph, "SIF", [128, 16, 16], F32)
        SW = aalloc(ph, "SW", [128, 256], F32)
        COMB = aalloc(ph, "COMB", [128, 8, 256], F32)
        EQ = aalloc(ph, "EQ", [128, 8, 256], F32)
        CS = aalloc(ph, "CS", [128, 8, 16], F32)
        CI = aalloc(ph, "CI", [128, 8, 16], U32)
        CA = aalloc(ph, "CA", [128, 8, 16], U32)
        CAF = aalloc(ph, "CAF", [128, 8, 16], F32)
        GE = aalloc(ph, "GE", [128, 8, 16], F32)
        I1 = aalloc(ph, "I1", [128, 8, 16], F32)
        I2 = aalloc(ph, "I2", [128, 8, 16], F32)
        EIDI = aalloc(ph, "EIDI", [128, 128], I32)
        AA = aalloc(ph, "AA", [128, 128], F32)
        WW = aalloc(ph, "WW", [128, 128], F32)
        sm = aalloc(ph, "sm", [128, 16], F32)
        ACC = aalloc(ph, "ACC", [128, 1024], F32)
        JUNK = aalloc(ph, "JUNK", [128, 1024], BF16)
        NG = 8
        RING = [aalloc(ph, "RING%d" % t, [128, 1024], F32) for t in range(NG)]
        barrier()
        load_w(W, 8, 2048, pwq_d[i], pwq_d)
        for c in range(2):
            ld(SK[:, c, :], skT_d[i, c], [skT_d], [SK.buf], eng="pool")
        load_ln(i, 2)
        MEMSET(EIDI[:], 0, [EIDI.buf])
        gi = [0]
        for (u, T) in units:
            x = X[u]
            make_xT(u, T)
            proj_fm(W, 0, 16, T, QT, 1.0)
            S3 = Sp.t[0:T, :].rearrange("p (c n) -> p c n", c=16)
            for c in range(16):
                mm(S3[:, c, :], QT[:, c, 0:T], SK[:, c % 2, :], True, True, [QT.buf, SK.buf], [Sb[c // 4]])
            for c in range(16):
                sb = Sb[c // 4]
                src = S3[:, c, :]
                MAX8(SV[0:T, c, 0:8], src, [sb], [SV.buf])
                MAXIDX(SI[0:T, c, 0:8], SV[0:T, c, 0:8], src, [sb, SV.buf], [SI.buf])
                MATCHREP(SW[0:T, 0:128], SV[0:T, c, 0:8], src, [sb, SV.buf], [SW.buf])
                MAX8(SV[0:T, c, 8:16], SW[0:T, 0:128], [SW.buf], [SV.buf])
                MAXIDX(SI[0:T, c, 8:16], SV[0:T, c, 8:16], SW[0:T, 0:128], [SW.buf, SV.buf], [SI.buf])
            SV4 = SV[0:T].rearrange("p (h c) k -> p h c k", c=2)
            SIF4 = SIF[0:T].rearrange("p (h c) k -> p h c k", c=2)
            C4 = COMB[0:T].rearrange("p h (a b) -> p h a b", a=16)
            E4 = EQ[0:T].rearrange("p h (a b) -> p h a b", a=16)
            TCOPY(SIF[0:T], SI[0:T], [SI.buf], [SIF.buf])
            TT(C4, SV4[:, :, 0, :].unsqueeze(3).to_broadcast([T, 8, 16, 16]),
               SV4[:, :, 1, :].unsqueeze(2).to_broadcast([T, 8, 16, 16]), ALU.add, [SV.buf], [COMB.buf])
            for h in range(8):
                src = COMB[0:T, h, :]
                MAX8(CS[0:T, h, 0:8], src, [COMB.buf], [CS.buf])
                MAXIDX(CI[0:T, h, 0:8], CS[0:T, h, 0:8], src, [COMB.buf, CS.buf], [CI.buf])
                MATCHREP(SW[0:T, :], CS[0:T, h, 0:8], src, [COMB.buf, CS.buf], [SW.buf])
                MAX8(CS[0:T, h, 8:16], SW[0:T, :], [SW.buf], [CS.buf])
                MAXIDX(CI[0:T, h, 8:16], CS[0:T, h, 8:16], SW[0:T, :], [SW.buf, CS.buf], [CI.buf])
            TT(GE[0:T], CS[0:T], CS[0:T, :, 0:1].to_broadcast([T, 8, 16]), ALU.subtract, [CS.buf], [GE.buf])
            ACTF(GE[0:T], GE[0:T], AF.Exp, [GE.buf], [GE.buf])
            TRED(sm[0:T, 0:8], GE[0:T], ALU.add, [GE.buf], [sm.buf])
            RECIP(sm[0:T, 0:8], sm[0:T, 0:8], [sm.buf], [sm.buf])
            TT(GE[0:T], GE[0:T], sm[0:T, 0:8].unsqueeze(2).to_broadcast([T, 8, 16]), ALU.mult, [GE.buf, sm.buf], [GE.buf])
            for which, Idst in ((0, I1), (1, I2)):
                if which == 0:
                    TSS(CA[0:T], CI[0:T], 4, ALU.logical_shift_right, [CI.buf], [CA.buf])
                else:
                    TSS(CA[0:T], CI[0:T], 15, ALU.bitwise_and, [CI.buf], [CA.buf])
                TCOPY(CAF[0:T], CA[0:T], [CA.buf], [CAF.buf])
                TT(E4, iota16[0:T, :].unsqueeze(1).unsqueeze(1).to_broadcast([T, 8, 16, 16]),
                   CAF[0:T].unsqueeze(3).to_broadcast([T, 8, 16, 16]), ALU.is_equal, [iota16, CAF.buf], [EQ.buf])
                TT(E4, E4, SIF4[:, :, which, :].unsqueeze(2).to_broadcast([T, 8, 16, 16]), ALU.mult,
                   [EQ.buf, SIF.buf], [EQ.buf])
                TRED(Idst[0:T], E4, ALU.add, [EQ.buf], [Idst.buf])
            STT(I1[0:T], I1[0:T], 128.0, I2[0:T], ALU.mult, ALU.add, [I1.buf, I2.buf], [I1.buf])
            TSC(I1[0:T], I1[0:T], float(i * 16384), None, ALU.add, None, [I1.buf], [I1.buf])
            TCOPY(EIDI[0:T, :], I1[0:T].rearrange("p h k -> p (h k)"), [I1.buf], [EIDI.buf])
            for jj in range(128):
                slot = RING[gi[0] % NG]
                gi[0] += 1
                GATHER(slot[:, :], pu_d[:, :], EIDI[:, jj:jj + 1], [EIDI.buf, pu_d], [slot.buf])
                TTR(JUNK[0:T, :], x[0:T, :], slot[0:T, :], AA[0:T, jj:jj + 1], [x, slot.buf], [JUNK.buf, AA.buf])
            ACTF(WW[0:T, :], AA[0:T, :], AF.Gelu, [AA.buf], [WW.buf])
            TT(WW[0:T, :], WW[0:T, :], GE[0:T].rearrange("p h k -> p (h k)"), ALU.mult, [WW.buf, GE.buf], [WW.buf])
            for jj in range(128):
                slot = RING[gi[0] % NG]
                gi[0] += 1
                GATHER(slot[:, :], pv_d[:, :], EIDI[:, jj:jj + 1], [EIDI.buf, pv_d], [slot.buf])
                if jj == 0:
                    TSC(ACC[0:T, :], slot[0:T, :], WW[0:T, 0:1], None, ALU.mult, None, [slot.buf, WW.buf], [ACC.buf])
                else:
                    STT(ACC[0:T, :], slot[0:T, :], WW[0:T, jj:jj + 1], ACC[0:T, :], ALU.mult, ALU.add,
                        [slot.buf, WW.buf, ACC.buf], [ACC.buf])
            resid_ln(u, T, ACC[0:T, :], [ACC.buf])
            if last:
                if u == SU:
                    st(ys_d[:, :], x[0:TS, :], [x], ys_d)
                elif u >= NH:
                    st(yp_d[(u - NH) * 128:(u - NH + 1) * 128, :], x[:, :], [x], yp_d)

    import os
    PH = os.environ.get("KPHASES", "acmp")
    for i in range(D):
        if i % 2 == 0:
            if "a" in PH:
                attn_phase(i)
        else:
            if "c" in PH:
                conv_phase(i)
        if "m" in PH:
            mem_phase(i)
        if "p" in PH:
            peer_phase(i, i == D - 1)
    P.emit()
    return nc, stack, P


def prep_inputs(cfg, inp):
    D, B, SEQ, DB, n = cfg.D, cfg.B, cfg.SEQ, cfg.DB, cfg.n
    NA, NC, NSB, NH, NOWN, cps = cfg.NA, cfg.NC, cfg.NSB, cfg.NH, cfg.NOWN, cfg.cps
    f = lambda a: np.ascontiguousarray(np.asarray(a, dtype=np.float32))
    g = {k: np.asarray(v) for k, v in inp.items()}
    wqkv_src = g["attn_w_qkv"]
    q = wqkv_src[:, :, 0:1024]
    kk = wqkv_src[:, :, 1024:1280]
    vv = wqkv_src[:, :, 1280:1536]
    kdup = np.concatenate([np.concatenate([kk[:, :, h * 64:(h + 1) * 64]] * 2, axis=2) for h in range(4)], axis=2)
    wqkv = f(np.concatenate([q, kdup, kk, vv], axis=2))
    ii = np.arange(128)
    maskA = np.full((128, 256), NEG, np.float32)
    maskA[:, 0:128][ii[None, :] >= ii[:, None]] = 0.0
    maskA[:, 128:256][ii[None, :] <= ii[:, None]] = 0.0
    maskN = maskA.copy()
    maskN[:, 0:128] = NEG
    ncv = max(NC, 1)
    if NC > 0:
        cw = g["conv_w"].reshape(NC, 3, 8, 128)
        cw = f(np.transpose(cw, (3, 0, 2, 1)).reshape(128, NC * 8 * 3))
        win = f(g["conv_w_in"])
        wout = f(g["conv_w_out"])
    else:
        cw = np.zeros((128, 24), np.float32)
        win = np.zeros((1, 1024, 3072), np.float32)
        wout = np.zeros((1, 1024, 1024), np.float32)
    skT = f(np.transpose(g["peer_sub_keys"], (0, 1, 3, 2)))
    shared = {
        "maskA": maskA,
        "wqkv": wqkv, "wo": f(g["attn_w_o"]), "sinks": f(g["attn_sinks"]).reshape(1, NA * 16),
        "win": win, "cw": cw, "wout": wout,
        "mwq": f(g["mem_w_q"]), "mwkv": f(g["mem_w_kv"]), "mwo": f(g["mem_w_o"]),
        "pwq": f(g["peer_w_q"]), "skT": skT,
        "pu": f(g["peer_u"]).reshape(D * 16384, 1024), "pv": f(g["peer_v"]).reshape(D * 16384, 1024),
        "lng": f(g["ln_g"]), "lnb": f(g["ln_b"]),
    }
    maps = []
    for c in range(n):
        b, qd = c // cps, c % cps
        t0 = qd * NOWN * 128
        xp = np.zeros(((NOWN + NH) * 128, 1024), np.float32)
        lo = t0 - NH * 128
        lo_c = max(lo, 0)
        xp[lo_c - lo:] = g["x_prompt"][b, lo_c:t0 + NOWN * 128]
        first = (qd == 0)
        sb = slice(c * NSB, (c + 1) * NSB)
        cwk = g["cache_win_k"][:, sb].reshape(NA, NSB, 128, 256)
        cwv = g["cache_win_v"][:, sb].reshape(NA, NSB, 128, 256)
        kT = np.transpose(g["cache_win_k"][:, sb], (0, 4, 1, 3, 2))
        cwkT = np.concatenate([kT, kT], axis=1)
        if NC > 0:
            stt = g["state_conv"][:, sb].reshape(NC, NSB, 2, 8, 128)
            stT = np.transpose(stt, (0, 4, 3, 1, 2)).reshape(NC, 128, 8 * NSB * 2)
        else:
            stT = np.zeros((1, 128, 8 * NSB * 2), np.float32)
        memT = np.transpose(g["mem_prompt"][b].reshape(256, 8, 128), (2, 1, 0)).reshape(128, 8 * 256)
        cmk = g["cache_mem_k"][:, sb].reshape(D, NSB, 256, 4, 2, 128)
        cmkT = np.transpose(cmk, (0, 1, 5, 3, 4, 2)).reshape(D, NSB, 128, 8 * 256)
        cmv = g["cache_mem_v"][:, sb].reshape(D, NSB, 256, 1024)
        m = dict(shared)
        m.update({
            "xp": f(xp), "xs": f(g["x_sample"][sb].reshape(NSB * 4, 1024)),
            "maskF": maskN if first else maskA,
            "cflag": np.full((128, 1), 0.0 if first else 1.0, np.float32),
            "cwkT": f(cwkT), "cwk": f(cwk), "cwv": f(cwv), "stT": f(stT),
            "memT": f(memT), "cmkT": f(cmkT), "cmv": f(cmv),
        })
        maps.append(m)
    return maps


def assemble(cfg, res):
    D, B, SEQ, DB, n = cfg.D, cfg.B, cfg.SEQ, cfg.DB, cfg.n
    NA, NC, NSB, NOWN, cps = cfg.NA, cfg.NC, cfg.NSB, cfg.NOWN, cfg.cps
    y_p = np.zeros((B, SEQ, 1024), np.float32)
    y_s = np.zeros((DB, 4, 1024), np.float32)
    wkp = np.zeros((NA, B, 128, 4, 64), np.float32)
    wvp = np.zeros((NA, B, 128, 4, 64), np.float32)
    cvp = np.zeros((NC, B, 2, 1024), np.float32)
    mkp = np.zeros((D, B, 256, 4, 256), np.float32)
    mvp = np.zeros((D, B, 256, 4, 256), np.float32)
    wks = np.zeros((NA, DB, 128, 4, 64), np.float32)
    wvs = np.zeros((NA, DB, 128, 4, 64), np.float32)
    cvs = np.zeros((NC, DB, 2, 1024), np.float32)
    for c in range(n):
        r = res[c]
        b, qd = c // cps, c % cps
        t0 = qd * NOWN * 128
        y_p[b, t0:t0 + NOWN * 128] = r["yp"]
        sb = slice(c * NSB, (c + 1) * NSB)
        y_s[sb] = r["ys"].reshape(NSB, 4, 1024)
        wks[:, sb] = r["wks"].reshape(NA, NSB, 128, 4, 64)
        wvs[:, sb] = r["wvs"].reshape(NA, NSB, 128, 4, 64)
        if NC > 0:
            cvs[:, sb] = r["cvs"].reshape(NC, NSB, 2, 1024)
        if qd == cps - 1:
            wkp[:, b] = r["wkp"].reshape(NA, 128, 4, 64)
            wvp[:, b] = r["wvp"].reshape(NA, 128, 4, 64)
            if NC > 0:
                cvp[:, b] = r["cvp"][:NC]
        if qd == 0:
            mkp[:, b] = r["mkp"].reshape(D, 256, 4, 256)
            mvp[:, b] = r["mvp"].reshape(D, 256, 4, 256)
    return (y_p, y_s, wkp, wvp, cvp, mkp, mvp, wks, wvs, cvs)


def run_cfg(cfg, inputs, trace=False):
    nc, stack, P = build(cfg)
    maps = prep_inputs(cfg, inputs)
    res = run_bass_kernel_spmd(nc, maps, core_ids=list(range(cfg.n)))
    return assemble(cfg, res.results)


def kernel(**inputs):
    cfg = Cfg()
    return run_cfg(cfg, inputs)
```

```python
from contextlib import ExitStack
import numpy as np
import concourse.bass as bass
import concourse.mybir as mybir
from concourse.alu_op_type import AluOpType as ALU
from concourse.bass_utils import run_bass_kernel_spmd

F32 = mybir.dt.float32
BF16 = mybir.dt.bfloat16
I32 = mybir.dt.int32
U32 = mybir.dt.uint32
AF = mybir.ActivationFunctionType
AX = mybir.AxisListType

SAME_ENGINE_SYNC = True


class Buf:
    def __init__(self, prog, name, t, space):
        self.prog = prog
        self.name = name
        self.t = t
        self.space = space
        self.sem = None
        self.dma_cnt = 0
        self.last_w = None
        self.readers = {}

    def __getitem__(self, key):
        return self.t[key]


class Op:
    __slots__ = ("eng", "fn", "deps", "signal", "sig_val", "is_dma", "sem_buf", "dma_val", "idx")

    def __init__(self, eng, fn, is_dma=False):
        self.eng = eng
        self.fn = fn
        self.deps = []
        self.signal = False
        self.sig_val = 0
        self.is_dma = is_dma
        self.sem_buf = None
        self.dma_val = 0


class Prog:
    ENGS = ("pe", "act", "dve", "pool", "sp")

    def __init__(self, nc, stack):
        self.nc = nc
        self.stack = stack
        self.ops = {e: [] for e in self.ENGS}
        self.bufs = []
        self.nops = 0
        self.out_bufs = []
        import os
        self.limit = int(os.environ.get("KCUT", "1000000000"))

    def sbuf(self, name, shape, dtype):
        t = self.stack.enter_context(self.nc.sbuf_tensor(name, list(shape), dtype))
        b = Buf(self, name, t, "sb")
        self.bufs.append(b)
        return b

    def psum(self, name, shape, dtype):
        t = self.stack.enter_context(self.nc.psum_tensor(name, list(shape), dtype))
        b = Buf(self, name, t, "ps")
        self.bufs.append(b)
        return b

    def dram(self, name, shape, dtype, kind):
        t = self.nc.dram_tensor(name, list(shape), dtype, kind=kind).ap()
        b = Buf(self, name, t, "dr")
        self.bufs.append(b)
        if kind == "ExternalOutput":
            self.out_bufs.append(b)
        return b

    def alias(self, name, t, space="sb"):
        b = Buf(self, name, t, space)
        self.bufs.append(b)
        return b

    def _track(self, op, reads, writes):
        deps = op.deps
        ps_reads = [b for b in reads if b.space == "ps"]
        if ps_reads:
            reads = [b for b in reads if b.space != "ps"]
            writes = list(writes) + [b for b in ps_reads if b not in writes]
        for b in reads:
            if b.last_w is not None:
                deps.append(b.last_w)
        for b in writes:
            if b.last_w is not None:
                deps.append(b.last_w)
            for r in b.readers.values():
                deps.append(r)
        for b in reads:
            key = ("dma", id(op)) if op.is_dma else op.eng
            b.readers[key] = op
        for b in writes:
            b.last_w = op
            b.readers = {}

    def op(self, eng, fn, reads=(), writes=()):
        if self.nops >= self.limit:
            return None
        o = Op(eng, fn)
        self._track(o, reads, writes)
        self.ops[eng].append(o)
        self.nops += 1
        return o

    def dma(self, eng, fn, reads=(), writes=(), sem_buf=None):
        if self.nops >= self.limit:
            return None
        o = Op(eng, fn, is_dma=True)
        if sem_buf is None:
            sem_buf = writes[0]
        o.sem_buf = sem_buf
        sem_buf.dma_cnt += 1
        o.dma_val = 16 * sem_buf.dma_cnt
        self._track(o, reads, writes)
        self.ops[eng].append(o)
        self.nops += 1
        return o

    def emit(self):
        nc = self.nc
        stack = self.stack
        for e in self.ENGS:
            for o in self.ops[e]:
                for d in o.deps:
                    if d.is_dma:
                        continue
                    if d.eng == "pe" and o.eng == "pe" and not o.is_dma:
                        continue
                    if (not SAME_ENGINE_SYNC) and d.eng == o.eng and not o.is_dma:
                        continue
                    d.signal = True
        esem = {}
        for e in self.ENGS:
            if e == "sp":
                continue
            esem[e] = stack.enter_context(nc.semaphore("s_" + e))
            c = 0
            for o in self.ops[e]:
                if o.signal and not o.is_dma:
                    c += 1
                    o.sig_val = c
        for b in self.bufs:
            if b.dma_cnt > 0:
                b.sem = stack.enter_context(nc.semaphore("d_" + b.name))
        self.max_sig = {e: max([o.sig_val for o in self.ops[e]] + [0]) for e in self.ENGS}
        block = stack.enter_context(nc.Block())
        engobj = {"pe": block.tensor, "act": block.scalar, "dve": block.vector, "pool": block.gpsimd, "sp": block.sync}
        prog = self

        def make(e):
            def body(eng):
                waited = {}
                for o in prog.ops[e]:
                    for d in o.deps:
                        if d.is_dma:
                            sem, val = d.sem_buf.sem, d.dma_val
                        else:
                            if d.eng == "pe" and e == "pe" and not o.is_dma:
                                continue
                            if (not SAME_ENGINE_SYNC) and d.eng == e and not o.is_dma:
                                continue
                            sem, val = esem[d.eng], d.sig_val
                        k = id(sem)
                        if waited.get(k, 0) >= val:
                            continue
                        waited[k] = val
                        eng.wait_ge(sem, val)
                    ins = o.fn(eng)
                    if o.is_dma:
                        ins.then_inc(o.sem_buf.sem, 16)
                    elif o.signal:
                        ins.then_inc(esem[e], 1)
                if e == "sp":
                    for b in prog.out_bufs:
                        if b.dma_cnt > 0:
                            eng.wait_ge(b.sem, 16 * b.dma_cnt)
            return body

        for e in self.ENGS:
            engobj[e](make(e))


ALPHA = 8.0 ** 0.25
LN_EPS = 1e-5
NEG = -1e30


class Cfg:
    def __init__(s, D=4, B=2, SEQ=8192, DB=128, n_cores=8):
        s.D, s.B, s.SEQ, s.DB, s.n = D, B, SEQ, DB, n_cores
        s.cps = n_cores // B
        s.NOWN = SEQ // 128 // s.cps
        s.NH = 3
        s.NPB = s.NOWN + s.NH
        s.NSB = DB // n_cores
        s.TS = s.NSB * 4
        s.NA = (D + 1) // 2
        s.NC = D // 2
        s.halo_skip = [(0, 1), (1, 1), (1, 2), (2, 3)] if (D == 4 and s.NH == 3) else [(0, 0)] * D


def build(cfg):
    nc = bass.Bass("TRN2", target_bir_lowering=False)
    stack = ExitStack()
    P = Prog(nc, stack)
    D, NPB, NH, NSB, TS, NA, NC = cfg.D, cfg.NPB, cfg.NH, cfg.NSB, cfg.TS, cfg.NA, cfg.NC
    NOWN = cfg.NOWN
    SU = NPB

    def din(name, shape, dt=F32):
        return P.dram(name, shape, dt, "ExternalInput")

    def dout(name, shape, dt=F32):
        return P.dram(name, shape, dt, "ExternalOutput")

    xp_d = din("xp", [NPB * 128, 1024])
    xs_d = din("xs", [TS, 1024])
    maskA_d = din("maskA", [128, 256])
    maskF_d = din("maskF", [128, 256])
    cflag_d = din("cflag", [128, 1])
    wqkv_d = din("wqkv", [NA, 1024, 2048])
    wo_d = din("wo", [NA, 1024, 1024])
    sinks_d = din("sinks", [1, NA * 16])
    cwkT_d = din("cwkT", [NA, 128, NSB, 4, 128])
    cwk_d = din("cwk", [NA, NSB, 128, 256])
    cwv_d = din("cwv", [NA, NSB, 128, 256])
    win_d = din("win", [max(NC, 1), 1024, 3072])
    cw_d = din("cw", [128, max(NC, 1) * 8 * 3])
    wout_d = din("wout", [max(NC, 1), 1024, 1024])
    stT_d = din("stT", [max(NC, 1), 128, 8 * NSB * 2])
    mwq_d = din("mwq", [D, 1024, 1024])
    mwkv_d = din("mwkv", [D, 1024, 2048])
    mwo_d = din("mwo", [D, 1024, 1024])
    memT_d = din("memT", [128, 8 * 256])
    cmkT_d = din("cmkT", [D, NSB, 128, 8 * 256])
    cmv_d = din("cmv", [D, NSB, 256, 1024])
    pwq_d = din("pwq", [D, 1024, 2048])
    skT_d = din("skT", [D, 2, 128, 128])
    puv_d = din("puv", [D * 16384, 2048])
    puvb_d = [P.dram("puvb%d" % i, [16384, 2048], BF16, "Internal") for i in range(D)]
    lng_d = din("lng", [D, 3, 1024])
    lnb_d = din("lnb", [D, 3, 1024])

    yp_d = dout("yp", [NOWN * 128, 1024])
    ys_d = dout("ys", [TS, 1024])
    wkp_d = dout("wkp", [NA, 128, 256])
    wvp_d = dout("wvp", [NA, 128, 256])
    cvp_d = dout("cvp", [max(NC, 1), 2, 1024])
    mkp_d = dout("mkp", [D, 256, 1024])
    mvp_d = dout("mvp", [D, 256, 1024])
    wks_d = dout("wks", [NA, NSB, 128, 256])
    wvs_d = dout("wvs", [NA, NSB, 128, 256])
    cvs_d = dout("cvs", [max(NC, 1), NSB * 2, 1024])

    X = [P.sbuf("X%d" % u, [128, 1024], F32) for u in range(NPB + 1)]
    identf = P.sbuf("identf", [128, 128], F32)
    identb = P.sbuf("identb", [128, 128], BF16)
    maskA = P.sbuf("maskA_s", [128, 256], F32)
    maskF = P.sbuf("maskF_s", [128, 256], F32)
    cflag = P.sbuf("cflag_s", [128, 1], F32)
    sinkb = P.sbuf("sinkb", [128, NA * 16], F32)
    CW = P.sbuf("CW", [128, max(NC, 1) * 8 * 3], F32)
    iota16 = P.sbuf("iota16", [128, 16], F32)
    GB = P.sbuf("GB", [128, 1024], F32)
    BB = P.sbuf("BB", [128, 1024], F32)
    XB = P.sbuf("XB", [128, 1024], BF16)
    OB = P.sbuf("OB", [128, 1024], BF16)
    XT = P.sbuf("XT", [128, 8, 128], BF16)
    OT = P.sbuf("OT", [128, 8, 128], BF16)
    Y = P.sbuf("Y", [128, 1024], F32)
    ST = P.sbuf("ST", [128, 12], F32)
    MV2 = P.sbuf("MV2", [128, 2], F32)
    SD = P.sbuf("SD", [128, 1], F32)
    RSTD = P.sbuf("RSTD", [128, 1], F32)
    NMR = P.sbuf("NMR", [128, 1], F32)
    DUM = P.sbuf("DUM", [128, 4], F32)
    tmpi = P.sbuf("tmpi", [128, 128], I32)
    tmpf = P.sbuf("tmpf", [128, 128], F32)
    tmpr = P.sbuf("tmpr", [128, 1], F32)

    ARENA_BYTES = 100 * 1024
    ARENA = P.sbuf("ARENA", [128, ARENA_BYTES // 4], F32)
    arena_views = {F32: ARENA[:], BF16: ARENA[:].bitcast(BF16), I32: ARENA[:].bitcast(I32), U32: ARENA[:].bitcast(U32)}
    esz = {F32: 4, BF16: 2, I32: 4, U32: 4}
    arena_cache = {}
    arena_off = [0]

    class AV:
        def __init__(self, buf, ap):
            self.buf = buf
            self.ap = ap

        def __getitem__(self, key):
            return self.ap[key]

    def aalloc(phase, name, shape, dt):
        n = int(np.prod(shape[1:]))
        nb = (n * esz[dt] + 63) // 64 * 64
        off = arena_off[0]
        arena_off[0] += nb
        assert arena_off[0] <= ARENA_BYTES, (phase, name, arena_off[0])
        key = (phase, name)
        if key in arena_cache:
            assert arena_cache[key][1] == off
            return arena_cache[key][0]
        e0 = off // esz[dt]
        ap = arena_views[dt][:, e0:e0 + n]
        if len(shape) == 3:
            ap = ap.rearrange("p (a b) -> p a b", a=shape[1])
        elif len(shape) == 4:
            ap = ap.rearrange("p (a b c) -> p a b c", a=shape[1], b=shape[2])
        buf = P.alias(phase + "_" + name, ap, "sb")
        av = AV(buf, ap)
        arena_cache[key] = (av, off)
        return av

    def barrier():
        bufs = [v[0].buf for v in arena_cache.values()]
        P.op("dve", lambda e: e.memset(DUM[:, 0:1], 0.0), reads=[], writes=bufs + [DUM])

    Fp = P.psum("Fp", [128, 1024], F32)
    Sp = P.psum("Sp", [128, 2048], F32)
    Ap = P.psum("Ap", [128, 512], F32)
    Bp = P.psum("Bp", [128, 512], F32)
    Sb = [P.alias("Sb%d" % i, Sp.t[:, i * 512:(i + 1) * 512], "ps") for i in range(4)]
    A_bf = Ap[:].bitcast(BF16).rearrange("p (k t) -> p k t", k=8)
    pj_banks = [(Bp, Bp.t), (Sb[3], Sp.t[:, 1536:2048])]
    pj_i = [0]

    def next_pj():
        b = pj_banks[pj_i[0] % 2]
        pj_i[0] += 1
        return b

    def mm(out, lhsT, rhs, start, stop, reads, writes):
        P.op("pe", lambda e: e.matmul(out, lhsT=lhsT, rhs=rhs, start=start, stop=stop), reads=reads, writes=writes)

    def tr(out, in_, ident, reads, writes):
        P.op("pe", lambda e: e.transpose(out=out, in_=in_, identity=ident), reads=reads, writes=writes)

    def dve(f, reads, writes):
        P.op("dve", f, reads=reads, writes=writes)

    def act(f, reads, writes):
        P.op("act", f, reads=reads, writes=writes)

    def pool(f, reads, writes):
        P.op("pool", f, reads=reads, writes=writes)

    def ld(out, in_, reads, writes, eng="sp", **kw):
        P.dma(eng, lambda e: e.dma_start(out=out, in_=in_, **kw), reads=reads, writes=writes)

    def st(out, in_, reads, dbuf, eng="sp", **kw):
        P.dma(eng, lambda e: e.dma_start(out=out, in_=in_, **kw), reads=reads, writes=[dbuf], sem_buf=dbuf)


    def TT(out, in0, in1, op, reads, writes, eng="dve"):
        P.op(eng, lambda e: e.tensor_tensor(out=out, in0=in0, in1=in1, op=op), reads=reads, writes=writes)

    def TSC(out, in0, s1, s2, op0, op1, reads, writes, eng="dve"):
        if op1 is None:
            P.op(eng, lambda e: e.tensor_scalar(out=out, in0=in0, scalar1=s1, scalar2=None, op0=op0), reads=reads, writes=writes)
        else:
            P.op(eng, lambda e: e.tensor_scalar(out=out, in0=in0, scalar1=s1, scalar2=s2, op0=op0, op1=op1), reads=reads, writes=writes)

    def STT(out, in0, scalar, in1, op0, op1, reads, writes):
        P.op("dve", lambda e: e.scalar_tensor_tensor(out=out, in0=in0, scalar=scalar, in1=in1, op0=op0, op1=op1),
             reads=reads, writes=writes)

    def TCOPY(out, in_, reads, writes, eng="dve"):
        P.op(eng, lambda e: e.tensor_copy(out=out, in_=in_), reads=reads, writes=writes)

    def TRED(out, in_, op, reads, writes):
        P.op("dve", lambda e: e.tensor_reduce(out=out, in_=in_, axis=AX.X, op=op), reads=reads, writes=writes)

    def ACTF(out, in_, func, reads, writes):
        P.op("act", lambda e: e.activation(out=out, in_=in_, func=func), reads=reads, writes=writes)

    def ACOPY(out, in_, reads, writes):
        P.op("act", lambda e: e.copy(out=out, in_=in_), reads=reads, writes=writes)

    def AMUL(out, in_, mul, reads, writes):
        P.op("act", lambda e: e.mul(out=out, in_=in_, mul=mul), reads=reads, writes=writes)

    def RECIP(out, in_, reads, writes):
        P.op("dve", lambda e: e.reciprocal(out=out, in_=in_), reads=reads, writes=writes)

    def MEMSET(out, val, writes, eng="pool"):
        P.op(eng, lambda e: e.memset(out, val), reads=[], writes=writes)

    def MAX8(out, in_, reads, writes):
        P.op("dve", lambda e: e.max(out=out, in_=in_), reads=reads, writes=writes)

    def MAXIDX(out, in_max, in_values, reads, writes):
        P.op("dve", lambda e: e.max_index(out=out, in_max=in_max, in_values=in_values), reads=reads, writes=writes)

    def MATCHREP(out, in_to_replace, in_values, reads, writes):
        P.op("dve", lambda e: e.match_replace(out=out, in_to_replace=in_to_replace, in_values=in_values, imm_value=NEG),
             reads=reads, writes=writes)

    def TSS(out, in_, scalar, op, reads, writes):
        P.op("dve", lambda e: e.tensor_single_scalar(out=out, in_=in_, scalar=scalar, op=op), reads=reads, writes=writes)

    def GATHER(slot_ap, table_ap, idx_ap, reads, writes):
        P.dma("pool", lambda e: e.indirect_dma_start(out=slot_ap, out_offset=None, in_=table_ap,
                                                     in_offset=bass.IndirectOffsetOnAxis(ap=idx_ap, axis=0)),
              reads=reads, writes=writes)

    def TTR(out, in0, in1, accum_out, reads, writes):
        P.op("dve", lambda e: e.scalar_tensor_tensor(out=out, in0=in0, scalar=1.0, in1=in1, op0=ALU.mult, op1=ALU.mult,
                                                     accum_out=accum_out), reads=reads, writes=writes)

    def load_w(dst_av, nk, ncols, src_ap, src_buf):
        for k in range(nk):
            ld(dst_av[:, k, 0:ncols], src_ap[k * 128:(k + 1) * 128, :], [src_buf], [dst_av.buf], eng="pool")

    def precast(i):
        NP = 16
        R = 16384 // NP
        for q in range(NP):
            ld(puvb_d[i][q * R:(q + 1) * R, :], puv_d[i * 16384 + q * R: i * 16384 + (q + 1) * R, :],
               [puv_d], [puvb_d[i]], eng="pool")

    pool(lambda e: e.iota(tmpi[:], pattern=[[1, 128]], base=0, channel_multiplier=0), [], [tmpi])
    dve(lambda e: e.tensor_copy(out=tmpf[:], in_=tmpi[:]), [tmpi], [tmpf])
    dve(lambda e: e.tensor_copy(out=iota16[:], in_=tmpi[:, 0:16]), [tmpi], [iota16])
    pool(lambda e: e.iota(tmpi[:, 0:1], pattern=[[1, 1]], base=0, channel_multiplier=1), [tmpf, iota16], [tmpi])
    dve(lambda e: e.tensor_copy(out=tmpr[:], in_=tmpi[:, 0:1]), [tmpi], [tmpr])
    dve(lambda e: e.tensor_scalar(out=identf[:], in0=tmpf[:], scalar1=tmpr[:, 0:1], scalar2=None, op0=ALU.is_equal),
        [tmpf, tmpr], [identf])
    dve(lambda e: e.tensor_copy(out=identb[:], in_=identf[:]), [identf], [identb])
    ld(maskA[:], maskA_d[:], [maskA_d], [maskA])
    ld(maskF[:], maskF_d[:], [maskF_d], [maskF])
    ld(cflag[:], cflag_d[:], [cflag_d], [cflag])
    ld(sinkb[:], sinks_d[0:1, :].partition_broadcast(128), [sinks_d], [sinkb])
    ld(CW[:], cw_d[:], [cw_d], [CW])
    for u in range(NPB):
        ld(X[u][:], xp_d[u * 128:(u + 1) * 128, :], [xp_d], [X[u]])
    ld(X[SU][0:TS, :], xs_d[:], [xs_d], [X[SU]])

    all_units = [(u, 128) for u in range(NPB)] + [(SU, TS)]

    def units_from(skip):
        return [(u, T) for (u, T) in all_units if u >= skip or u == SU]

    def transpose8(src, dst, T):
        for k in range(8):
            tr(A_bf[:, k, 0:T], src[0:T, k * 128:(k + 1) * 128], identb[0:T, 0:T], [src, identb], [Ap])
        TCOPY(dst[:, :, 0:T], A_bf[:, :, 0:T], [Ap], [dst])

    def make_xT(u, T):
        ACOPY(XB[0:T, :], X[u][0:T, :], [X[u]], [XB])
        transpose8(XB, XT, T)

    def proj_fm(W, col0, nch, T, dst, scale, src=XT):
        for c0 in range(0, nch, 4):
            n = min(4, nch - c0)
            pb, pt = next_pj()
            pv = pt.rearrange("p (c t) -> p c t", c=4)
            for c in range(n):
                for k in range(8):
                    mm(pv[:, c, 0:T], W[:, k, col0 + (c0 + c) * 128: col0 + (c0 + c + 1) * 128], src[:, k, 0:T],
                       k == 0, k == 7, [W.buf, src], [pb])
            AMUL(dst[:, c0:c0 + n, 0:T], pv[:, 0:n, 0:T], scale, [pb], [dst.buf])

    def proj_F(W, T, srcT, srcbuf):
        for n in range(2):
            for k in range(8):
                mm(Fp[0:T, n * 512:(n + 1) * 512], srcT[:, k, 0:T], W[:, k, n * 512:(n + 1) * 512], k == 0, k == 7,
                   [srcbuf, W.buf], [Fp])

    def load_ln(i, j):
        ld(GB[:], lng_d[i, j:j + 1, :].partition_broadcast(128), [lng_d], [GB])
        ld(BB[:], lnb_d[i, j:j + 1, :].partition_broadcast(128), [lnb_d], [BB])

    def resid_ln(u, T, f_ap, f_bufs):
        x = X[u]
        STT(Y[0:T, :], x[0:T, :], ALPHA, f_ap, ALU.mult, ALU.add, [x] + f_bufs, [Y])
        P.op("dve", lambda e: e.bn_stats(out=ST[0:T, 0:6], in_=Y[0:T, 0:512]), reads=[Y], writes=[ST])
        P.op("dve", lambda e: e.bn_stats(out=ST[0:T, 6:12], in_=Y[0:T, 512:1024]), reads=[Y], writes=[ST])
        P.op("dve", lambda e: e.bn_aggr(out=MV2[0:T, :], in_=ST[0:T, :]), reads=[ST], writes=[MV2])
        TSC(SD[0:T, :], MV2[0:T, 1:2], LN_EPS, None, ALU.add, None, [MV2], [SD])
        P.op("act", lambda e: e.sqrt(out=SD[0:T, :], in_=SD[0:T, :]), reads=[SD], writes=[SD])
        RECIP(RSTD[0:T, :], SD[0:T, :], [SD], [RSTD])
        STT(NMR[0:T, :], MV2[0:T, 0:1], -1.0, RSTD[0:T, :], ALU.mult, ALU.mult, [MV2, RSTD], [NMR])
        TSC(x[0:T, :], Y[0:T, :], RSTD[0:T, 0:1], NMR[0:T, 0:1], ALU.mult, ALU.add, [Y, RSTD, NMR], [x])
        TT(x[0:T, :], x[0:T, :], GB[0:T, :], ALU.mult, [x, GB], [x], eng="pool")
        TT(x[0:T, :], x[0:T, :], BB[0:T, :], ALU.add, [x, BB], [x], eng="pool")

    def state_out(uc_out_ap, cols_ap, cols_bufs, n, UC, USB, dst_ap, dst_buf):
        TCOPY(uc_out_ap, cols_ap, cols_bufs, [UC.buf], eng="pool")
        for k in range(8):
            tr(Fp[0:n, k * 128:(k + 1) * 128], UC[:, k, 0:n], identf[:, :], [UC.buf, identf], [Fp])
        ACOPY(USB[0:n, :], Fp[0:n, :], [Fp], [USB.buf])
        st(dst_ap, USB[0:n, :], [USB.buf], dst_buf)

    def attn_group(T, j, kvh, QT, qc0, ktp, ktp_buf, kto, kto_buf, vp, vp_buf, vo, vo_buf, mask, SS, PBt, PTS, sm,
                   Odst, Odst_buf):
        NK = 256
        S3 = Sp.t[0:T, 0:1024].rearrange("p (g s) -> p g s", g=4)
        for g in range(4):
            h = kvh * 4 + g
            ch, hf = h // 2, h % 2
            s = 2 * (g % 2) + g // 2
            ps = slice(hf * 64, hf * 64 + 64)
            sb = Sb[s // 2]
            mm(S3[:, s, 0:128], QT[ps, ch, qc0:qc0 + T], ktp(ps), True, True, [QT.buf, ktp_buf], [sb])
            mm(S3[:, s, 128:NK], QT[ps, ch, qc0:qc0 + T], kto(ps), True, True, [QT.buf, kto_buf], [sb])
        mx, den, es = sm[0:T, 0:4], sm[0:T, 4:8], sm[0:T, 8:12]
        mx3 = mx.rearrange("p (a b) -> p a b", a=2)
        es3 = es.rearrange("p (a b) -> p a b", a=2)
        b0 = j * 16 + kvh * 4
        sk3 = sinkb[0:T, b0:b0 + 4].rearrange("p (hi lo) -> p lo hi", hi=2)
        TT(SS[0:T, :, 0:NK], S3[:, :, 0:NK], mask[0:T, 0:NK].unsqueeze(1).to_broadcast([T, 4, NK]), ALU.add,
           [Sb[0], Sb[1], mask], [SS.buf])
        TRED(mx, SS[0:T, :, 0:NK], ALU.max, [SS.buf], [sm.buf])
        TT(mx3, mx3, sk3, ALU.max, [sm.buf, sinkb], [sm.buf])
        TT(SS[0:T, :, 0:NK], SS[0:T, :, 0:NK], mx.unsqueeze(2).to_broadcast([T, 4, NK]), ALU.subtract,
           [SS.buf, sm.buf], [SS.buf])
        ACTF(PBt[0:T, :, 0:NK], SS[0:T, :, 0:NK], AF.Exp, [SS.buf], [PBt.buf])
        TRED(den, PBt[0:T, :, 0:NK], ALU.add, [PBt.buf], [sm.buf])
        TT(es3, sk3, mx3, ALU.subtract, [sm.buf, sinkb], [sm.buf])
        ACTF(es, es, AF.Exp, [sm.buf], [sm.buf])
        TT(den, den, es, ALU.add, [sm.buf], [sm.buf])
        RECIP(den, den, [sm.buf], [sm.buf])
        for s in range(4):
            tr(A_bf[:, s * 2, 0:T], PBt[0:T, s, 0:128], identb[0:T, 0:T], [PBt.buf, identb], [Ap])
            tr(A_bf[:, s * 2 + 1, 0:T], PBt[0:T, s, 128:NK], identb[0:T, 0:T], [PBt.buf, identb], [Ap])
        ACOPY(PTS[:, :, 0:T], A_bf[:, :, 0:T], [Ap], [PTS.buf])
        O3 = Sp.t[0:T, 1024:1280].rearrange("p (g d) -> p g d", g=4)
        for s in range(4):
            mm(O3[:, s, :], PTS[:, s * 2, 0:T], vp, True, False, [PTS.buf, vp_buf], [Sb[2]])
            mm(O3[:, s, :], PTS[:, s * 2 + 1, 0:T], vo, False, True, [PTS.buf, vo_buf], [Sb[2]])
        TT(Odst.rearrange("p (a b) d -> p a b d", a=2), O3.rearrange("p (a b) d -> p b a d", a=2),
           den.rearrange("p (a b) -> p b a", a=2).unsqueeze(3).to_broadcast([T, 2, 2, 64]), ALU.mult,
           [Sb[2], sm.buf], [Odst_buf])

    def attn_phase(i, skip):
        j = i // 2
        units = units_from(skip)
        ph = "attn"
        arena_off[0] = 0
        W = aalloc(ph, "W", [128, 8, 2048], BF16)
        WO = aalloc(ph, "WO", [128, 8, 1024], BF16)
        QT = aalloc(ph, "QT", [128, 8, 128], BF16)
        KT = [aalloc(ph, "KT%d" % t, [128, 4, 128], BF16) for t in range(2)]
        VB = [aalloc(ph, "VB%d" % t, [128, 256], BF16) for t in range(2)]
        KVTOK = aalloc(ph, "KVTOK", [128, 512], F32)
        SS = aalloc(ph, "SS", [128, 4, 256], F32)
        PBt = aalloc(ph, "PB", [128, 4, 256], BF16)
        PTS = aalloc(ph, "PTS", [128, 8, 128], BF16)
        sm = aalloc(ph, "sm", [128, 16], F32)
        KTC = [aalloc(ph, "KTC%d" % t, [128, 4, 128], BF16) for t in range(2)]
        VC = [aalloc(ph, "VC%d" % t, [128, 256], BF16) for t in range(2)]
        VOWN = aalloc(ph, "VOWN", [128, 256], BF16)
        KTO = aalloc(ph, "KTO", [128, 4, 128], BF16)
        OB4 = [aalloc(ph, "OB4%d" % t, [128, 1024], BF16) for t in range(2)]
        barrier()
        load_w(W, 8, 2048, wqkv_d[j], wqkv_d)
        load_w(WO, 8, 1024, wo_d[j], wo_d)
        load_ln(i, 0)
        if i == 0:
            precast(0)
        for t in range(2):
            MEMSET(KT[t][:], 0.0, [KT[t].buf])
            MEMSET(VB[t][:], 0.0, [VB[t].buf])
        for (u, T) in units:
            samp = (u == SU)
            cur = u % 2
            prv = 1 - cur
            make_xT(u, T)
            proj_fm(W, 0, 8, T, QT, 0.125)
            proj_fm(W, 1024, 4, T, KT[cur], 1.0)
            pb, pt = next_pj()
            for k in range(8):
                mm(pt[0:T, 0:512], XT[:, k, 0:T], W[:, k, 1536:2048], k == 0, k == 7, [XT, W.buf], [pb])
            ACOPY(KVTOK[0:T, :], pt[0:T, 0:512], [pb], [KVTOK.buf])
            if not samp:
                TCOPY(VB[cur][0:T, :], KVTOK[0:T, 256:512], [KVTOK.buf], [VB[cur].buf])
                mask = maskF if u == NH else maskA
                for kvh in range(4):
                    attn_group(T, j, kvh, QT, 0,
                               lambda ps, kvh=kvh, prv=prv: KT[prv][ps, kvh, :], KT[prv].buf,
                               lambda ps, kvh=kvh, cur=cur: KT[cur][ps, kvh, :], KT[cur].buf,
                               VB[prv][:, kvh * 64:(kvh + 1) * 64], VB[prv].buf,
                               VB[cur][:, kvh * 64:(kvh + 1) * 64], VB[cur].buf,
                               mask, SS, PBt, PTS, sm,
                               OB[0:T, kvh * 256:(kvh + 1) * 256].rearrange("p (g d) -> p g d", g=4), OB)
                if u == NPB - 1:
                    st(wkp_d[j], KVTOK[:, 0:256], [KVTOK.buf], wkp_d)
                    st(wvp_d[j], KVTOK[:, 256:512], [KVTOK.buf], wvp_d)
            else:
                st(wks_d[j][:, 0:124, :], cwk_d[j][:, 4:128, :], [cwk_d], wks_d)
                st(wvs_d[j][:, 0:124, :], cwv_d[j][:, 4:128, :], [cwv_d], wvs_d)
                for b in range(NSB):
                    st(wks_d[j, b, 124:128, :], KVTOK[b * 4:(b + 1) * 4, 0:256], [KVTOK.buf], wks_d)
                    st(wvs_d[j, b, 124:128, :], KVTOK[b * 4:(b + 1) * 4, 256:512], [KVTOK.buf], wvs_d)
                MEMSET(KTO[:], 0.0, [KTO.buf])
                MEMSET(VOWN[:], 0.0, [VOWN.buf])
                for b in range(NSB):
                    t2 = b % 2
                    ld(KTC[t2][:], cwkT_d[j, :, b], [cwkT_d], [KTC[t2].buf], eng="pool")
                    ld(VC[t2][:], cwv_d[j, b], [cwv_d], [VC[t2].buf], eng="pool")
                    TCOPY(KTO[:, :, 0:4], KT[cur][:, :, b * 4:(b + 1) * 4], [KT[cur].buf], [KTO.buf], eng="pool")
                    pb, pt = next_pj()
                    for k in range(8):
                        mm(pt[0:4, 0:256], XT[:, k, b * 4:(b + 1) * 4], W[:, k, 1792:2048], k == 0, k == 7,
                           [XT, W.buf], [pb])
                    ACOPY(VOWN[0:4, :], pt[0:4, 0:256], [pb], [VOWN.buf])
                    for kvh in range(4):
                        attn_group(4, j, kvh, QT, b * 4,
                                   lambda ps, kvh=kvh, t2=t2: KTC[t2][ps, kvh, :], KTC[t2].buf,
                                   lambda ps, kvh=kvh: KTO[ps, kvh, :], KTO.buf,
                                   VC[t2][:, kvh * 64:(kvh + 1) * 64], VC[t2].buf,
                                   VOWN[:, kvh * 64:(kvh + 1) * 64], VOWN.buf,
                                   maskA, SS, PBt, PTS, sm,
                                   OB4[t2][0:4, kvh * 256:(kvh + 1) * 256].rearrange("p (g d) -> p g d", g=4),
                                   OB4[t2].buf)
                    ld(OB[b * 4:(b + 1) * 4, :], OB4[t2][0:4, :], [OB4[t2].buf], [OB])
            transpose8(OB, OT, T)
            proj_F(WO, T, OT, OT)
            resid_ln(u, T, Fp[0:T, :], [Fp])

    def conv_phase(i, skip):
        j = i // 2
        units = units_from(skip)
        ph = "conv"
        arena_off[0] = 0
        W = aalloc(ph, "W", [128, 8, 3072], BF16)
        WO = aalloc(ph, "WO", [128, 8, 1024], BF16)
        UP = aalloc(ph, "UP", [128, 8, 1, 130], F32)
        UPS = aalloc(ph, "UPS", [128, 8, NSB, 6], F32)
        STT_ = aalloc(ph, "STT", [128, 8, NSB, 2], F32)
        HS = aalloc(ph, "HS", [128, 128], F32)
        Z = aalloc(ph, "Z", [128, 128], F32)
        GZ = aalloc(ph, "GZ", [128, 8, 128], BF16)
        UC = aalloc(ph, "UC", [128, 8, 32], F32)
        USB = aalloc(ph, "USB", [128, 1024], F32)
        barrier()
        load_w(W, 8, 3072, win_d[j], win_d)
        load_w(WO, 8, 1024, wout_d[j], wout_d)
        load_ln(i, 0)
        MEMSET(UP[:, :, :, 0:2], 0.0, [UP.buf])
        ld(STT_[:].rearrange("p a b c -> p (a b c)"), stT_d[j], [stT_d], [STT_.buf])
        TCOPY(UPS[:, :, :, 0:2], STT_[:], [STT_.buf], [UPS.buf], eng="pool")
        for (u, T) in units:
            samp = (u == SU)
            up = UPS if samp else UP
            nb, L = (NSB, 4) if samp else (1, 128)
            if u == NH:
                TSC(UP[:, :, :, 0:2], UP[:, :, :, 0:2], cflag[:, 0:1], None, ALU.mult, None, [UP.buf, cflag], [UP.buf])
            make_xT(u, T)

            def v3(ap, nb=nb):
                return ap.rearrange("p (b l) -> p b l", b=nb)
            for c in range(8):
                pb, pt = next_pj()
                pv = pt.rearrange("p (c t) -> p c t", c=4)
                for part in range(3):
                    for k in range(8):
                        mm(pv[:, part, 0:T], W[:, k, part * 1024 + c * 128: part * 1024 + (c + 1) * 128], XT[:, k, 0:T],
                           k == 0, k == 7, [W.buf, XT], [pb])
                ACOPY(HS[:, 0:T], pv[:, 2, 0:T], [pb], [HS.buf])
                TT(up[:, c, :, 2:2 + L], v3(pv[:, 1, 0:T]), v3(HS[:, 0:T]), ALU.mult, [pb, HS.buf], [up.buf])
                o = (j * 8 + c) * 3
                TSC(v3(Z[:, 0:T]), up[:, c, :, 0:L], CW[:, o:o + 1], None, ALU.mult, None, [up.buf, CW], [Z.buf])
                STT(v3(Z[:, 0:T]), up[:, c, :, 1:1 + L], CW[:, o + 1:o + 2], v3(Z[:, 0:T]), ALU.mult, ALU.add,
                    [up.buf, CW, Z.buf], [Z.buf])
                STT(v3(Z[:, 0:T]), up[:, c, :, 2:2 + L], CW[:, o + 2:o + 3], v3(Z[:, 0:T]), ALU.mult, ALU.add,
                    [up.buf, CW, Z.buf], [Z.buf])
                TT(GZ[:, c, 0:T], pv[:, 0, 0:T], Z[:, 0:T], ALU.mult, [pb, Z.buf], [GZ.buf])
            proj_F(WO, T, GZ, GZ.buf)
            resid_ln(u, T, Fp[0:T, :], [Fp])
            if samp:
                state_out(UC[:, :, 0:NSB * 2].rearrange("p a (b r) -> p a b r", r=2), UPS[:, :, :, 4:6], [UPS.buf],
                          NSB * 2, UC, USB, cvs_d[j], cvs_d)
            else:
                if u == NPB - 1:
                    state_out(UC[:, :, 0:2], UP[:, :, 0, 128:130], [UP.buf], 2, UC, USB, cvp_d[j], cvp_d)
                TCOPY(UP[:, :, :, 0:2], UP[:, :, :, 128:130], [UP.buf], [UP.buf], eng="pool")

    def mem_unit(T, QT, qc0, MKt, MVt, SS, PBt, PTS, sm, Odst, Odst_buf):
        S3 = Sp.t[0:T, 0:1024].rearrange("p (g s) -> p g s", g=4)
        for h in range(4):
            sb = Sb[h // 2]
            mm(S3[:, h, :], QT[:, 2 * h, qc0:qc0 + T], MKt[:, 2 * h, :], True, False, [QT.buf, MKt.buf], [sb])
            mm(S3[:, h, :], QT[:, 2 * h + 1, qc0:qc0 + T], MKt[:, 2 * h + 1, :], False, True, [QT.buf, MKt.buf], [sb])
        mx, den = sm[0:T, 0:4], sm[0:T, 4:8]
        TRED(mx, S3, ALU.max, [Sb[0], Sb[1]], [sm.buf])
        TT(SS[0:T, :, :], S3, mx.unsqueeze(2).to_broadcast([T, 4, 256]), ALU.subtract, [Sb[0], Sb[1], sm.buf], [SS.buf])
        ACTF(PBt[0:T, :, :], SS[0:T, :, :], AF.Exp, [SS.buf], [PBt.buf])
        TRED(den, PBt[0:T, :, :], ALU.add, [PBt.buf], [sm.buf])
        RECIP(den, den, [sm.buf], [sm.buf])
        for h in range(4):
            for mc in range(2):
                tr(A_bf[:, h * 2 + mc, 0:T], PBt[0:T, h, mc * 128:(mc + 1) * 128], identb[0:T, 0:T], [PBt.buf, identb], [Ap])
        ACOPY(PTS[:, :, 0:T], A_bf[:, :, 0:T], [Ap], [PTS.buf])
        for h in range(4):
            for mc in range(2):
                mm(Fp[0:T, h * 256:(h + 1) * 256], PTS[:, h * 2 + mc, 0:T], MVt[:, mc, h * 256:(h + 1) * 256],
                   mc == 0, mc == 1, [PTS.buf, MVt.buf], [Fp])
        TT(Odst, Fp[0:T, :].rearrange("p (h d) -> p h d", h=4), den.unsqueeze(2).to_broadcast([T, 4, 256]), ALU.mult,
           [Fp, sm.buf], [Odst_buf])

    def mem_phase(i, skip):
        ph = "mem"
        units = units_from(skip)
        arena_off[0] = 0
        MK = aalloc(ph, "MK", [128, 8, 256], BF16)
        MVt = aalloc(ph, "MV", [128, 2, 1024], BF16)
        mark = arena_off[0]
        WKV = aalloc(ph, "WKV", [128, 8, 2048], BF16)
        MEMT = aalloc(ph, "MEMT", [128, 8, 256], BF16)
        Y2 = aalloc(ph, "Y2", [128, 1024], F32)
        barrier()
        load_w(WKV, 8, 2048, mwkv_d[i], mwkv_d)
        ld(MEMT[:].rearrange("p a b -> p (a b)"), memT_d[:], [memT_d], [MEMT.buf], eng="pool")
        for mc in range(2):
            for kv in range(2):
                for n in range(2):
                    for k in range(8):
                        mm(Fp[:, n * 512:(n + 1) * 512], MEMT[:, k, mc * 128:(mc + 1) * 128],
                           WKV[:, k, kv * 1024 + n * 512: kv * 1024 + (n + 1) * 512], k == 0, k == 7,
                           [MEMT.buf, WKV.buf], [Fp])
                if kv == 0:
                    ACOPY(Y[:, :], Fp[:, :], [Fp], [Y])
                    st(mkp_d[i, mc * 128:(mc + 1) * 128, :], Y[:, :], [Y], mkp_d)
                else:
                    ACOPY(Y2[:, :], Fp[:, :], [Fp], [Y2.buf])
                    st(mvp_d[i, mc * 128:(mc + 1) * 128, :], Y2[:, :], [Y2.buf], mvp_d)
                    TCOPY(MVt[:, mc, :], Fp[:, :], [Fp], [MVt.buf])
        for c0 in range(0, 8, 2):
            pb, pt = next_pj()
            pv = pt.rearrange("p (c t) -> p c t", c=2)
            for c in range(2):
                for k in range(8):
                    mm(pv[:, c, :], WKV[:, k, (c0 + c) * 128:(c0 + c + 1) * 128], MEMT[:, k, :], k == 0, k == 7,
                       [WKV.buf, MEMT.buf], [pb])
            ACOPY(MK[:, c0:c0 + 2, :], pv[:, :, :], [pb], [MK.buf])
        arena_off[0] = mark
        WQ = aalloc(ph, "WQ", [128, 8, 1024], BF16)
        WO = aalloc(ph, "WO", [128, 8, 1024], BF16)
        QT = aalloc(ph, "QT", [128, 8, 128], BF16)
        SS = aalloc(ph, "SS", [128, 4, 256], F32)
        PBt = aalloc(ph, "PB", [128, 4, 256], BF16)
        PTS = aalloc(ph, "PTS", [128, 8, 128], BF16)
        sm = aalloc(ph, "sm", [128, 16], F32)
        MKC = [aalloc(ph, "MKC%d" % t, [128, 8, 256], BF16) for t in range(2)]
        MVC = [aalloc(ph, "MVC%d" % t, [128, 2, 1024], BF16) for t in range(2)]
        OB4 = [aalloc(ph, "OB4%d" % t, [128, 1024], BF16) for t in range(2)]
        barrier()
        load_w(WQ, 8, 1024, mwq_d[i], mwq_d)
        load_w(WO, 8, 1024, mwo_d[i], mwo_d)
        load_ln(i, 1)
        for (u, T) in units:
            samp = (u == SU)
            make_xT(u, T)
            proj_fm(WQ, 0, 8, T, QT, 1.0 / 16.0)
            if not samp:
                mem_unit(T, QT, 0, MK, MVt, SS, PBt, PTS, sm, OB[0:T, :].rearrange("p (h d) -> p h d", h=4), OB)
            else:
                for b in range(NSB):
                    t2 = b % 2
                    ld(MKC[t2][:].rearrange("p a b -> p (a b)"), cmkT_d[i, b], [cmkT_d], [MKC[t2].buf], eng="pool")
                    ld(MVC[t2][:], cmv_d[i, b].rearrange("(mc p) c -> p mc c", p=128), [cmv_d], [MVC[t2].buf], eng="pool")
                    mem_unit(4, QT, b * 4, MKC[t2], MVC[t2], SS, PBt, PTS, sm,
                             OB4[t2][0:4, :].rearrange("p (h d) -> p h d", h=4), OB4[t2].buf)
                    ld(OB[b * 4:(b + 1) * 4, :], OB4[t2][0:4, :], [OB4[t2].buf], [OB])
            transpose8(OB, OT, T)
            proj_F(WO, T, OT, OT)
            resid_ln(u, T, Fp[0:T, :], [Fp])

    def peer_phase(i, last, skip):
        ph = "peer"
        units = units_from(skip)
        arena_off[0] = 0
        W = aalloc(ph, "W", [128, 8, 2048], BF16)
        SK = aalloc(ph, "SK", [128, 2, 128], BF16)
        QT = aalloc(ph, "QT", [128, 16, 128], BF16)
        SV = aalloc(ph, "SV", [128, 16, 16], F32)
        SI = aalloc(ph, "SI", [128, 16, 16], U32)
        SIF = aalloc(ph, "SIF", [128, 16, 16], F32)
        SW = aalloc(ph, "SW", [128, 256], F32)
        COMB = aalloc(ph, "COMB", [128, 8, 256], F32)
        EQ = aalloc(ph, "EQ", [128, 8, 256], F32)
        CS = aalloc(ph, "CS", [128, 8, 16], F32)
        CI = aalloc(ph, "CI", [128, 8, 16], U32)
        CA = aalloc(ph, "CA", [128, 8, 16], U32)
        CAF = aalloc(ph, "CAF", [128, 8, 16], F32)
        GE = aalloc(ph, "GE", [128, 8, 16], F32)
        I1 = aalloc(ph, "I1", [128, 8, 16], F32)
        I2 = aalloc(ph, "I2", [128, 8, 16], F32)
        EIDI = aalloc(ph, "EIDI", [128, 128], I32)
        GEF = aalloc(ph, "GEF", [128, 128], F32)
        AAp = [aalloc(ph, "AA%d" % t, [128, 2], F32) for t in range(2)]
        WWp = [aalloc(ph, "WW%d" % t, [128, 2], F32) for t in range(2)]
        sm = aalloc(ph, "sm", [128, 16], F32)
        ACC = aalloc(ph, "ACC", [128, 1024], F32)
        JUNK = aalloc(ph, "JUNK", [128, 1024], BF16)
        NG = 8
        RING = [aalloc(ph, "RING%d" % t, [128, 2048], BF16) for t in range(NG)]
        barrier()
        load_w(W, 8, 2048, pwq_d[i], pwq_d)
        for c in range(2):
            ld(SK[:, c, :], skT_d[i, c], [skT_d], [SK.buf], eng="pool")
        load_ln(i, 2)
        if i + 1 < D:
            precast(i + 1)
        MEMSET(EIDI[:], 0, [EIDI.buf])
        gi = [0]
        for (u, T) in units:
            x = X[u]
            make_xT(u, T)
            proj_fm(W, 0, 16, T, QT, 1.0)
            S3 = Sp.t[0:T, :].rearrange("p (c n) -> p c n", c=16)
            for c in range(16):
                mm(S3[:, c, :], QT[:, c, 0:T], SK[:, c % 2, :], True, True, [QT.buf, SK.buf], [Sb[c // 4]])
            for c in range(16):
                sb = Sb[c // 4]
                src = S3[:, c, :]
                MAX8(SV[0:T, c, 0:8], src, [sb], [SV.buf])
                MAXIDX(SI[0:T, c, 0:8], SV[0:T, c, 0:8], src, [sb, SV.buf], [SI.buf])
                MATCHREP(SW[0:T, 0:128], SV[0:T, c, 0:8], src, [sb, SV.buf], [SW.buf])
                MAX8(SV[0:T, c, 8:16], SW[0:T, 0:128], [SW.buf], [SV.buf])
                MAXIDX(SI[0:T, c, 8:16], SV[0:T, c, 8:16], SW[0:T, 0:128], [SW.buf, SV.buf], [SI.buf])
            SV4 = SV[0:T].rearrange("p (h c) k -> p h c k", c=2)
            SIF4 = SIF[0:T].rearrange("p (h c) k -> p h c k", c=2)
            C4 = COMB[0:T].rearrange("p h (a b) -> p h a b", a=16)
            E4 = EQ[0:T].rearrange("p h (a b) -> p h a b", a=16)
            TCOPY(SIF[0:T], SI[0:T], [SI.buf], [SIF.buf])
            TT(C4, SV4[:, :, 0, :].unsqueeze(3).to_broadcast([T, 8, 16, 16]),
               SV4[:, :, 1, :].unsqueeze(2).to_broadcast([T, 8, 16, 16]), ALU.add, [SV.buf], [COMB.buf])
            for h in range(8):
                src = COMB[0:T, h, :]
                MAX8(CS[0:T, h, 0:8], src, [COMB.buf], [CS.buf])
                MAXIDX(CI[0:T, h, 0:8], CS[0:T, h, 0:8], src, [COMB.buf, CS.buf], [CI.buf])
                MATCHREP(SW[0:T, :], CS[0:T, h, 0:8], src, [COMB.buf, CS.buf], [SW.buf])
                MAX8(CS[0:T, h, 8:16], SW[0:T, :], [SW.buf], [CS.buf])
                MAXIDX(CI[0:T, h, 8:16], CS[0:T, h, 8:16], SW[0:T, :], [SW.buf, CS.buf], [CI.buf])
            TT(GE[0:T], CS[0:T], CS[0:T, :, 0:1].to_broadcast([T, 8, 16]), ALU.subtract, [CS.buf], [GE.buf])
            ACTF(GE[0:T], GE[0:T], AF.Exp, [GE.buf], [GE.buf])
            TRED(sm[0:T, 0:8], GE[0:T], ALU.add, [GE.buf], [sm.buf])
            RECIP(sm[0:T, 0:8], sm[0:T, 0:8], [sm.buf], [sm.buf])
            TT(GE[0:T], GE[0:T], sm[0:T, 0:8].unsqueeze(2).to_broadcast([T, 8, 16]), ALU.mult, [GE.buf, sm.buf], [GE.buf])
            for which, Idst in ((0, I1), (1, I2)):
                if which == 0:
                    TSS(CA[0:T], CI[0:T], 4, ALU.logical_shift_right, [CI.buf], [CA.buf])
                else:
                    TSS(CA[0:T], CI[0:T], 15, ALU.bitwise_and, [CI.buf], [CA.buf])
                TCOPY(CAF[0:T], CA[0:T], [CA.buf], [CAF.buf])
                TT(E4, iota16[0:T, :].unsqueeze(1).unsqueeze(1).to_broadcast([T, 8, 16, 16]),
                   CAF[0:T].unsqueeze(3).to_broadcast([T, 8, 16, 16]), ALU.is_equal, [iota16, CAF.buf], [EQ.buf])
                TT(E4, E4, SIF4[:, :, which, :].unsqueeze(2).to_broadcast([T, 8, 16, 16]), ALU.mult,
                   [EQ.buf, SIF.buf], [EQ.buf])
                TRED(Idst[0:T], E4, ALU.add, [EQ.buf], [Idst.buf])
            STT(I1[0:T], I1[0:T], 128.0, I2[0:T], ALU.mult, ALU.add, [I1.buf, I2.buf], [I1.buf])
            TCOPY(EIDI[0:T, :], I1[0:T].rearrange("p h k -> p (h k)"), [I1.buf], [EIDI.buf])
            TCOPY(GEF[0:T, :], GE[0:T].rearrange("p h k -> p (h k)"), [GE.buf], [GEF.buf])
            NB = 64

            def dots(b):
                for q in range(2):
                    jj = 2 * b + q
                    slot = RING[jj % NG]
                    GATHER(slot[:, :], puvb_d[i][:, :], EIDI[:, jj:jj + 1], [EIDI.buf, puvb_d[i]], [slot.buf])
                    TTR(JUNK[0:T, :], x[0:T, :], slot[0:T, 0:1024], AAp[b % 2][0:T, q:q + 1], [x, slot.buf],
                        [JUNK.buf, AAp[b % 2].buf])
                ACTF(WWp[b % 2][0:T, :], AAp[b % 2][0:T, :], AF.Gelu, [AAp[b % 2].buf], [WWp[b % 2].buf])

            def axpys(b):
                TT(WWp[b % 2][0:T, :], WWp[b % 2][0:T, :], GEF[0:T, 2 * b:2 * b + 2], ALU.mult,
                   [WWp[b % 2].buf, GEF.buf], [WWp[b % 2].buf])
                for q in range(2):
                    jj = 2 * b + q
                    slot = RING[jj % NG]
                    if jj == 0:
                        TSC(ACC[0:T, :], slot[0:T, 1024:2048], WWp[b % 2][0:T, q:q + 1], None, ALU.mult, None,
                            [slot.buf, WWp[b % 2].buf], [ACC.buf])
                    else:
                        STT(ACC[0:T, :], slot[0:T, 1024:2048], WWp[b % 2][0:T, q:q + 1], ACC[0:T, :], ALU.mult, ALU.add,
                            [slot.buf, WWp[b % 2].buf, ACC.buf], [ACC.buf])

            dots(0)
            for b in range(NB):
                if b + 1 < NB:
                    dots(b + 1)
                axpys(b)
            resid_ln(u, T, ACC[0:T, :], [ACC.buf])
            if last:
                if u == SU:
                    st(ys_d[:, :], x[0:TS, :], [x], ys_d)
                elif u >= NH:
                    st(yp_d[(u - NH) * 128:(u - NH + 1) * 128, :], x[:, :], [x], yp_d)

    HS = cfg.halo_skip
    for i in range(D):
        s_mix, s_rest = HS[i] if i < len(HS) else (0, 0)
        if i % 2 == 0:
            attn_phase(i, s_mix)
        else:
            conv_phase(i, s_mix)
        mem_phase(i, s_rest)
        peer_phase(i, i == D - 1, s_rest)
    P.emit()
    return nc, stack, P


def prep_inputs(cfg, inp):
    D, B, SEQ, DB, n = cfg.D, cfg.B, cfg.SEQ, cfg.DB, cfg.n
    NA, NC, NSB, NH, NOWN, cps = cfg.NA, cfg.NC, cfg.NSB, cfg.NH, cfg.NOWN, cfg.cps
    f = lambda a: np.ascontiguousarray(np.asarray(a, dtype=np.float32))
    g = {k: np.asarray(v) for k, v in inp.items()}
    wqkv_src = g["attn_w_qkv"]
    q = wqkv_src[:, :, 0:1024]
    kk = wqkv_src[:, :, 1024:1280]
    vv = wqkv_src[:, :, 1280:1536]
    kdup = np.concatenate([np.concatenate([kk[:, :, h * 64:(h + 1) * 64]] * 2, axis=2) for h in range(4)], axis=2)
    wqkv = f(np.concatenate([q, kdup, kk, vv], axis=2))
    ii = np.arange(128)
    maskA = np.full((128, 256), NEG, np.float32)
    maskA[:, 0:128][ii[None, :] >= ii[:, None]] = 0.0
    maskA[:, 128:256][ii[None, :] <= ii[:, None]] = 0.0
    maskN = maskA.copy()
    maskN[:, 0:128] = NEG
    ncv = max(NC, 1)
    if NC > 0:
        cw = g["conv_w"].reshape(NC, 3, 8, 128)
        cw = f(np.transpose(cw, (3, 0, 2, 1)).reshape(128, NC * 8 * 3))
        win = f(g["conv_w_in"])
        wout = f(g["conv_w_out"])
    else:
        cw = np.zeros((128, 24), np.float32)
        win = np.zeros((1, 1024, 3072), np.float32)
        wout = np.zeros((1, 1024, 1024), np.float32)
    skT = f(np.transpose(g["peer_sub_keys"], (0, 1, 3, 2)))
    shared = {
        "maskA": maskA,
        "wqkv": wqkv, "wo": f(g["attn_w_o"]), "sinks": f(g["attn_sinks"]).reshape(1, NA * 16),
        "win": win, "cw": cw, "wout": wout,
        "mwq": f(g["mem_w_q"]), "mwkv": f(g["mem_w_kv"]), "mwo": f(g["mem_w_o"]),
        "pwq": f(g["peer_w_q"]), "skT": skT,
        "puv": f(np.concatenate([g["peer_u"], g["peer_v"]], axis=-1)).reshape(D * 16384, 2048),
        "lng": f(g["ln_g"]), "lnb": f(g["ln_b"]),
    }
    maps = []
    for c in range(n):
        b, qd = c // cps, c % cps
        t0 = qd * NOWN * 128
        xp = np.zeros(((NOWN + NH) * 128, 1024), np.float32)
        lo = t0 - NH * 128
        lo_c = max(lo, 0)
        xp[lo_c - lo:] = g["x_prompt"][b, lo_c:t0 + NOWN * 128]
        first = (qd == 0)
        sb = slice(c * NSB, (c + 1) * NSB)
        cwk = g["cache_win_k"][:, sb].reshape(NA, NSB, 128, 256)
        cwv = g["cache_win_v"][:, sb].reshape(NA, NSB, 128, 256)
        kT = np.transpose(g["cache_win_k"][:, sb], (0, 4, 1, 3, 2))
        cwkT = np.concatenate([kT, kT], axis=1)
        if NC > 0:
            stt = g["state_conv"][:, sb].reshape(NC, NSB, 2, 8, 128)
            stT = np.transpose(stt, (0, 4, 3, 1, 2)).reshape(NC, 128, 8 * NSB * 2)
        else:
            stT = np.zeros((1, 128, 8 * NSB * 2), np.float32)
        memT = np.transpose(g["mem_prompt"][b].reshape(256, 8, 128), (2, 1, 0)).reshape(128, 8 * 256)
        cmk = g["cache_mem_k"][:, sb].reshape(D, NSB, 256, 4, 2, 128)
        cmkT = np.transpose(cmk, (0, 1, 5, 3, 4, 2)).reshape(D, NSB, 128, 8 * 256)
        cmv = g["cache_mem_v"][:, sb].reshape(D, NSB, 256, 1024)
        m = dict(shared)
        m.update({
            "xp": f(xp), "xs": f(g["x_sample"][sb].reshape(NSB * 4, 1024)),
            "maskF": maskN if first else maskA,
            "cflag": np.full((128, 1), 0.0 if first else 1.0, np.float32),
            "cwkT": f(cwkT), "cwk": f(cwk), "cwv": f(cwv), "stT": f(stT),
            "memT": f(memT), "cmkT": f(cmkT), "cmv": f(cmv),
        })
        maps.append(m)
    return maps


def assemble(cfg, res):
    D, B, SEQ, DB, n = cfg.D, cfg.B, cfg.SEQ, cfg.DB, cfg.n
    NA, NC, NSB, NOWN, cps = cfg.NA, cfg.NC, cfg.NSB, cfg.NOWN, cfg.cps
    y_p = np.zeros((B, SEQ, 1024), np.float32)
    y_s = np.zeros((DB, 4, 1024), np.float32)
    wkp = np.zeros((NA, B, 128, 4, 64), np.float32)
    wvp = np.zeros((NA, B, 128, 4, 64), np.float32)
    cvp = np.zeros((NC, B, 2, 1024), np.float32)
    mkp = np.zeros((D, B, 256, 4, 256), np.float32)
    mvp = np.zeros((D, B, 256, 4, 256), np.float32)
    wks = np.zeros((NA, DB, 128, 4, 64), np.float32)
    wvs = np.zeros((NA, DB, 128, 4, 64), np.float32)
    cvs = np.zeros((NC, DB, 2, 1024), np.float32)
    for c in range(n):
        r = res[c]
        b, qd = c // cps, c % cps
        t0 = qd * NOWN * 128
        y_p[b, t0:t0 + NOWN * 128] = r["yp"]
        sb = slice(c * NSB, (c + 1) * NSB)
        y_s[sb] = r["ys"].reshape(NSB, 4, 1024)
        wks[:, sb] = r["wks"].reshape(NA, NSB, 128, 4, 64)
        wvs[:, sb] = r["wvs"].reshape(NA, NSB, 128, 4, 64)
        if NC > 0:
            cvs[:, sb] = r["cvs"].reshape(NC, NSB, 2, 1024)
        if qd == cps - 1:
            wkp[:, b] = r["wkp"].reshape(NA, 128, 4, 64)
            wvp[:, b] = r["wvp"].reshape(NA, 128, 4, 64)
            if NC > 0:
                cvp[:, b] = r["cvp"][:NC]
        if qd == 0:
            mkp[:, b] = r["mkp"].reshape(D, 256, 4, 256)
            mvp[:, b] = r["mvp"].reshape(D, 256, 4, 256)
    return (y_p, y_s, wkp, wvp, cvp, mkp, mvp, wks, wvs, cvs)


def run_cfg(cfg, inputs, trace=False):
    nc, stack, P = build(cfg)
    maps = prep_inputs(cfg, inputs)
    res = run_bass_kernel_spmd(nc, maps, core_ids=list(range(cfg.n)))
    return assemble(cfg, res.results)


def kernel(**inputs):
    cfg = Cfg()
    return run_cfg(cfg, inputs)
```

```python
from contextlib import ExitStack
import numpy as np
import concourse.bass as bass
import concourse.mybir as mybir
from concourse.alu_op_type import AluOpType as ALU
from concourse.bass_utils import run_bass_kernel_spmd

F32 = mybir.dt.float32
BF16 = mybir.dt.bfloat16
I32 = mybir.dt.int32
U32 = mybir.dt.uint32
AF = mybir.ActivationFunctionType
AX = mybir.AxisListType

SAME_ENGINE_SYNC = True


class Buf:
    def __init__(self, prog, name, t, space):
        self.prog = prog
        self.name = name
        self.t = t
        self.space = space
        self.sem = None
        self.dma_cnt = 0
        self.last_w = None
        self.readers = {}

    def __getitem__(self, key):
        return self.t[key]


class Op:
    __slots__ = ("eng", "fn", "deps", "signal", "sig_val", "is_dma", "sem_buf", "dma_val", "idx")

    def __init__(self, eng, fn, is_dma=False):
        self.eng = eng
        self.fn = fn
        self.deps = []
        self.signal = False
        self.sig_val = 0
        self.is_dma = is_dma
        self.sem_buf = None
        self.dma_val = 0


class Prog:
    ENGS = ("pe", "act", "dve", "pool", "sp")

    def __init__(self, nc, stack):
        self.nc = nc
        self.stack = stack
        self.ops = {e: [] for e in self.ENGS}
        self.bufs = []
        self.nops = 0
        self.out_bufs = []
        import os
        self.limit = int(os.environ.get("KCUT", "1000000000"))

    def sbuf(self, name, shape, dtype):
        t = self.stack.enter_context(self.nc.sbuf_tensor(name, list(shape), dtype))
        b = Buf(self, name, t, "sb")
        self.bufs.append(b)
        return b

    def psum(self, name, shape, dtype):
        t = self.stack.enter_context(self.nc.psum_tensor(name, list(shape), dtype))
        b = Buf(self, name, t, "ps")
        self.bufs.append(b)
        return b

    def dram(self, name, shape, dtype, kind):
        t = self.nc.dram_tensor(name, list(shape), dtype, kind=kind).ap()
        b = Buf(self, name, t, "dr")
        self.bufs.append(b)
        if kind == "ExternalOutput":
            self.out_bufs.append(b)
        return b

    def alias(self, name, t, space="sb"):
        b = Buf(self, name, t, space)
        self.bufs.append(b)
        return b

    def _track(self, op, reads, writes):
        deps = op.deps
        ps_reads = [b for b in reads if b.space == "ps"]
        if ps_reads:
            reads = [b for b in reads if b.space != "ps"]
            writes = list(writes) + [b for b in ps_reads if b not in writes]
        for b in reads:
            if b.last_w is not None:
                deps.append(b.last_w)
        for b in writes:
            if b.last_w is not None:
                deps.append(b.last_w)
            for r in b.readers.values():
                deps.append(r)
        for b in reads:
            key = ("dma", id(op)) if op.is_dma else op.eng
            b.readers[key] = op
        for b in writes:
            b.last_w = op
            b.readers = {}

    def op(self, eng, fn, reads=(), writes=()):
        if self.nops >= self.limit:
            return None
        o = Op(eng, fn)
        self._track(o, reads, writes)
        self.ops[eng].append(o)
        self.nops += 1
        return o

    def dma(self, eng, fn, reads=(), writes=(), sem_buf=None):
        if self.nops >= self.limit:
            return None
        o = Op(eng, fn, is_dma=True)
        if sem_buf is None:
            sem_buf = writes[0]
        o.sem_buf = sem_buf
        sem_buf.dma_cnt += 1
        o.dma_val = 16 * sem_buf.dma_cnt
        self._track(o, reads, writes)
        self.ops[eng].append(o)
        self.nops += 1
        return o

    def emit(self):
        nc = self.nc
        stack = self.stack
        for e in self.ENGS:
            for o in self.ops[e]:
                for d in o.deps:
                    if d.is_dma:
                        continue
                    if d.eng == "pe" and o.eng == "pe" and not o.is_dma:
                        continue
                    if (not SAME_ENGINE_SYNC) and d.eng == o.eng and not o.is_dma:
                        continue
                    d.signal = True
        esem = {}
        for e in self.ENGS:
            if e == "sp":
                continue
            esem[e] = stack.enter_context(nc.semaphore("s_" + e))
            c = 0
            for o in self.ops[e]:
                if o.signal and not o.is_dma:
                    c += 1
                    o.sig_val = c
        for b in self.bufs:
            if b.dma_cnt > 0:
                b.sem = stack.enter_context(nc.semaphore("d_" + b.name))
        self.max_sig = {e: max([o.sig_val for o in self.ops[e]] + [0]) for e in self.ENGS}
        block = stack.enter_context(nc.Block())
        engobj = {"pe": block.tensor, "act": block.scalar, "dve": block.vector, "pool": block.gpsimd, "sp": block.sync}
        prog = self

        def make(e):
            def body(eng):
                waited = {}
                for o in prog.ops[e]:
                    for d in o.deps:
                        if d.is_dma:
                            sem, val = d.sem_buf.sem, d.dma_val
                        else:
                            if d.eng == "pe" and e == "pe" and not o.is_dma:
                                continue
                            if (not SAME_ENGINE_SYNC) and d.eng == e and not o.is_dma:
                                continue
                            sem, val = esem[d.eng], d.sig_val
                        k = id(sem)
                        if waited.get(k, 0) >= val:
                            continue
                        waited[k] = val
                        eng.wait_ge(sem, val)
                    ins = o.fn(eng)
                    if o.is_dma:
                        ins.then_inc(o.sem_buf.sem, 16)
                    elif o.signal:
                        ins.then_inc(esem[e], 1)
                if e == "sp":
                    for b in prog.out_bufs:
                        if b.dma_cnt > 0:
                            eng.wait_ge(b.sem, 16 * b.dma_cnt)
            return body

        for e in self.ENGS:
            engobj[e](make(e))


ALPHA = 8.0 ** 0.25
LN_EPS = 1e-5
NEG = -1e30


class Cfg:
    def __init__(s, D=4, B=2, SEQ=8192, DB=128, n_cores=8):
        s.D, s.B, s.SEQ, s.DB, s.n = D, B, SEQ, DB, n_cores
        s.cps = n_cores // B
        s.NOWN = SEQ // 128 // s.cps
        s.NH = 3
        s.NPB = s.NOWN + s.NH
        s.NSB = DB // n_cores
        s.TS = s.NSB * 4
        s.NA = (D + 1) // 2
        s.NC = D // 2
        s.halo_skip = [(0, 1), (1, 1), (1, 2), (2, 3)] if (D == 4 and s.NH == 3) else [(0, 0)] * D


def build(cfg):
    nc = bass.Bass("TRN2", target_bir_lowering=False)
    stack = ExitStack()
    P = Prog(nc, stack)
    D, NPB, NH, NSB, TS, NA, NC = cfg.D, cfg.NPB, cfg.NH, cfg.NSB, cfg.TS, cfg.NA, cfg.NC
    NOWN = cfg.NOWN
    SU = NPB

    def din(name, shape, dt=F32):
        return P.dram(name, shape, dt, "ExternalInput")

    def dout(name, shape, dt=F32):
        return P.dram(name, shape, dt, "ExternalOutput")

    xp_d = din("xp", [NPB * 128, 1024])
    xs_d = din("xs", [TS, 1024])
    maskA_d = din("maskA", [128, 256])
    maskF_d = din("maskF", [128, 256])
    cflag_d = din("cflag", [128, 1])
    wqkv_d = din("wqkv", [NA, 1024, 2048])
    wo_d = din("wo", [NA, 1024, 1024])
    sinks_d = din("sinks", [1, NA * 16])
    cwkT_d = din("cwkT", [NA, 128, NSB, 4, 128])
    cwk_d = din("cwk", [NA, NSB, 128, 256])
    cwv_d = din("cwv", [NA, NSB, 128, 256])
    win_d = din("win", [max(NC, 1), 1024, 3072])
    cw_d = din("cw", [128, max(NC, 1) * 8 * 3])
    wout_d = din("wout", [max(NC, 1), 1024, 1024])
    stT_d = din("stT", [max(NC, 1), 128, 8 * NSB * 2])
    mwq_d = din("mwq", [D, 1024, 1024])
    mwkv_d = din("mwkv", [D, 1024, 2048])
    mwo_d = din("mwo", [D, 1024, 1024])
    memT_d = din("memT", [128, 8 * 256])
    cmkT_d = din("cmkT", [D, NSB, 128, 8 * 256])
    cmv_d = din("cmv", [D, NSB, 256, 1024])
    pwq_d = din("pwq", [D, 1024, 2048])
    skT_d = din("skT", [D, 2, 128, 128])
    puv_d = din("puv", [D * 16384, 2048])
    puvb_d = [P.dram("puvb%d" % i, [16384, 2048], BF16, "Internal") for i in range(D)]
    lng_d = din("lng", [D, 3, 1024])
    lnb_d = din("lnb", [D, 3, 1024])

    yp_d = dout("yp", [NOWN * 128, 1024])
    ys_d = dout("ys", [TS, 1024])
    wkp_d = dout("wkp", [NA, 128, 256])
    wvp_d = dout("wvp", [NA, 128, 256])
    cvp_d = dout("cvp", [max(NC, 1), 2, 1024])
    mkp_d = dout("mkp", [D, 256, 1024])
    mvp_d = dout("mvp", [D, 256, 1024])
    wks_d = dout("wks", [NA, NSB, 128, 256])
    wvs_d = dout("wvs", [NA, NSB, 128, 256])
    cvs_d = dout("cvs", [max(NC, 1), NSB * 2, 1024])

    X = [P.sbuf("X%d" % u, [128, 1024], F32) for u in range(NPB + 1)]
    identf = P.sbuf("identf", [128, 128], F32)
    identb = P.sbuf("identb", [128, 128], BF16)
    maskA = P.sbuf("maskA_s", [128, 256], F32)
    maskF = P.sbuf("maskF_s", [128, 256], F32)
    cflag = P.sbuf("cflag_s", [128, 1], F32)
    sinkb = P.sbuf("sinkb", [128, NA * 16], F32)
    CW = P.sbuf("CW", [128, max(NC, 1) * 8 * 3], F32)
    iota16 = P.sbuf("iota16", [128, 16], F32)
    GB = P.sbuf("GB", [128, 1024], F32)
    BB = P.sbuf("BB", [128, 1024], F32)
    XB = P.sbuf("XB", [128, 1024], BF16)
    OB = P.sbuf("OB", [128, 1024], BF16)
    XT = P.sbuf("XT", [128, 8, 128], BF16)
    OT = P.sbuf("OT", [128, 8, 128], BF16)
    Y = P.sbuf("Y", [128, 1024], F32)
    ST = P.sbuf("ST", [128, 12], F32)
    MV2 = P.sbuf("MV2", [128, 2], F32)
    SD = P.sbuf("SD", [128, 1], F32)
    RSTD = P.sbuf("RSTD", [128, 1], F32)
    NMR = P.sbuf("NMR", [128, 1], F32)
    DUM = P.sbuf("DUM", [128, 4], F32)
    tmpi = P.sbuf("tmpi", [128, 128], I32)
    tmpf = P.sbuf("tmpf", [128, 128], F32)
    tmpr = P.sbuf("tmpr", [128, 1], F32)

    ARENA_BYTES = 100 * 1024
    ARENA = P.sbuf("ARENA", [128, ARENA_BYTES // 4], F32)
    arena_views = {F32: ARENA[:], BF16: ARENA[:].bitcast(BF16), I32: ARENA[:].bitcast(I32), U32: ARENA[:].bitcast(U32)}
    esz = {F32: 4, BF16: 2, I32: 4, U32: 4}
    arena_cache = {}
    arena_off = [0]

    class AV:
        def __init__(self, buf, ap):
            self.buf = buf
            self.ap = ap

        def __getitem__(self, key):
            return self.ap[key]

    def aalloc(phase, name, shape, dt):
        n = int(np.prod(shape[1:]))
        nb = (n * esz[dt] + 63) // 64 * 64
        off = arena_off[0]
        arena_off[0] += nb
        assert arena_off[0] <= ARENA_BYTES, (phase, name, arena_off[0])
        key = (phase, name)
        if key in arena_cache:
            assert arena_cache[key][1] == off
            return arena_cache[key][0]
        e0 = off // esz[dt]
        ap = arena_views[dt][:, e0:e0 + n]
        if len(shape) == 3:
            ap = ap.rearrange("p (a b) -> p a b", a=shape[1])
        elif len(shape) == 4:
            ap = ap.rearrange("p (a b c) -> p a b c", a=shape[1], b=shape[2])
        buf = P.alias(phase + "_" + name, ap, "sb")
        av = AV(buf, ap)
        arena_cache[key] = (av, off)
        return av

    def barrier():
        bufs = [v[0].buf for v in arena_cache.values()]
        P.op("dve", lambda e: e.memset(DUM[:, 0:1], 0.0), reads=[], writes=bufs + [DUM])

    Fp = P.psum("Fp", [128, 1024], F32)
    Sp = P.psum("Sp", [128, 2048], F32)
    Ap = P.psum("Ap", [128, 512], F32)
    Bp = P.psum("Bp", [128, 512], F32)
    Sb = [P.alias("Sb%d" % i, Sp.t[:, i * 512:(i + 1) * 512], "ps") for i in range(4)]
    A_bf = Ap[:].bitcast(BF16).rearrange("p (k t) -> p k t", k=8)
    pj_banks = [(Bp, Bp.t), (Sb[3], Sp.t[:, 1536:2048])]
    pj_i = [0]

    def next_pj():
        b = pj_banks[pj_i[0] % 2]
        pj_i[0] += 1
        return b

    def mm(out, lhsT, rhs, start, stop, reads, writes):
        P.op("pe", lambda e: e.matmul(out, lhsT=lhsT, rhs=rhs, start=start, stop=stop), reads=reads, writes=writes)

    def tr(out, in_, ident, reads, writes):
        P.op("pe", lambda e: e.transpose(out=out, in_=in_, identity=ident), reads=reads, writes=writes)

    def dve(f, reads, writes):
        P.op("dve", f, reads=reads, writes=writes)

    def act(f, reads, writes):
        P.op("act", f, reads=reads, writes=writes)

    def pool(f, reads, writes):
        P.op("pool", f, reads=reads, writes=writes)

    def ld(out, in_, reads, writes, eng="sp", **kw):
        P.dma(eng, lambda e: e.dma_start(out=out, in_=in_, **kw), reads=reads, writes=writes)

    def st(out, in_, reads, dbuf, eng="sp", **kw):
        P.dma(eng, lambda e: e.dma_start(out=out, in_=in_, **kw), reads=reads, writes=[dbuf], sem_buf=dbuf)


    def TT(out, in0, in1, op, reads, writes, eng="dve"):
        P.op(eng, lambda e: e.tensor_tensor(out=out, in0=in0, in1=in1, op=op), reads=reads, writes=writes)

    def TSC(out, in0, s1, s2, op0, op1, reads, writes, eng="dve"):
        if op1 is None:
            P.op(eng, lambda e: e.tensor_scalar(out=out, in0=in0, scalar1=s1, scalar2=None, op0=op0), reads=reads, writes=writes)
        else:
            P.op(eng, lambda e: e.tensor_scalar(out=out, in0=in0, scalar1=s1, scalar2=s2, op0=op0, op1=op1), reads=reads, writes=writes)

    def STT(out, in0, scalar, in1, op0, op1, reads, writes):
        P.op("dve", lambda e: e.scalar_tensor_tensor(out=out, in0=in0, scalar=scalar, in1=in1, op0=op0, op1=op1),
             reads=reads, writes=writes)

    def TCOPY(out, in_, reads, writes, eng="dve"):
        P.op(eng, lambda e: e.tensor_copy(out=out, in_=in_), reads=reads, writes=writes)

    def TRED(out, in_, op, reads, writes):
        P.op("dve", lambda e: e.tensor_reduce(out=out, in_=in_, axis=AX.X, op=op), reads=reads, writes=writes)

    def ACTF(out, in_, func, reads, writes):
        P.op("act", lambda e: e.activation(out=out, in_=in_, func=func), reads=reads, writes=writes)

    def ACOPY(out, in_, reads, writes):
        P.op("act", lambda e: e.copy(out=out, in_=in_), reads=reads, writes=writes)

    def AMUL(out, in_, mul, reads, writes):
        P.op("act", lambda e: e.mul(out=out, in_=in_, mul=mul), reads=reads, writes=writes)

    def RECIP(out, in_, reads, writes):
        P.op("dve", lambda e: e.reciprocal(out=out, in_=in_), reads=reads, writes=writes)

    def MEMSET(out, val, writes, eng="pool"):
        P.op(eng, lambda e: e.memset(out, val), reads=[], writes=writes)

    def MAX8(out, in_, reads, writes):
        P.op("dve", lambda e: e.max(out=out, in_=in_), reads=reads, writes=writes)

    def MAXIDX(out, in_max, in_values, reads, writes):
        P.op("dve", lambda e: e.max_index(out=out, in_max=in_max, in_values=in_values), reads=reads, writes=writes)

    def MATCHREP(out, in_to_replace, in_values, reads, writes):
        P.op("dve", lambda e: e.match_replace(out=out, in_to_replace=in_to_replace, in_values=in_values, imm_value=NEG),
             reads=reads, writes=writes)

    def TSS(out, in_, scalar, op, reads, writes):
        P.op("dve", lambda e: e.tensor_single_scalar(out=out, in_=in_, scalar=scalar, op=op), reads=reads, writes=writes)

    def GATHER(slot_ap, table_ap, idx_ap, reads, writes):
        P.dma("pool", lambda e: e.indirect_dma_start(out=slot_ap, out_offset=None, in_=table_ap,
                                                     in_offset=bass.IndirectOffsetOnAxis(ap=idx_ap, axis=0)),
              reads=reads, writes=writes)

    def TTR(out, in0, in1, accum_out, reads, writes):
        P.op("dve", lambda e: e.scalar_tensor_tensor(out=out, in0=in0, scalar=1.0, in1=in1, op0=ALU.mult, op1=ALU.mult,
                                                     accum_out=accum_out), reads=reads, writes=writes)

    def load_w(dst_av, nk, ncols, src_ap, src_buf):
        for k in range(nk):
            ld(dst_av[:, k, 0:ncols], src_ap[k * 128:(k + 1) * 128, :], [src_buf], [dst_av.buf], eng="pool")

    def precast(i):
        NP = 16
        R = 16384 // NP
        for q in range(NP):
            ld(puvb_d[i][q * R:(q + 1) * R, :], puv_d[i * 16384 + q * R: i * 16384 + (q + 1) * R, :],
               [puv_d], [puvb_d[i]], eng="pool")

    pool(lambda e: e.iota(tmpi[:], pattern=[[1, 128]], base=0, channel_multiplier=0), [], [tmpi])
    dve(lambda e: e.tensor_copy(out=tmpf[:], in_=tmpi[:]), [tmpi], [tmpf])
    dve(lambda e: e.tensor_copy(out=iota16[:], in_=tmpi[:, 0:16]), [tmpi], [iota16])
    pool(lambda e: e.iota(tmpi[:, 0:1], pattern=[[1, 1]], base=0, channel_multiplier=1), [tmpf, iota16], [tmpi])
    dve(lambda e: e.tensor_copy(out=tmpr[:], in_=tmpi[:, 0:1]), [tmpi], [tmpr])
    dve(lambda e: e.tensor_scalar(out=identf[:], in0=tmpf[:], scalar1=tmpr[:, 0:1], scalar2=None, op0=ALU.is_equal),
        [tmpf, tmpr], [identf])
    dve(lambda e: e.tensor_copy(out=identb[:], in_=identf[:]), [identf], [identb])
    ld(maskA[:], maskA_d[:], [maskA_d], [maskA])
    ld(maskF[:], maskF_d[:], [maskF_d], [maskF])
    ld(cflag[:], cflag_d[:], [cflag_d], [cflag])
    ld(sinkb[:], sinks_d[0:1, :].partition_broadcast(128), [sinks_d], [sinkb])
    ld(CW[:], cw_d[:], [cw_d], [CW])
    for u in range(NPB):
        ld(X[u][:], xp_d[u * 128:(u + 1) * 128, :], [xp_d], [X[u]])
    ld(X[SU][0:TS, :], xs_d[:], [xs_d], [X[SU]])

    all_units = [(u, 128) for u in range(NPB)] + [(SU, TS)]

    def units_from(skip):
        return [(u, T) for (u, T) in all_units if u >= skip or u == SU]

    def transpose8(src, dst, T):
        for k in range(8):
            tr(A_bf[:, k, 0:T], src[0:T, k * 128:(k + 1) * 128], identb[0:T, 0:T], [src, identb], [Ap])
        TCOPY(dst[:, :, 0:T], A_bf[:, :, 0:T], [Ap], [dst])

    def make_xT(u, T):
        ACOPY(XB[0:T, :], X[u][0:T, :], [X[u]], [XB])
        transpose8(XB, XT, T)

    def proj_fm(W, col0, nch, T, dst, scale, src=XT):
        for c0 in range(0, nch, 4):
            n = min(4, nch - c0)
            pb, pt = next_pj()
            pv = pt.rearrange("p (c t) -> p c t", c=4)
            for c in range(n):
                for k in range(8):
                    mm(pv[:, c, 0:T], W[:, k, col0 + (c0 + c) * 128: col0 + (c0 + c + 1) * 128], src[:, k, 0:T],
                       k == 0, k == 7, [W.buf, src], [pb])
            AMUL(dst[:, c0:c0 + n, 0:T], pv[:, 0:n, 0:T], scale, [pb], [dst.buf])

    def proj_F(W, T, srcT, srcbuf):
        for n in range(2):
            for k in range(8):
                mm(Fp[0:T, n * 512:(n + 1) * 512], srcT[:, k, 0:T], W[:, k, n * 512:(n + 1) * 512], k == 0, k == 7,
                   [srcbuf, W.buf], [Fp])

    def load_ln(i, j):
        ld(GB[:], lng_d[i, j:j + 1, :].partition_broadcast(128), [lng_d], [GB])
        ld(BB[:], lnb_d[i, j:j + 1, :].partition_broadcast(128), [lnb_d], [BB])

    def resid_ln(u, T, f_ap, f_bufs):
        x = X[u]
        STT(Y[0:T, :], x[0:T, :], ALPHA, f_ap, ALU.mult, ALU.add, [x] + f_bufs, [Y])
        P.op("dve", lambda e: e.bn_stats(out=ST[0:T, 0:6], in_=Y[0:T, 0:512]), reads=[Y], writes=[ST])
        P.op("dve", lambda e: e.bn_stats(out=ST[0:T, 6:12], in_=Y[0:T, 512:1024]), reads=[Y], writes=[ST])
        P.op("dve", lambda e: e.bn_aggr(out=MV2[0:T, :], in_=ST[0:T, :]), reads=[ST], writes=[MV2])
        TSC(SD[0:T, :], MV2[0:T, 1:2], LN_EPS, None, ALU.add, None, [MV2], [SD])
        P.op("act", lambda e: e.sqrt(out=SD[0:T, :], in_=SD[0:T, :]), reads=[SD], writes=[SD])
        RECIP(RSTD[0:T, :], SD[0:T, :], [SD], [RSTD])
        STT(NMR[0:T, :], MV2[0:T, 0:1], -1.0, RSTD[0:T, :], ALU.mult, ALU.mult, [MV2, RSTD], [NMR])
        TSC(x[0:T, :], Y[0:T, :], RSTD[0:T, 0:1], NMR[0:T, 0:1], ALU.mult, ALU.add, [Y, RSTD, NMR], [x])
        TT(x[0:T, :], x[0:T, :], GB[0:T, :], ALU.mult, [x, GB], [x], eng="pool")
        TT(x[0:T, :], x[0:T, :], BB[0:T, :], ALU.add, [x, BB], [x], eng="pool")

    def state_out(uc_out_ap, cols_ap, cols_bufs, n, UC, USB, dst_ap, dst_buf):
        TCOPY(uc_out_ap, cols_ap, cols_bufs, [UC.buf], eng="pool")
        for k in range(8):
            tr(Fp[0:n, k * 128:(k + 1) * 128], UC[:, k, 0:n], identf[:, :], [UC.buf, identf], [Fp])
        ACOPY(USB[0:n, :], Fp[0:n, :], [Fp], [USB.buf])
        st(dst_ap, USB[0:n, :], [USB.buf], dst_buf)

    def attn_group(T, j, kvh, QT, qc0, ktp, ktp_buf, kto, kto_buf, vp, vp_buf, vo, vo_buf, mask, SS, PBt, PTS, sm,
                   Odst, Odst_buf):
        NK = 256
        S3 = Sp.t[0:T, 0:1024].rearrange("p (g s) -> p g s", g=4)
        for g in range(4):
            h = kvh * 4 + g
            ch, hf = h // 2, h % 2
            s = 2 * (g % 2) + g // 2
            ps = slice(hf * 64, hf * 64 + 64)
            sb = Sb[s // 2]
            mm(S3[:, s, 0:128], QT[ps, ch, qc0:qc0 + T], ktp(ps), True, True, [QT.buf, ktp_buf], [sb])
            mm(S3[:, s, 128:NK], QT[ps, ch, qc0:qc0 + T], kto(ps), True, True, [QT.buf, kto_buf], [sb])
        mx, den, es = sm[0:T, 0:4], sm[0:T, 4:8], sm[0:T, 8:12]
        mx3 = mx.rearrange("p (a b) -> p a b", a=2)
        es3 = es.rearrange("p (a b) -> p a b", a=2)
        b0 = j * 16 + kvh * 4
        sk3 = sinkb[0:T, b0:b0 + 4].rearrange("p (hi lo) -> p lo hi", hi=2)
        TT(SS[0:T, :, 0:NK], S3[:, :, 0:NK], mask[0:T, 0:NK].unsqueeze(1).to_broadcast([T, 4, NK]), ALU.add,
           [Sb[0], Sb[1], mask], [SS.buf])
        TRED(mx, SS[0:T, :, 0:NK], ALU.max, [SS.buf], [sm.buf])
        TT(mx3, mx3, sk3, ALU.max, [sm.buf, sinkb], [sm.buf])
        TT(SS[0:T, :, 0:NK], SS[0:T, :, 0:NK], mx.unsqueeze(2).to_broadcast([T, 4, NK]), ALU.subtract,
           [SS.buf, sm.buf], [SS.buf])
        ACTF(PBt[0:T, :, 0:NK], SS[0:T, :, 0:NK], AF.Exp, [SS.buf], [PBt.buf])
        TRED(den, PBt[0:T, :, 0:NK], ALU.add, [PBt.buf], [sm.buf])
        TT(es3, sk3, mx3, ALU.subtract, [sm.buf, sinkb], [sm.buf])
        ACTF(es, es, AF.Exp, [sm.buf], [sm.buf])
        TT(den, den, es, ALU.add, [sm.buf], [sm.buf])
        RECIP(den, den, [sm.buf], [sm.buf])
        for s in range(4):
            tr(A_bf[:, s * 2, 0:T], PBt[0:T, s, 0:128], identb[0:T, 0:T], [PBt.buf, identb], [Ap])
            tr(A_bf[:, s * 2 + 1, 0:T], PBt[0:T, s, 128:NK], identb[0:T, 0:T], [PBt.buf, identb], [Ap])
        ACOPY(PTS[:, :, 0:T], A_bf[:, :, 0:T], [Ap], [PTS.buf])
        O3 = Sp.t[0:T, 1024:1280].rearrange("p (g d) -> p g d", g=4)
        for s in range(4):
            mm(O3[:, s, :], PTS[:, s * 2, 0:T], vp, True, False, [PTS.buf, vp_buf], [Sb[2]])
            mm(O3[:, s, :], PTS[:, s * 2 + 1, 0:T], vo, False, True, [PTS.buf, vo_buf], [Sb[2]])
        TT(Odst.rearrange("p (a b) d -> p a b d", a=2), O3.rearrange("p (a b) d -> p b a d", a=2),
           den.rearrange("p (a b) -> p b a", a=2).unsqueeze(3).to_broadcast([T, 2, 2, 64]), ALU.mult,
           [Sb[2], sm.buf], [Odst_buf])

    def attn_phase(i, skip):
        j = i // 2
        units = units_from(skip)
        ph = "attn"
        arena_off[0] = 0
        W = aalloc(ph, "W", [128, 8, 2048], BF16)
        WO = aalloc(ph, "WO", [128, 8, 1024], BF16)
        QT = aalloc(ph, "QT", [128, 8, 128], BF16)
        KT = [aalloc(ph, "KT%d" % t, [128, 4, 128], BF16) for t in range(2)]
        VB = [aalloc(ph, "VB%d" % t, [128, 256], BF16) for t in range(2)]
        KVTOK = aalloc(ph, "KVTOK", [128, 512], F32)
        SS = aalloc(ph, "SS", [128, 4, 256], F32)
        PBt = aalloc(ph, "PB", [128, 4, 256], BF16)
        PTS = aalloc(ph, "PTS", [128, 8, 128], BF16)
        sm = aalloc(ph, "sm", [128, 16], F32)
        KTC = [aalloc(ph, "KTC%d" % t, [128, 4, 128], BF16) for t in range(2)]
        VC = [aalloc(ph, "VC%d" % t, [128, 256], BF16) for t in range(2)]
        VOWN = aalloc(ph, "VOWN", [128, 256], BF16)
        KTO = aalloc(ph, "KTO", [128, 4, 128], BF16)
        OB4 = [aalloc(ph, "OB4%d" % t, [128, 1024], BF16) for t in range(2)]
        barrier()
        load_w(W, 8, 2048, wqkv_d[j], wqkv_d)
        load_w(WO, 8, 1024, wo_d[j], wo_d)
        load_ln(i, 0)
        if i == 0:
            precast(0)
        for t in range(2):
            MEMSET(KT[t][:], 0.0, [KT[t].buf])
            MEMSET(VB[t][:], 0.0, [VB[t].buf])
        for (u, T) in units:
            samp = (u == SU)
            cur = u % 2
            prv = 1 - cur
            make_xT(u, T)
            proj_fm(W, 0, 8, T, QT, 0.125)
            proj_fm(W, 1024, 4, T, KT[cur], 1.0)
            pb, pt = next_pj()
            for k in range(8):
                mm(pt[0:T, 0:512], XT[:, k, 0:T], W[:, k, 1536:2048], k == 0, k == 7, [XT, W.buf], [pb])
            ACOPY(KVTOK[0:T, :], pt[0:T, 0:512], [pb], [KVTOK.buf])
            if not samp:
                TCOPY(VB[cur][0:T, :], KVTOK[0:T, 256:512], [KVTOK.buf], [VB[cur].buf])
                mask = maskF if u == NH else maskA
                for kvh in range(4):
                    attn_group(T, j, kvh, QT, 0,
                               lambda ps, kvh=kvh, prv=prv: KT[prv][ps, kvh, :], KT[prv].buf,
                               lambda ps, kvh=kvh, cur=cur: KT[cur][ps, kvh, :], KT[cur].buf,
                               VB[prv][:, kvh * 64:(kvh + 1) * 64], VB[prv].buf,
                               VB[cur][:, kvh * 64:(kvh + 1) * 64], VB[cur].buf,
                               mask, SS, PBt, PTS, sm,
                               OB[0:T, kvh * 256:(kvh + 1) * 256].rearrange("p (g d) -> p g d", g=4), OB)
                if u == NPB - 1:
                    st(wkp_d[j], KVTOK[:, 0:256], [KVTOK.buf], wkp_d)
                    st(wvp_d[j], KVTOK[:, 256:512], [KVTOK.buf], wvp_d)
            else:
                st(wks_d[j][:, 0:124, :], cwk_d[j][:, 4:128, :], [cwk_d], wks_d)
                st(wvs_d[j][:, 0:124, :], cwv_d[j][:, 4:128, :], [cwv_d], wvs_d)
                for b in range(NSB):
                    st(wks_d[j, b, 124:128, :], KVTOK[b * 4:(b + 1) * 4, 0:256], [KVTOK.buf], wks_d)
                    st(wvs_d[j, b, 124:128, :], KVTOK[b * 4:(b + 1) * 4, 256:512], [KVTOK.buf], wvs_d)
                MEMSET(KTO[:], 0.0, [KTO.buf])
                MEMSET(VOWN[:], 0.0, [VOWN.buf])
                for b in range(NSB):
                    t2 = b % 2
                    ld(KTC[t2][:], cwkT_d[j, :, b], [cwkT_d], [KTC[t2].buf], eng="pool")
                    ld(VC[t2][:], cwv_d[j, b], [cwv_d], [VC[t2].buf], eng="pool")
                    TCOPY(KTO[:, :, 0:4], KT[cur][:, :, b * 4:(b + 1) * 4], [KT[cur].buf], [KTO.buf], eng="pool")
                    pb, pt = next_pj()
                    for k in range(8):
                        mm(pt[0:4, 0:256], XT[:, k, b * 4:(b + 1) * 4], W[:, k, 1792:2048], k == 0, k == 7,
                           [XT, W.buf], [pb])
                    ACOPY(VOWN[0:4, :], pt[0:4, 0:256], [pb], [VOWN.buf])
                    for kvh in range(4):
                        attn_group(4, j, kvh, QT, b * 4,
                                   lambda ps, kvh=kvh, t2=t2: KTC[t2][ps, kvh, :], KTC[t2].buf,
                                   lambda ps, kvh=kvh: KTO[ps, kvh, :], KTO.buf,
                                   VC[t2][:, kvh * 64:(kvh + 1) * 64], VC[t2].buf,
                                   VOWN[:, kvh * 64:(kvh + 1) * 64], VOWN.buf,
                                   maskA, SS, PBt, PTS, sm,
                                   OB4[t2][0:4, kvh * 256:(kvh + 1) * 256].rearrange("p (g d) -> p g d", g=4),
                                   OB4[t2].buf)
                    ld(OB[b * 4:(b + 1) * 4, :], OB4[t2][0:4, :], [OB4[t2].buf], [OB])
            transpose8(OB, OT, T)
            proj_F(WO, T, OT, OT)
            resid_ln(u, T, Fp[0:T, :], [Fp])

    def conv_phase(i, skip):
        j = i // 2
        units = units_from(skip)
        ph = "conv"
        arena_off[0] = 0
        W = aalloc(ph, "W", [128, 8, 3072], BF16)
        WO = aalloc(ph, "WO", [128, 8, 1024], BF16)
        UP = aalloc(ph, "UP", [128, 8, 1, 130], F32)
        UPS = aalloc(ph, "UPS", [128, 8, NSB, 6], F32)
        STT_ = aalloc(ph, "STT", [128, 8, NSB, 2], F32)
        HS = aalloc(ph, "HS", [128, 128], F32)
        Z = aalloc(ph, "Z", [128, 128], F32)
        GZ = aalloc(ph, "GZ", [128, 8, 128], BF16)
        UC = aalloc(ph, "UC", [128, 8, 32], F32)
        USB = aalloc(ph, "USB", [128, 1024], F32)
        barrier()
        load_w(W, 8, 3072, win_d[j], win_d)
        load_w(WO, 8, 1024, wout_d[j], wout_d)
        load_ln(i, 0)
        MEMSET(UP[:, :, :, 0:2], 0.0, [UP.buf])
        ld(STT_[:].rearrange("p a b c -> p (a b c)"), stT_d[j], [stT_d], [STT_.buf])
        TCOPY(UPS[:, :, :, 0:2], STT_[:], [STT_.buf], [UPS.buf], eng="pool")
        for (u, T) in units:
            samp = (u == SU)
            up = UPS if samp else UP
            nb, L = (NSB, 4) if samp else (1, 128)
            if u == NH:
                TSC(UP[:, :, :, 0:2], UP[:, :, :, 0:2], cflag[:, 0:1], None, ALU.mult, None, [UP.buf, cflag], [UP.buf])
            make_xT(u, T)

            def v3(ap, nb=nb):
                return ap.rearrange("p (b l) -> p b l", b=nb)
            for c in range(8):
                pb, pt = next_pj()
                pv = pt.rearrange("p (c t) -> p c t", c=4)
                for part in range(3):
                    for k in range(8):
                        mm(pv[:, part, 0:T], W[:, k, part * 1024 + c * 128: part * 1024 + (c + 1) * 128], XT[:, k, 0:T],
                           k == 0, k == 7, [W.buf, XT], [pb])
                ACOPY(HS[:, 0:T], pv[:, 2, 0:T], [pb], [HS.buf])
                TT(up[:, c, :, 2:2 + L], v3(pv[:, 1, 0:T]), v3(HS[:, 0:T]), ALU.mult, [pb, HS.buf], [up.buf])
                o = (j * 8 + c) * 3
                TSC(v3(Z[:, 0:T]), up[:, c, :, 0:L], CW[:, o:o + 1], None, ALU.mult, None, [up.buf, CW], [Z.buf])
                STT(v3(Z[:, 0:T]), up[:, c, :, 1:1 + L], CW[:, o + 1:o + 2], v3(Z[:, 0:T]), ALU.mult, ALU.add,
                    [up.buf, CW, Z.buf], [Z.buf])
                STT(v3(Z[:, 0:T]), up[:, c, :, 2:2 + L], CW[:, o + 2:o + 3], v3(Z[:, 0:T]), ALU.mult, ALU.add,
                    [up.buf, CW, Z.buf], [Z.buf])
                TT(GZ[:, c, 0:T], pv[:, 0, 0:T], Z[:, 0:T], ALU.mult, [pb, Z.buf], [GZ.buf])
            proj_F(WO, T, GZ, GZ.buf)
            resid_ln(u, T, Fp[0:T, :], [Fp])
            if samp:
                state_out(UC[:, :, 0:NSB * 2].rearrange("p a (b r) -> p a b r", r=2), UPS[:, :, :, 4:6], [UPS.buf],
                          NSB * 2, UC, USB, cvs_d[j], cvs_d)
            else:
                if u == NPB - 1:
                    state_out(UC[:, :, 0:2], UP[:, :, 0, 128:130], [UP.buf], 2, UC, USB, cvp_d[j], cvp_d)
                TCOPY(UP[:, :, :, 0:2], UP[:, :, :, 128:130], [UP.buf], [UP.buf], eng="pool")

    def mem_unit(T, QT, qc0, MKt, MVt, SS, PBt, PTS, sm, Odst, Odst_buf):
        S3 = Sp.t[0:T, 0:1024].rearrange("p (g s) -> p g s", g=4)
        for h in range(4):
            sb = Sb[h // 2]
            mm(S3[:, h, :], QT[:, 2 * h, qc0:qc0 + T], MKt[:, 2 * h, :], True, False, [QT.buf, MKt.buf], [sb])
            mm(S3[:, h, :], QT[:, 2 * h + 1, qc0:qc0 + T], MKt[:, 2 * h + 1, :], False, True, [QT.buf, MKt.buf], [sb])
        mx, den = sm[0:T, 0:4], sm[0:T, 4:8]
        TRED(mx, S3, ALU.max, [Sb[0], Sb[1]], [sm.buf])
        TT(SS[0:T, :, :], S3, mx.unsqueeze(2).to_broadcast([T, 4, 256]), ALU.subtract, [Sb[0], Sb[1], sm.buf], [SS.buf])
        ACTF(PBt[0:T, :, :], SS[0:T, :, :], AF.Exp, [SS.buf], [PBt.buf])
        TRED(den, PBt[0:T, :, :], ALU.add, [PBt.buf], [sm.buf])
        RECIP(den, den, [sm.buf], [sm.buf])
        for h in range(4):
            for mc in range(2):
                tr(A_bf[:, h * 2 + mc, 0:T], PBt[0:T, h, mc * 128:(mc + 1) * 128], identb[0:T, 0:T], [PBt.buf, identb], [Ap])
        ACOPY(PTS[:, :, 0:T], A_bf[:, :, 0:T], [Ap], [PTS.buf])
        for h in range(4):
            for mc in range(2):
                mm(Fp[0:T, h * 256:(h + 1) * 256], PTS[:, h * 2 + mc, 0:T], MVt[:, mc, h * 256:(h + 1) * 256],
                   mc == 0, mc == 1, [PTS.buf, MVt.buf], [Fp])
        TT(Odst, Fp[0:T, :].rearrange("p (h d) -> p h d", h=4), den.unsqueeze(2).to_broadcast([T, 4, 256]), ALU.mult,
           [Fp, sm.buf], [Odst_buf])

    def mem_phase(i, skip):
        ph = "mem"
        units = units_from(skip)
        arena_off[0] = 0
        MK = aalloc(ph, "MK", [128, 8, 256], BF16)
        MVt = aalloc(ph, "MV", [128, 2, 1024], BF16)
        mark = arena_off[0]
        WKV = aalloc(ph, "WKV", [128, 8, 2048], BF16)
        MEMT = aalloc(ph, "MEMT", [128, 8, 256], BF16)
        Y2 = aalloc(ph, "Y2", [128, 1024], F32)
        barrier()
        load_w(WKV, 8, 2048, mwkv_d[i], mwkv_d)
        ld(MEMT[:].rearrange("p a b -> p (a b)"), memT_d[:], [memT_d], [MEMT.buf], eng="pool")
        for mc in range(2):
            for kv in range(2):
                for n in range(2):
                    for k in range(8):
                        mm(Fp[:, n * 512:(n + 1) * 512], MEMT[:, k, mc * 128:(mc + 1) * 128],
                           WKV[:, k, kv * 1024 + n * 512: kv * 1024 + (n + 1) * 512], k == 0, k == 7,
                           [MEMT.buf, WKV.buf], [Fp])
                if kv == 0:
                    ACOPY(Y[:, :], Fp[:, :], [Fp], [Y])
                    st(mkp_d[i, mc * 128:(mc + 1) * 128, :], Y[:, :], [Y], mkp_d)
                else:
                    ACOPY(Y2[:, :], Fp[:, :], [Fp], [Y2.buf])
                    st(mvp_d[i, mc * 128:(mc + 1) * 128, :], Y2[:, :], [Y2.buf], mvp_d)
                    TCOPY(MVt[:, mc, :], Fp[:, :], [Fp], [MVt.buf])
        for c0 in range(0, 8, 2):
            pb, pt = next_pj()
            pv = pt.rearrange("p (c t) -> p c t", c=2)
            for c in range(2):
                for k in range(8):
                    mm(pv[:, c, :], WKV[:, k, (c0 + c) * 128:(c0 + c + 1) * 128], MEMT[:, k, :], k == 0, k == 7,
                       [WKV.buf, MEMT.buf], [pb])
            ACOPY(MK[:, c0:c0 + 2, :], pv[:, :, :], [pb], [MK.buf])
        arena_off[0] = mark
        WQ = aalloc(ph, "WQ", [128, 8, 1024], BF16)
        WO = aalloc(ph, "WO", [128, 8, 1024], BF16)
        QT = aalloc(ph, "QT", [128, 8, 128], BF16)
        SS = aalloc(ph, "SS", [128, 4, 256], F32)
        PBt = aalloc(ph, "PB", [128, 4, 256], BF16)
        PTS = aalloc(ph, "PTS", [128, 8, 128], BF16)
        sm = aalloc(ph, "sm", [128, 16], F32)
        MKC = [aalloc(ph, "MKC%d" % t, [128, 8, 256], BF16) for t in range(2)]
        MVC = [aalloc(ph, "MVC%d" % t, [128, 2, 1024], BF16) for t in range(2)]
        OB4 = [aalloc(ph, "OB4%d" % t, [128, 1024], BF16) for t in range(2)]
        barrier()
        load_w(WQ, 8, 1024, mwq_d[i], mwq_d)
        load_w(WO, 8, 1024, mwo_d[i], mwo_d)
        load_ln(i, 1)
        for (u, T) in units:
            samp = (u == SU)
            make_xT(u, T)
            proj_fm(WQ, 0, 8, T, QT, 1.0 / 16.0)
            if not samp:
                mem_unit(T, QT, 0, MK, MVt, SS, PBt, PTS, sm, OB[0:T, :].rearrange("p (h d) -> p h d", h=4), OB)
            else:
                for b in range(NSB):
                    t2 = b % 2
                    ld(MKC[t2][:].rearrange("p a b -> p (a b)"), cmkT_d[i, b], [cmkT_d], [MKC[t2].buf], eng="pool")
                    ld(MVC[t2][:], cmv_d[i, b].rearrange("(mc p) c -> p mc c", p=128), [cmv_d], [MVC[t2].buf], eng="pool")
                    mem_unit(4, QT, b * 4, MKC[t2], MVC[t2], SS, PBt, PTS, sm,
                             OB4[t2][0:4, :].rearrange("p (h d) -> p h d", h=4), OB4[t2].buf)
                    ld(OB[b * 4:(b + 1) * 4, :], OB4[t2][0:4, :], [OB4[t2].buf], [OB])
            transpose8(OB, OT, T)
            proj_F(WO, T, OT, OT)
            resid_ln(u, T, Fp[0:T, :], [Fp])

    def peer_phase(i, last, skip):
        ph = "peer"
        units = units_from(skip)
        arena_off[0] = 0
        W = aalloc(ph, "W", [128, 8, 2048], BF16)
        SK = aalloc(ph, "SK", [128, 2, 128], BF16)
        QT = aalloc(ph, "QT", [128, 16, 128], BF16)
        SV = aalloc(ph, "SV", [128, 16, 16], F32)
        SI = aalloc(ph, "SI", [128, 16, 16], U32)
        SIF = aalloc(ph, "SIF", [128, 16, 16], F32)
        SW = aalloc(ph, "SW", [128, 256], F32)
        COMB = aalloc(ph, "COMB", [128, 8, 256], F32)
        EQ = aalloc(ph, "EQ", [128, 8, 256], F32)
        CS = aalloc(ph, "CS", [128, 8, 16], F32)
        CI = aalloc(ph, "CI", [128, 8, 16], U32)
        CA = aalloc(ph, "CA", [128, 8, 16], U32)
        CAF = aalloc(ph, "CAF", [128, 8, 16], F32)
        GE = aalloc(ph, "GE", [128, 8, 16], F32)
        I1 = aalloc(ph, "I1", [128, 8, 16], F32)
        I2 = aalloc(ph, "I2", [128, 8, 16], F32)
        EIDI = aalloc(ph, "EIDI", [128, 128], I32)
        AAp = [aalloc(ph, "AA%d" % t, [128, 2], F32) for t in range(2)]
        WWp = [aalloc(ph, "WW%d" % t, [128, 2], F32) for t in range(2)]
        sm = aalloc(ph, "sm", [128, 16], F32)
        XBD = [aalloc(ph, "XBD%d" % t, [128, 1024], BF16) for t in range(2)]
        NDG = 4
        DG = [aalloc(ph, "DG%d" % t, [128, 128], BF16) for t in range(NDG)]
        JUNK = aalloc(ph, "JUNK", [128, 1024], BF16)
        NG = 8
        RING = [aalloc(ph, "RING%d" % t, [128, 2048], BF16) for t in range(NG)]
        barrier()
        load_w(W, 8, 2048, pwq_d[i], pwq_d)
        for c in range(2):
            ld(SK[:, c, :], skT_d[i, c], [skT_d], [SK.buf], eng="pool")
        load_ln(i, 2)
        if i + 1 < D:
            precast(i + 1)
        MEMSET(EIDI[:], 0, [EIDI.buf])
        gi = [0]
        for un, (u, T) in enumerate(units):
            x = X[u]
            xbd = XBD[un % 2]
            make_xT(u, T)
            ACOPY(xbd[0:T, :], x[0:T, :], [x], [xbd.buf])
            proj_fm(W, 0, 16, T, QT, 1.0)
            S3 = Sp.t[0:T, :].rearrange("p (c n) -> p c n", c=16)
            for c in range(16):
                mm(S3[:, c, :], QT[:, c, 0:T], SK[:, c % 2, :], True, True, [QT.buf, SK.buf], [Sb[c // 4]])
            for c in range(16):
                sb = Sb[c // 4]
                src = S3[:, c, :]
                MAX8(SV[0:T, c, 0:8], src, [sb], [SV.buf])
                MAXIDX(SI[0:T, c, 0:8], SV[0:T, c, 0:8], src, [sb, SV.buf], [SI.buf])
                MATCHREP(SW[0:T, 0:128], SV[0:T, c, 0:8], src, [sb, SV.buf], [SW.buf])
                MAX8(SV[0:T, c, 8:16], SW[0:T, 0:128], [SW.buf], [SV.buf])
                MAXIDX(SI[0:T, c, 8:16], SV[0:T, c, 8:16], SW[0:T, 0:128], [SW.buf, SV.buf], [SI.buf])
            SV4 = SV[0:T].rearrange("p (h c) k -> p h c k", c=2)
            SIF4 = SIF[0:T].rearrange("p (h c) k -> p h c k", c=2)
            C4 = COMB[0:T].rearrange("p h (a b) -> p h a b", a=16)
            E4 = EQ[0:T].rearrange("p h (a b) -> p h a b", a=16)
            TCOPY(SIF[0:T], SI[0:T], [SI.buf], [SIF.buf])
            TT(C4, SV4[:, :, 0, :].unsqueeze(3).to_broadcast([T, 8, 16, 16]),
               SV4[:, :, 1, :].unsqueeze(2).to_broadcast([T, 8, 16, 16]), ALU.add, [SV.buf], [COMB.buf])
            for h in range(8):
                src = COMB[0:T, h, :]
                MAX8(CS[0:T, h, 0:8], src, [COMB.buf], [CS.buf])
                MAXIDX(CI[0:T, h, 0:8], CS[0:T, h, 0:8], src, [COMB.buf, CS.buf], [CI.buf])
                MATCHREP(SW[0:T, :], CS[0:T, h, 0:8], src, [COMB.buf, CS.buf], [SW.buf])
                MAX8(CS[0:T, h, 8:16], SW[0:T, :], [SW.buf], [CS.buf])
                MAXIDX(CI[0:T, h, 8:16], CS[0:T, h, 8:16], SW[0:T, :], [SW.buf, CS.buf], [CI.buf])
            TT(GE[0:T], CS[0:T], CS[0:T, :, 0:1].to_broadcast([T, 8, 16]), ALU.subtract, [CS.buf], [GE.buf])
            ACTF(GE[0:T], GE[0:T], AF.Exp, [GE.buf], [GE.buf])
            TRED(sm[0:T, 0:8], GE[0:T], ALU.add, [GE.buf], [sm.buf])
            RECIP(sm[0:T, 0:8], sm[0:T, 0:8], [sm.buf], [sm.buf])
            TT(GE[0:T], GE[0:T], sm[0:T, 0:8].unsqueeze(2).to_broadcast([T, 8, 16]), ALU.mult, [GE.buf, sm.buf], [GE.buf])
            for which, Idst in ((0, I1), (1, I2)):
                if which == 0:
                    TSS(CA[0:T], CI[0:T], 4, ALU.logical_shift_right, [CI.buf], [CA.buf])
                else:
                    TSS(CA[0:T], CI[0:T], 15, ALU.bitwise_and, [CI.buf], [CA.buf])
                TCOPY(CAF[0:T], CA[0:T], [CA.buf], [CAF.buf])
                TT(E4, iota16[0:T, :].unsqueeze(1).unsqueeze(1).to_broadcast([T, 8, 16, 16]),
                   CAF[0:T].unsqueeze(3).to_broadcast([T, 8, 16, 16]), ALU.is_equal, [iota16, CAF.buf], [EQ.buf])
                TT(E4, E4, SIF4[:, :, which, :].unsqueeze(2).to_broadcast([T, 8, 16, 16]), ALU.mult,
                   [EQ.buf, SIF.buf], [EQ.buf])
                TRED(Idst[0:T], E4, ALU.add, [EQ.buf], [Idst.buf])
            STT(I1[0:T], I1[0:T], 128.0, I2[0:T], ALU.mult, ALU.add, [I1.buf, I2.buf], [I1.buf])
            TCOPY(EIDI[0:T, :], I1[0:T].rearrange("p h k -> p (h k)"), [I1.buf], [EIDI.buf])
            GEF = GE[0:T].rearrange("p h k -> p (h k)")
            NB = 64

            def dots(b):
                for q in range(2):
                    jj = 2 * b + q
                    slot = RING[jj % NG]
                    GATHER(slot[:, :], puvb_d[i][:, :], EIDI[:, jj:jj + 1], [EIDI.buf, puvb_d[i]], [slot.buf])
                    TTR(JUNK[0:T, :], xbd[0:T, :], slot[0:T, 0:1024], AAp[b % 2][0:T, q:q + 1], [xbd.buf, slot.buf],
                        [JUNK.buf, AAp[b % 2].buf])
                ACTF(WWp[b % 2][0:T, :], AAp[b % 2][0:T, :], AF.Gelu, [AAp[b % 2].buf], [WWp[b % 2].buf])

            def axpys(b):
                ww = WWp[b % 2]
                TT(ww[0:T, :], ww[0:T, :], GEF[:, 2 * b:2 * b + 2], ALU.mult, [ww.buf, GE.buf], [ww.buf])
                for q in range(2):
                    jj = 2 * b + q
                    slot = RING[jj % NG]
                    dg = DG[jj % NDG]
                    P.op("act", lambda e, dg=dg, ww=ww, q=q, T=T: e.activation(out=dg[0:T, 0:T], in_=identb[0:T, 0:T],
                                                                                 func=AF.Copy, scale=ww[0:T, q:q + 1]),
                         reads=[identb, ww.buf], writes=[dg.buf])
                    for n in range(2):
                        mm(Fp[0:T, n * 512:(n + 1) * 512], dg[0:T, 0:T], slot[0:T, 1024 + n * 512:1024 + (n + 1) * 512],
                           jj == 0, jj == 2 * NB - 1, [dg.buf, slot.buf], [Fp])

            dots(0)
            for b in range(NB):
                if b + 1 < NB:
                    dots(b + 1)
                axpys(b)
            resid_ln(u, T, Fp[0:T, :], [Fp])
            if last:
                if u == SU:
                    st(ys_d[:, :], x[0:TS, :], [x], ys_d)
                elif u >= NH:
                    st(yp_d[(u - NH) * 128:(u - NH + 1) * 128, :], x[:, :], [x], yp_d)

    HS = cfg.halo_skip
    for i in range(D):
        s_mix, s_rest = HS[i] if i < len(HS) else (0, 0)
        if i % 2 == 0:
            attn_phase(i, s_mix)
        else:
            conv_phase(i, s_mix)
        mem_phase(i, s_rest)
        peer_phase(i, i == D - 1, s_rest)
    P.emit()
    return nc, stack, P


def prep_inputs(cfg, inp):
    D, B, SEQ, DB, n = cfg.D, cfg.B, cfg.SEQ, cfg.DB, cfg.n
    NA, NC, NSB, NH, NOWN, cps = cfg.NA, cfg.NC, cfg.NSB, cfg.NH, cfg.NOWN, cfg.cps
    f = lambda a: np.ascontiguousarray(np.asarray(a, dtype=np.float32))
    g = {k: np.asarray(v) for k, v in inp.items()}
    wqkv_src = g["attn_w_qkv"]
    q = wqkv_src[:, :, 0:1024]
    kk = wqkv_src[:, :, 1024:1280]
    vv = wqkv_src[:, :, 1280:1536]
    kdup = np.concatenate([np.concatenate([kk[:, :, h * 64:(h + 1) * 64]] * 2, axis=2) for h in range(4)], axis=2)
    wqkv = f(np.concatenate([q, kdup, kk, vv], axis=2))
    ii = np.arange(128)
    maskA = np.full((128, 256), NEG, np.float32)
    maskA[:, 0:128][ii[None, :] >= ii[:, None]] = 0.0
    maskA[:, 128:256][ii[None, :] <= ii[:, None]] = 0.0
    maskN = maskA.copy()
    maskN[:, 0:128] = NEG
    ncv = max(NC, 1)
    if NC > 0:
        cw = g["conv_w"].reshape(NC, 3, 8, 128)
        cw = f(np.transpose(cw, (3, 0, 2, 1)).reshape(128, NC * 8 * 3))
        win = f(g["conv_w_in"])
        wout = f(g["conv_w_out"])
    else:
        cw = np.zeros((128, 24), np.float32)
        win = np.zeros((1, 1024, 3072), np.float32)
        wout = np.zeros((1, 1024, 1024), np.float32)
    skT = f(np.transpose(g["peer_sub_keys"], (0, 1, 3, 2)))
    shared = {
        "maskA": maskA,
        "wqkv": wqkv, "wo": f(g["attn_w_o"]), "sinks": f(g["attn_sinks"]).reshape(1, NA * 16),
        "win": win, "cw": cw, "wout": wout,
        "mwq": f(g["mem_w_q"]), "mwkv": f(g["mem_w_kv"]), "mwo": f(g["mem_w_o"]),
        "pwq": f(g["peer_w_q"]), "skT": skT,
        "puv": f(np.concatenate([g["peer_u"], g["peer_v"]], axis=-1)).reshape(D * 16384, 2048),
        "lng": f(g["ln_g"]), "lnb": f(g["ln_b"]),
    }
    maps = []
    for c in range(n):
        b, qd = c // cps, c % cps
        t0 = qd * NOWN * 128
        xp = np.zeros(((NOWN + NH) * 128, 1024), np.float32)
        lo = t0 - NH * 128
        lo_c = max(lo, 0)
        xp[lo_c - lo:] = g["x_prompt"][b, lo_c:t0 + NOWN * 128]
        first = (qd == 0)
        sb = slice(c * NSB, (c + 1) * NSB)
        cwk = g["cache_win_k"][:, sb].reshape(NA, NSB, 128, 256)
        cwv = g["cache_win_v"][:, sb].reshape(NA, NSB, 128, 256)
        kT = np.transpose(g["cache_win_k"][:, sb], (0, 4, 1, 3, 2))
        cwkT = np.concatenate([kT, kT], axis=1)
        if NC > 0:
            stt = g["state_conv"][:, sb].reshape(NC, NSB, 2, 8, 128)
            stT = np.transpose(stt, (0, 4, 3, 1, 2)).reshape(NC, 128, 8 * NSB * 2)
        else:
            stT = np.zeros((1, 128, 8 * NSB * 2), np.float32)
        memT = np.transpose(g["mem_prompt"][b].reshape(256, 8, 128), (2, 1, 0)).reshape(128, 8 * 256)
        cmk = g["cache_mem_k"][:, sb].reshape(D, NSB, 256, 4, 2, 128)
        cmkT = np.transpose(cmk, (0, 1, 5, 3, 4, 2)).reshape(D, NSB, 128, 8 * 256)
        cmv = g["cache_mem_v"][:, sb].reshape(D, NSB, 256, 1024)
        m = dict(shared)
        m.update({
            "xp": f(xp), "xs": f(g["x_sample"][sb].reshape(NSB * 4, 1024)),
            "maskF": maskN if first else maskA,
            "cflag": np.full((128, 1), 0.0 if first else 1.0, np.float32),
            "cwkT": f(cwkT), "cwk": f(cwk), "cwv": f(cwv), "stT": f(stT),
            "memT": f(memT), "cmkT": f(cmkT), "cmv": f(cmv),
        })
        maps.append(m)
    return maps


def assemble(cfg, res):
    D, B, SEQ, DB, n = cfg.D, cfg.B, cfg.SEQ, cfg.DB, cfg.n
    NA, NC, NSB, NOWN, cps = cfg.NA, cfg.NC, cfg.NSB, cfg.NOWN, cfg.cps
    y_p = np.zeros((B, SEQ, 1024), np.float32)
    y_s = np.zeros((DB, 4, 1024), np.float32)
    wkp = np.zeros((NA, B, 128, 4, 64), np.float32)
    wvp = np.zeros((NA, B, 128, 4, 64), np.float32)
    cvp = np.zeros((NC, B, 2, 1024), np.float32)
    mkp = np.zeros((D, B, 256, 4, 256), np.float32)
    mvp = np.zeros((D, B, 256, 4, 256), np.float32)
    wks = np.zeros((NA, DB, 128, 4, 64), np.float32)
    wvs = np.zeros((NA, DB, 128, 4, 64), np.float32)
    cvs = np.zeros((NC, DB, 2, 1024), np.float32)
    for c in range(n):
        r = res[c]
        b, qd = c // cps, c % cps
        t0 = qd * NOWN * 128
        y_p[b, t0:t0 + NOWN * 128] = r["yp"]
        sb = slice(c * NSB, (c + 1) * NSB)
        y_s[sb] = r["ys"].reshape(NSB, 4, 1024)
        wks[:, sb] = r["wks"].reshape(NA, NSB, 128, 4, 64)
        wvs[:, sb] = r["wvs"].reshape(NA, NSB, 128, 4, 64)
        if NC > 0:
            cvs[:, sb] = r["cvs"].reshape(NC, NSB, 2, 1024)
        if qd == cps - 1:
            wkp[:, b] = r["wkp"].reshape(NA, 128, 4, 64)
            wvp[:, b] = r["wvp"].reshape(NA, 128, 4, 64)
            if NC > 0:
                cvp[:, b] = r["cvp"][:NC]
        if qd == 0:
            mkp[:, b] = r["mkp"].reshape(D, 256, 4, 256)
            mvp[:, b] = r["mvp"].reshape(D, 256, 4, 256)
    return (y_p, y_s, wkp, wvp, cvp, mkp, mvp, wks, wvs, cvs)


def run_cfg(cfg, inputs, trace=False):
    nc, stack, P = build(cfg)
    maps = prep_inputs(cfg, inputs)
    res = run_bass_kernel_spmd(nc, maps, core_ids=list(range(cfg.n)))
    return assemble(cfg, res.results)


def kernel(**inputs):
    cfg = Cfg()
    return run_cfg(cfg, inputs)
```
